# Optimizing a Trainium2 kernel written in Bass

```python
import math
import jax
import jax.numpy as jnp
from jax import lax
import numpy as np

D_MODEL = 1024
BATCH = 4
SEQ = 4096
DEPTH = 2
DEC_BATCH = 8
DEC_SEQ = 8192
PAST_LEN = 128

GRID_W = 64
NA_HEADS = 16
NA_HEAD_DIM = D_MODEL // NA_HEADS
NA_WIN_ROWS = 8
NA_WIN_COLS = 16
NA_RPB_ROWS = 2 * NA_WIN_ROWS - 1
NA_RPB_COLS = 2 * NA_WIN_COLS - 1
DA_HEAD_DIM = 64
DA_HEADS = D_MODEL // (2 * DA_HEAD_DIM)
DA_SUBLN_EPS = 1e-5
Q_BLOCK = 128
ROPE_THETA = 10000.0
D_FF = 2816
CONV_W = 3
NORM_EPS = 1e-6

N_NA_LAYERS = (DEPTH + 1) // 2
N_DA_LAYERS = DEPTH // 2

kernel_name = 'hybrid_natten_diffattn_convglu_encoder'


def rms_norm(x, g, eps=NORM_EPS):
    xf = x.astype(jnp.float32)
    y = xf * lax.rsqrt(jnp.mean(xf * xf, axis=-1, keepdims=True) + eps)
    return (y * g.astype(jnp.float32)).astype(x.dtype)


def rotary_tables(T, dim):
    inv = 1.0 / (ROPE_THETA ** (jnp.arange(0, dim, 2, dtype=jnp.float32) / dim))
    ang = jnp.arange(T, dtype=jnp.float32)[:, None] * inv[None, :]
    ang = jnp.concatenate([ang, ang], axis=-1)
    return jnp.cos(ang), jnp.sin(ang)


def apply_rotary(x, cos, sin):
    half = x.shape[-1] // 2
    rot = jnp.concatenate([-x[..., half:], x[..., :half]], axis=-1)
    return (x.astype(jnp.float32) * cos + rot.astype(jnp.float32) * sin).astype(x.dtype)


def neighborhood_attention(x, w_qkv, rpb, w_o):
    B, T, _ = x.shape
    rows = T // GRID_W
    wr = min(NA_WIN_ROWS, rows)
    qkv = (x @ w_qkv).reshape(B, rows, GRID_W, 3, NA_HEADS, NA_HEAD_DIM)
    q = qkv[:, :, :, 0].transpose(1, 0, 3, 2, 4)
    k = qkv[:, :, :, 1].transpose(0, 3, 1, 2, 4)
    v = qkv[:, :, :, 2].transpose(0, 3, 1, 2, 4)
    cols = np.arange(GRID_W)
    col_start = np.clip(cols - NA_WIN_COLS // 2, 0, GRID_W - NA_WIN_COLS)
    col_idx = col_start[:, None] + np.arange(NA_WIN_COLS)[None, :]
    dc_idx = col_idx - cols[:, None] + (NA_WIN_COLS - 1)
    scale = NA_HEAD_DIM ** -0.5

    def row_block(args):
        q_r, r = args
        r0 = jnp.clip(r - wr // 2, 0, rows - wr)
        k_rows = lax.dynamic_slice_in_dim(k, r0, wr, axis=2)
        v_rows = lax.dynamic_slice_in_dim(v, r0, wr, axis=2)
        k_nb = k_rows[:, :, :, col_idx, :]
        v_nb = v_rows[:, :, :, col_idx, :]
        s = jnp.einsum('bhqd,bhrqcd->bhqrc', q_r, k_nb).astype(jnp.float32) * scale
        dr_idx = r0 + jnp.arange(wr) - r + (NA_WIN_ROWS - 1)
        bias = rpb[:, dr_idx][:, :, dc_idx]
        s = s + bias.transpose(0, 2, 1, 3).astype(jnp.float32)[None]
        p = jax.nn.softmax(s.reshape(B, NA_HEADS, GRID_W, wr * NA_WIN_COLS), axis=-1)
        p = p.reshape(B, NA_HEADS, GRID_W, wr, NA_WIN_COLS).astype(v.dtype)
        return jnp.einsum('bhqrc,bhrqcd->bhqd', p, v_nb)

    o = lax.map(row_block, (q, jnp.arange(rows)))
    o = o.transpose(1, 0, 3, 2, 4).reshape(B, T, D_MODEL)
    return o @ w_o


def diff_attention(x, w_q, w_k, w_v, lq1, lk1, lq2, lk2, subln_g, w_o, lambda_init):
    B, T, _ = x.shape
    q = (x @ w_q).reshape(B, T, DA_HEADS, 2, DA_HEAD_DIM)
    k = (x @ w_k).reshape(B, T, DA_HEADS, 2, DA_HEAD_DIM)
    v = (x @ w_v).reshape(B, T, DA_HEADS, 2 * DA_HEAD_DIM)
    cos, sin = rotary_tables(T, DA_HEAD_DIM)
    cos, sin = cos[:, None, None, :], sin[:, None, None, :]
    q = apply_rotary(q, cos, sin).transpose(0, 2, 3, 1, 4)
    k = apply_rotary(k, cos, sin).transpose(0, 2, 3, 1, 4)
    v = v.transpose(0, 2, 1, 3)
    lam = (jnp.exp(jnp.sum(lq1.astype(jnp.float32) * lk1.astype(jnp.float32)))
           - jnp.exp(jnp.sum(lq2.astype(jnp.float32) * lk2.astype(jnp.float32)))
           + lambda_init)
    scale = DA_HEAD_DIM ** -0.5
    nb = T // Q_BLOCK
    qb = q.reshape(B, DA_HEADS, 2, nb, Q_BLOCK, DA_HEAD_DIM).transpose(3, 0, 1, 2, 4, 5)

    def q_block(qi):
        s = jnp.einsum('bhiqd,bhikd->bhiqk', qi, k).astype(jnp.float32) * scale
        p = jax.nn.softmax(s, axis=-1)
        a = (p[:, :, 0] - lam * p[:, :, 1]).astype(v.dtype)
        return jnp.einsum('bhqk,bhkd->bhqd', a, v)

    o = lax.map(q_block, qb)
    o = rms_norm(o, subln_g, DA_SUBLN_EPS) * (1.0 - lambda_init)
    o = o.transpose(1, 0, 3, 2, 4).reshape(B, T, DA_HEADS * 2 * DA_HEAD_DIM)
    return o @ w_o


def conv_glu_ffn(x, w_in, conv_w, conv_b, w_out):
    h = x @ w_in
    hp = jnp.pad(h, ((0, 0), (1, 1), (0, 0)))
    h = hp[:, :-2] * conv_w[0] + hp[:, 1:-1] * conv_w[1] + hp[:, 2:] * conv_w[2] + conv_b
    gate, up = jnp.split(h, 2, axis=-1)
    return (jax.nn.gelu(gate, approximate=True) * up) @ w_out


def lambda_init_fn(layer_idx):
    return 0.8 - 0.6 * math.exp(-0.3 * layer_idx)


def trunk(x, attn_pre_g, attn_post_g, ffn_pre_g, ffn_post_g,
          na_w_qkv, na_rpb, na_w_o,
          da_w_q, da_w_k, da_w_v, da_lambda_q1, da_lambda_k1, da_lambda_q2, da_lambda_k2,
          da_subln_g, da_w_o,
          ffn_w_in, ffn_conv_w, ffn_conv_b, ffn_w_out):
    for i in range(DEPTH):
        j = i // 2
        h = rms_norm(x, attn_pre_g[i])
        if i % 2 == 0:
            m = neighborhood_attention(h, na_w_qkv[j], na_rpb[j], na_w_o[j])
        else:
            m = diff_attention(h, da_w_q[j], da_w_k[j], da_w_v[j],
                               da_lambda_q1[j], da_lambda_k1[j], da_lambda_q2[j], da_lambda_k2[j],
                               da_subln_g[j], da_w_o[j], lambda_init_fn(i))
        x = x + rms_norm(m, attn_post_g[i])
        h = rms_norm(x, ffn_pre_g[i])
        f = conv_glu_ffn(h, ffn_w_in[i], ffn_conv_w[i], ffn_conv_b[i], ffn_w_out[i])
        x = x + rms_norm(f, ffn_post_g[i])
    return x


def _normal(k, shape, scale):
    return jax.random.normal(k, shape, jnp.float32) * scale


def setup_inputs(seed: int = 0) -> dict:
    key = jax.random.key(seed)
    ks = jax.random.split(key, 24)
    D, F = D_MODEL, D_FF
    da_qk = 2 * DA_HEADS * DA_HEAD_DIM
    da_v = DA_HEADS * 2 * DA_HEAD_DIM
    return {
        'x_prompt': _normal(ks[0], (BATCH, SEQ, D), 1.0),
        'x_sample': _normal(ks[1], (DEC_BATCH, DEC_SEQ, D), 1.0),
        'attn_pre_g': 1.0 + _normal(ks[2], (DEPTH, D), 0.1),
        'attn_post_g': 1.0 + _normal(ks[3], (DEPTH, D), 0.1),
        'ffn_pre_g': 1.0 + _normal(ks[4], (DEPTH, D), 0.1),
        'ffn_post_g': 1.0 + _normal(ks[5], (DEPTH, D), 0.1),
        'na_w_qkv': _normal(ks[6], (N_NA_LAYERS, D, 3 * D), D ** -0.5),
        'na_rpb': _normal(ks[7], (N_NA_LAYERS, NA_HEADS, NA_RPB_ROWS, NA_RPB_COLS), 0.1),
        'na_w_o': _normal(ks[8], (N_NA_LAYERS, D, D), D ** -0.5),
        'da_w_q': _normal(ks[9], (N_DA_LAYERS, D, da_qk), D ** -0.5),
        'da_w_k': _normal(ks[10], (N_DA_LAYERS, D, da_qk), D ** -0.5),
        'da_w_v': _normal(ks[11], (N_DA_LAYERS, D, da_v), D ** -0.5),
        'da_lambda_q1': _normal(ks[12], (N_DA_LAYERS, DA_HEAD_DIM), 0.1),
        'da_lambda_k1': _normal(ks[13], (N_DA_LAYERS, DA_HEAD_DIM), 0.1),
        'da_lambda_q2': _normal(ks[14], (N_DA_LAYERS, DA_HEAD_DIM), 0.1),
        'da_lambda_k2': _normal(ks[15], (N_DA_LAYERS, DA_HEAD_DIM), 0.1),
        'da_subln_g': 1.0 + _normal(ks[16], (N_DA_LAYERS, 2 * DA_HEAD_DIM), 0.1),
        'da_w_o': _normal(ks[17], (N_DA_LAYERS, da_v, D), da_v ** -0.5),
        'ffn_w_in': _normal(ks[18], (DEPTH, D, 2 * F), D ** -0.5),
        'ffn_conv_w': _normal(ks[19], (DEPTH, CONV_W, 2 * F), CONV_W ** -0.5),
        'ffn_conv_b': _normal(ks[20], (DEPTH, 2 * F), 0.01),
        'ffn_w_out': _normal(ks[21], (DEPTH, F, D), F ** -0.5),
    }


def reference(x_prompt, x_sample, attn_pre_g, attn_post_g, ffn_pre_g, ffn_post_g,
              na_w_qkv, na_rpb, na_w_o,
              da_w_q, da_w_k, da_w_v, da_lambda_q1, da_lambda_k1, da_lambda_q2, da_lambda_k2,
              da_subln_g, da_w_o,
              ffn_w_in, ffn_conv_w, ffn_conv_b, ffn_w_out):
    y_prompt = trunk(x_prompt, attn_pre_g, attn_post_g, ffn_pre_g, ffn_post_g,
                     na_w_qkv, na_rpb, na_w_o,
                     da_w_q, da_w_k, da_w_v, da_lambda_q1, da_lambda_k1, da_lambda_q2, da_lambda_k2,
                     da_subln_g, da_w_o,
                     ffn_w_in, ffn_conv_w, ffn_conv_b, ffn_w_out)
    y_sample = trunk(x_sample, attn_pre_g, attn_post_g, ffn_pre_g, ffn_post_g,
                     na_w_qkv, na_rpb, na_w_o,
                     da_w_q, da_w_k, da_w_v, da_lambda_q1, da_lambda_k1, da_lambda_q2, da_lambda_k2,
                     da_subln_g, da_w_o,
                     ffn_w_in, ffn_conv_w, ffn_conv_b, ffn_w_out)
    return (y_prompt, y_sample)
```

```python
import math
from contextlib import ExitStack
import numpy as np
import ml_dtypes
import concourse.bass as bass
import concourse.mybir as mybir
from concourse.bass_utils import run_bass_kernel_spmd

F32 = mybir.dt.float32
BF16 = mybir.dt.bfloat16
AF = mybir.ActivationFunctionType
ALU = mybir.AluOpType

D = 1024
DFF = 2816
NFC = 22
NA_H = 16
DA_H = 8
NEG = -30000.0
NORM_EPS = 1e-6
SUBLN_EPS = 1e-5
LAMBDA_INIT = 0.8 - 0.6 * math.exp(-0.3 * 1)
ROPE_THETA = 10000.0


class Buf:
    __slots__ = ("name", "w", "r")

    def __init__(self, name=""):
        self.name = name
        self.w = None
        self.r = {}


class Eng:
    def __init__(self, nc, h, name):
        self.h = h
        self.name = name
        self.sem = nc.alloc_semaphore(name="s_" + name)
        self.count = 0
        self.known = {}


class FW:
    NDMASEM = 48

    def __init__(self, nc):
        self.nc = nc
        self.pe = Eng(nc, nc.tensor, "pe")
        self.act = Eng(nc, nc.scalar, "act")
        self.dve = Eng(nc, nc.vector, "dve")
        self.pool = Eng(nc, nc.gpsimd, "pool")
        self.sp = Eng(nc, nc.sync, "sp")
        self.engs = [self.pe, self.act, self.dve, self.pool, self.sp]
        self.dsems = [nc.alloc_semaphore(name=f"d{i}") for i in range(self.NDMASEM)]
        self.dval = [0] * self.NDMASEM
        self.dnext = 0
        self.nops = 0
        self.ndma = 0

    def wait(self, eng, tok):
        if tok is None:
            return
        sem, val = tok
        if sem is eng.sem:
            if eng is self.pe:
                return
            if val > eng.count:
                return
        k = id(sem)
        if eng.known.get(k, 0) >= val:
            return
        eng.h.wait_ge(sem, val)
        eng.known[k] = val

    def _deps(self, eng, reads, writes):
        for b in reads:
            self.wait(eng, b.w)
        for b in writes:
            self.wait(eng, b.w)
            for tok in list(b.r.values()):
                self.wait(eng, tok)

    def _commit(self, tok, reads, writes):
        k = id(tok[0])
        for b in reads:
            if k not in b.r or b.r[k][1] < tok[1]:
                b.r[k] = tok
        for b in writes:
            b.w = tok
            b.r = {}

    def op(self, eng, fn, reads=(), writes=(), inc=True):
        self._deps(eng, reads, writes)
        ins = fn()
        self.nops += 1
        if inc:
            eng.count += 1
            ins.then_inc(eng.sem, 1)
            tok = (eng.sem, eng.count)
        else:
            tok = (eng.sem, eng.count + 1)
        self._commit(tok, reads, writes)
        return tok

    def dma(self, eng, out, in_, reads=(), writes=(), **kw):
        self._deps(eng, reads, writes)
        i = self.dnext
        self.dnext = (self.dnext + 1) % self.NDMASEM
        sem = self.dsems[i]
        if self.dval[i] > 0:
            self.wait(eng, (sem, self.dval[i]))
        self.dval[i] += 16
        eng.h.dma_start(out=out, in_=in_, **kw).then_inc(sem, 16)
        self.ndma += 1
        tok = (sem, self.dval[i])
        self._commit(tok, reads, writes)
        return tok

    def barrier(self):
        for e in self.engs:
            for o in self.engs:
                if o is not e and o.count > 0:
                    self.wait(e, (o.sem, o.count))
            for i, sem in enumerate(self.dsems):
                if self.dval[i] > 0:
                    self.wait(e, (sem, self.dval[i]))

    def finish(self):
        for i, sem in enumerate(self.dsems):
            if self.dval[i] > 0:
                self.wait(self.sp, (sem, self.dval[i]))
        for e in (self.pe, self.act, self.dve, self.pool):
            if e.count > 0:
                self.wait(self.sp, (e.sem, e.count))


class Ring:
    def __init__(self, tiles):
        self.tiles = tiles
        self.bufs = [Buf() for _ in tiles]
        self.i = -1

    def next(self):
        self.i = (self.i + 1) % len(self.tiles)
        return self.tiles[self.i], self.bufs[self.i]

    def cur(self):
        return self.tiles[self.i], self.bufs[self.i]


class Builder:
    def __init__(self, seqs, debug=False):
        self.seqs = seqs
        self.NT = sum(T for _, T in seqs)
        self.TMAX = max(T for _, T in seqs)
        self.debug = debug
        self.nc = nc = bass.Bass("TRN2", target_bir_lowering=False)
        self.fw = FW(nc)
        NT = self.NT
        dt = nc.dram_tensor
        I = "ExternalInput"
        self.x = dt("x", [NT, D], F32, kind=I).ap()
        self.gvec = dt("gvec", [8, D], F32, kind=I).ap()
        self.w_qkv = dt("w_qkv", [D, 3 * D], F32, kind=I).ap()
        self.na_wo = dt("na_wo", [D, D], F32, kind=I).ap()
        self.da_wq = dt("da_wq", [D, D], F32, kind=I).ap()
        self.da_wk = dt("da_wk", [D, D], F32, kind=I).ap()
        self.da_wv = dt("da_wv", [D, D], F32, kind=I).ap()
        self.da_wo = dt("da_wo", [D, D], F32, kind=I).ap()
        self.w_in = dt("w_in", [2, D, 2 * DFF], F32, kind=I).ap()
        self.w_out = dt("w_out", [2, DFF, D], F32, kind=I).ap()
        self.conv_w = dt("conv_w", [2, 3, 2 * DFF], F32, kind=I).ap()
        self.conv_b = dt("conv_b", [2, 2 * DFF], F32, kind=I).ap()
        self.bias_int = dt("bias_int", [NA_H, 128, 640], F32, kind=I).ap()
        self.bias_sp = dt("bias_sp", [4, NA_H, 128, 512], F32, kind=I).ap()
        self.rot_cos = dt("rot_cos", [128, self.TMAX], F32, kind=I).ap()
        self.rot_sin = dt("rot_sin", [128, self.TMAX], F32, kind=I).ap()
        self.lam_in = dt("lam_in", [4, 64], F32, kind=I).ap()
        self.subln_g = dt("subln_g", [128], F32, kind=I).ap()
        self.ident = dt("ident", [128, 128], F32, kind=I).ap()
        self.y = dt("y", [NT, D], F32, kind="ExternalOutput").ap()
        S = "ExternalOutput" if debug else "Internal"
        self.XA = dt("XA", [NT, D], F32, kind=S).ap()
        self.XB = dt("XB", [NT, D], F32, kind=S).ap()
        self.WINB = dt("WINB", [2, 11, 128, 8, 512], BF16, kind="Internal").ap()
        NBLK = NT // 128
        self.NQT = dt("NQT", [NBLK, 128, 8, 128], BF16, kind="Internal").ap()
        self.NKT = dt("NKT", [NBLK, 128, 8, 128], BF16, kind="Internal").ap()
        self.NV = dt("NV", [128, NBLK, NA_H * 65], BF16, kind="Internal").ap()
        self.DQT = dt("DQT", [8, 128, NT], BF16, kind="Internal").ap()
        self.DKT = dt("DKT", [8, 128, NT], BF16, kind="Internal").ap()
        self.DV = dt("DV", [DA_H, 128, NBLK, 129], BF16, kind="Internal").ap()
        self.DOT = dt("DOT", [DA_H, 128, NT], BF16, kind="Internal").ap()
        self.XC = self.XA

    _uid = 0

    def _nm(self, name):
        Builder._uid += 1
        return f"{name}_{Builder._uid}"

    def sb(self, st, name, shape, dtype):
        return st.enter_context(self.nc.sbuf_tensor(self._nm(name), shape, dtype))

    def ps(self, st, name, shape, dtype=F32):
        return st.enter_context(self.nc.psum_tensor(self._nm(name), shape, dtype))

    def ring(self, st, name, n, shape, dtype):
        return Ring([self.sb(st, f"{name}{i}", shape, dtype) for i in range(n)])

    def consts(self, st):
        nc, fw = self.nc, self.fw
        c = {}
        c["idf"] = self.sb(st, "idf", [128, 128], F32)
        c["idb"] = self.sb(st, "idb", [128, 128], BF16)
        c["nh"] = self.sb(st, "nhalf", [128, 1], F32)
        c["B"] = Buf("consts")
        fw.dma(fw.sp, c["idf"][:], self.ident[:, :], writes=[c["B"]])
        fw.dma(fw.pool, c["idb"][:], self.ident[:, :], writes=[c["B"]])
        fw.op(fw.pool, lambda: nc.gpsimd.memset(c["nh"][:], -0.5), writes=[c["B"]])
        return c

    def load_gfm(self, st, name, row):
        nc, fw = self.nc, self.fw
        t = self.sb(st, name, [128, 8], F32)
        b = Buf(name)
        with nc.allow_non_contiguous_dma(reason="tiny gain vector"):
            fw.dma(fw.sp, t[:], self.gvec[row, :].rearrange("(c p) -> p c", p=128), writes=[b])
        return t, b

    def load_gbc(self, st, name, row):
        fw = self.fw
        t = self.sb(st, name, [128, D], F32)
        b = Buf(name)
        fw.dma(fw.sp, t[:], self.gvec[row, :].partition_broadcast(128), writes=[b])
        return t, b

    def rstd(self, ss, ssb, n, inv_dim, eps, c):
        nc, fw = self.nc, self.fw
        fw.op(fw.pool, lambda: nc.gpsimd.tensor_scalar(out=ss[:n, 1:2], in0=ss[:n, 0:1], scalar1=inv_dim, scalar2=eps,
                                                        op0=ALU.mult, op1=ALU.add), reads=[ssb], writes=[ssb])
        fw.op(fw.pool, lambda: nc.gpsimd.tensor_tensor(out=ss[:n, 2:3], in0=ss[:n, 1:2], in1=c["nh"][:n, :], op=ALU.pow),
              reads=[ssb, c["B"]], writes=[ssb])

    def make_frontend(self, st, c, gfm, gfmb, tp, tpb):
        fe = {"xt": self.ring(st, "fe_xt", 3, [128, D], F32), "xs": self.ring(st, "fe_xs", 2, [128, D], F32),
              "junk": self.sb(st, "fe_junk", [128, D], BF16), "junkb": Buf(),
              "ss": self.ring(st, "fe_ss", 4, [128, 4], F32), "c": c, "gfm": gfm, "gfmb": gfmb, "tp": tp, "tpb": tpb}
        return fe

    def frontend(self, fe, src, n, hT, hTb, col0):
        nc, fw = self.nc, self.fw
        c = fe["c"]
        xt, xtb = fe["xt"].next()
        xs, xsb = fe["xs"].next()
        ss, ssb = fe["ss"].next()
        tp, tpb = fe["tp"], fe["tpb"]
        fw.dma(fw.sp, xt[:n, :], src, writes=[xtb])
        fw.op(fw.act, lambda: nc.scalar.activation(out=fe["junk"][:n, :], in_=xt[:n, :], func=AF.Square, accum_out=ss[:n, 0:1]),
              reads=[xtb], writes=[fe["junkb"], ssb])
        self.rstd(ss, ssb, n, 1.0 / D, NORM_EPS, c)
        fw.op(fw.act, lambda: nc.scalar.activation(out=xs[:n, :], in_=xt[:n, :], func=AF.Copy, scale=ss[:n, 2:3]),
              reads=[xtb, ssb], writes=[xsb])
        for k in range(8):
            fw.op(fw.pe, lambda k=k: nc.tensor.transpose(out=tp[:, k, :n], in_=xs[:n, k * 128:(k + 1) * 128], identity=c["idf"][:n, :n]),
                  reads=[xsb, c["B"]], writes=[tpb], inc=(k == 7))
        fw.op(fw.dve, lambda: nc.vector.tensor_tensor(out=hT[:, :, col0:col0 + n], in0=tp[:, :, :n],
                                                      in1=fe["gfm"][:, :].unsqueeze(2).to_broadcast([128, 8, n]), op=ALU.mult),
              reads=[tpb, fe["gfmb"]], writes=[hTb])

    def make_backend(self, st, c, gbc, gbcb):
        be = {"xr": self.ring(st, "be_xr", 3, [128, D], F32), "t": self.ring(st, "be_t", 2, [128, D], F32),
              "o": self.ring(st, "be_o", 3, [128, D], F32), "junk": self.sb(st, "be_junk", [128, D], BF16), "junkb": Buf(),
              "ss": self.ring(st, "be_ss", 4, [128, 4], F32), "c": c, "gbc": gbc, "gbcb": gbcb}
        return be

    def backend_prefetch(self, be, res_src, n):
        xr, xrb = be["xr"].next()
        self.fw.dma(self.fw.sp, xr[:n, :], res_src, writes=[xrb])
        return xr, xrb

    def backend(self, be, m, mb, n, xr, xrb, dst):
        nc, fw = self.nc, self.fw
        ss, ssb = be["ss"].next()
        t, tb = be["t"].next()
        o, ob = be["o"].next()
        m2 = m[:n, :, :].rearrange("p a b -> p (a b)")
        fw.op(fw.act, lambda: nc.scalar.activation(out=be["junk"][:n, :], in_=m2, func=AF.Square, accum_out=ss[:n, 0:1]),
              reads=[mb], writes=[be["junkb"], ssb])
        self.rstd(ss, ssb, n, 1.0 / D, NORM_EPS, be["c"])
        fw.op(fw.dve, lambda: nc.vector.scalar_tensor_tensor(out=t[:n, :], in0=m2, scalar=ss[:n, 2:3], in1=be["gbc"][:n, :],
                                                             op0=ALU.mult, op1=ALU.mult), reads=[mb, ssb, be["gbcb"]], writes=[tb])
        fw.op(fw.pool, lambda: nc.gpsimd.tensor_tensor(out=o[:n, :], in0=t[:n, :], in1=xr[:n, :], op=ALU.add),
              reads=[tb, xrb], writes=[ob])
        fw.dma(fw.sp, dst, o[:n, :], reads=[ob])

    def phase_p0(self):
        nc, fw = self.nc, self.fw
        with ExitStack() as st:
            stg = self.ring(st, "p0_stg", 3, [128, 8, 512], BF16)
            for l in range(2):
                for g in range(11):
                    t, tb = stg.next()
                    src = self.w_in[l]
                    fw.dma(fw.pool, t[:, :, 0:256], src[:, 256 * g:256 * g + 256].rearrange("(kc p) n -> p kc n", p=128), writes=[tb])
                    fw.dma(fw.pool, t[:, :, 256:512], src[:, DFF + 256 * g:DFF + 256 * g + 256].rearrange("(kc p) n -> p kc n", p=128), writes=[tb])
                    fw.dma(fw.sp, self.WINB[l, g], t[:], reads=[tb])
            fw.barrier()

    def phase_ffn(self, l, src, dst):
        nc, fw = self.nc, self.fw
        with ExitStack() as st:
            c = self.consts(st)
            gfm, gfmb = self.load_gfm(st, "ffn_gfm", 4 + l)
            gbc, gbcb = self.load_gbc(st, "ffn_gbc", 6 + l)
            wout = self.sb(st, "ffn_wout", [128, NFC, D], BF16)
            woutb = Buf()
            for f0 in range(0, NFC, 6):
                f1 = min(NFC, f0 + 6)
                fw.dma(fw.pool, wout[:, f0:f1, :], self.w_out[l, f0 * 128:f1 * 128, :].rearrange("(f p) n -> p f n", p=128), writes=[woutb])
            cw = self.sb(st, "ffn_cw", [128, 3, 2 * NFC], F32)
            cb = self.sb(st, "ffn_cb", [128, 2 * NFC], F32)
            cwb = Buf()
            with nc.allow_non_contiguous_dma(reason="small conv params"):
                for k in range(3):
                    fw.dma(fw.sp, cw[:, k, :], self.conv_w[l, k, :].rearrange("(c p) -> p c", p=128), writes=[cwb])
                fw.dma(fw.sp, cb[:, :], self.conv_b[l, :].rearrange("(c p) -> p c", p=128), writes=[cwb])
            tp = self.ps(st, "ffn_tp", [128, 8, 128])
            tpb = Buf()
            pgu = [self.ps(st, f"ffn_pgu{i}", [128, 2, 512]) for i in range(2)]
            pgub = [[Buf(), Buf()] for _ in range(2)]
            pm = self.ps(st, "ffn_pm", [128, 2, 512])
            pmb = Buf()
            fe = self.make_frontend(st, c, gfm, gfmb, tp, tpb)
            be = self.make_backend(st, c, gbc, gbcb)
            hT = self.ring(st, "ffn_hT", 2, [128, 8, 512], BF16)
            wg = self.ring(st, "ffn_wg", 3, [128, 8, 512], BF16)
            aT = self.sb(st, "ffn_aT", [128, NFC, 513], BF16)
            aTb = Buf()
            carry = self.sb(st, "ffn_carry", [128, 2 * NFC, 2], F32)
            carryb = [Buf() for _ in range(2 * NFC)]
            R = self.ring(st, "ffn_R", 4, [128, 515], F32)
            T1 = self.ring(st, "ffn_T1", 3, [128, 513], F32)
            T2 = self.ring(st, "ffn_T2", 3, [128, 513], F32)
            TG = self.ring(st, "ffn_TG", 2, [128, 513], F32)
            TU = self.ring(st, "ffn_TU", 2, [128, 513], F32)
            GG = self.ring(st, "ffn_GG", 2, [128, 513], F32)
            for t_, b_ in zip(R.tiles, R.bufs):
                fw.op(fw.pool, lambda t_=t_: nc.gpsimd.memset(t_[:, :], 0.0), writes=[b_])

            for (toff, T) in self.seqs:
                ntile = T // 512
                fw.op(fw.pool, lambda: nc.gpsimd.memset(carry[:, :, :], 0.0), writes=carryb)

                def do_frontend(i):
                    h, hb = hT.next()
                    for b in range(4):
                        t0 = toff + 512 * i + 128 * b
                        self.frontend(fe, src[t0:t0 + 128, :], 128, h, hb, 128 * b)
                    return h, hb

                nxt = do_frontend(0)
                for i in range(ntile):
                    h, hb = nxt
                    last = (i == ntile - 1)
                    W = 513 if last else 512
                    gi = 0
                    wq = []
                    for g in range(2):
                        w_, wb_ = wg.next()
                        fw.dma(fw.sp, w_[:], self.WINB[l, g], writes=[wb_])
                        wq.append((w_, wb_))
                    for g in range(11):
                        if g + 2 < 11:
                            w_, wb_ = wg.next()
                            fw.dma(fw.sp, w_[:], self.WINB[l, g + 2], writes=[wb_])
                            wq.append((w_, wb_))
                        w_, wb_ = wq[g]
                        for jf in range(2):
                            f = 2 * g + jf
                            pi = f % 2
                            P_ = pgu[pi]
                            Pb = pgub[pi]
                            tg = tu = None
                            for gu in range(2):
                                cidx = f + NFC * gu
                                col = 256 * gu + 128 * jf
                                for kc in range(8):
                                    fw.op(fw.pe, lambda kc=kc, col=col, gu=gu: nc.tensor.matmul(P_[:, gu, :], w_[:, kc, col:col + 128], h[:, kc, :],
                                                                                             start=(kc == 0), stop=(kc == 7)),
                                          reads=[wb_, hb], writes=[Pb[gu]], inc=(kc == 7))
                                r, rb = R.next()
                                fw.op(fw.act, lambda r=r, gu=gu: nc.scalar.activation(out=r[:, 2:514], in_=P_[:, gu, :], func=AF.Copy),
                                      reads=[Pb[gu]], writes=[rb])
                                fw.op(fw.pool, lambda r=r, cidx=cidx: nc.gpsimd.tensor_copy(out=r[:, 0:2], in_=carry[:, cidx, :]),
                                      reads=[carryb[cidx]], writes=[rb])
                                fw.op(fw.pool, lambda r=r, cidx=cidx: nc.gpsimd.tensor_copy(out=carry[:, cidx, :], in_=r[:, 512:514]),
                                      reads=[rb], writes=[carryb[cidx]])
                                t1, t1b = T1.next()
                                t2, t2b = T2.next()
                                t3, t3b = (TG if gu == 0 else TU).next()
                                fw.op(fw.act, lambda r=r, t1=t1, cidx=cidx: nc.scalar.activation(out=t1[:, 0:W], in_=r[:, 1:1 + W], func=AF.Identity,
                                                                                                 scale=cw[:, 1, cidx:cidx + 1], bias=cb[:, cidx:cidx + 1]),
                                      reads=[rb, cwb], writes=[t1b])
                                fw.op(fw.dve, lambda r=r, t1=t1, t2=t2, cidx=cidx: nc.vector.scalar_tensor_tensor(out=t2[:, 0:W], in0=r[:, 0:W], scalar=cw[:, 0, cidx:cidx + 1],
                                                                                                               in1=t1[:, 0:W], op0=ALU.mult, op1=ALU.add),
                                      reads=[rb, t1b, cwb], writes=[t2b])
                                fw.op(fw.dve, lambda r=r, t2=t2, t3=t3, cidx=cidx: nc.vector.scalar_tensor_tensor(out=t3[:, 0:W], in0=r[:, 2:2 + W], scalar=cw[:, 2, cidx:cidx + 1],
                                                                                                               in1=t2[:, 0:W], op0=ALU.mult, op1=ALU.add),
                                      reads=[rb, t2b, cwb], writes=[t3b])
                                if gu == 0:
                                    tg, tgb = t3, t3b
                                else:
                                    tu, tub = t3, t3b
                            gg, ggb = GG.next()
                            fw.op(fw.act, lambda gg=gg, tg=tg: nc.scalar.activation(out=gg[:, 0:W], in_=tg[:, 0:W], func=AF.Gelu_apprx_tanh),
                                  reads=[tgb], writes=[ggb])
                            fw.op(fw.pool, lambda gg=gg, tu=tu, f=f: nc.gpsimd.tensor_tensor(out=aT[:, f, 0:W], in0=gg[:, 0:W], in1=tu[:, 0:W], op=ALU.mult),
                                  reads=[ggb, tub], writes=[aTb])
                    if not last:
                        nxt = do_frontend(i + 1)
                    j0 = 1 if i == 0 else 0
                    blocks = []
                    j = j0
                    while j < 512:
                        n = min(128, 512 - j)
                        blocks.append((j, n))
                        j += n
                    if W == 513:
                        blocks.append((512, 1))
                    for (j, n) in blocks:
                        tk = toff + 512 * i - 1 + j
                        xr, xrb = self.backend_prefetch(be, src[tk:tk + n, :], n)
                        for half in range(2):
                            for f in range(NFC):
                                fw.op(fw.pe, lambda f=f, half=half, j=j, n=n: nc.tensor.matmul(pm[:n, half, :], aT[:, f, j:j + n], wout[:, f, half * 512:(half + 1) * 512],
                                                                                              start=(f == 0), stop=(f == NFC - 1)),
                                      reads=[aTb, woutb], writes=[pmb], inc=(f == NFC - 1 and half == 1))
                        self.backend(be, pm, pmb, n, xr, xrb, dst[tk:tk + n, :])
            fw.barrier()

    def phase_proj(self, mode, src):
        nc, fw = self.nc, self.fw
        na = (mode == "na")
        with ExitStack() as st:
            c = self.consts(st)
            gfm, gfmb = self.load_gfm(st, "pj_gfm", 0 if na else 1)
            w = self.sb(st, "pj_w", [128, 8, 3 * D], BF16)
            wb = Buf()
            if na:
                for k3 in range(3):
                    fw.dma(fw.pool, w[:, :, k3 * D:(k3 + 1) * D], self.w_qkv[:, k3 * D:(k3 + 1) * D].rearrange("(kc p) n -> p kc n", p=128), writes=[wb])
            else:
                for k3, ww in enumerate((self.da_wq, self.da_wk, self.da_wv)):
                    fw.dma(fw.pool, w[:, :, k3 * D:(k3 + 1) * D], ww.rearrange("(kc p) n -> p kc n", p=128), writes=[wb])
            tp = self.ps(st, "pj_tp", [128, 8, 128])
            tpb = Buf()
            pp = [self.ps(st, f"pj_pp{i}", [128, 2, 512]) for i in range(3)]
            ppb = [[Buf(), Buf()] for _ in range(3)]
            fe = self.make_frontend(st, c, gfm, gfmb, tp, tpb)
            hT = self.ring(st, "pj_hT", 2, [128, 8, 512], BF16)
            if na:
                stq = self.ring(st, "pj_stq", 2, [128, 4, 8, 128], BF16)
                stk = self.ring(st, "pj_stk", 2, [128, 4, 8, 128], BF16)
                vaug = self.ring(st, "pj_vaug", 2, [128, 4, NA_H, 65], BF16)
                for t_, b_ in zip(vaug.tiles, vaug.bufs):
                    fw.op(fw.pool, lambda t_=t_: nc.gpsimd.memset(t_[:, :, :, 64:65], 1.0), writes=[b_])
            else:
                stq = self.ring(st, "pj_stq", 2, [128, 8, 512], BF16)
                stk = self.ring(st, "pj_stk", 2, [128, 8, 512], BF16)
                vaug = self.ring(st, "pj_vaug", 2, [128, DA_H, 4, 129], BF16)
                for t_, b_ in zip(vaug.tiles, vaug.bufs):
                    fw.op(fw.pool, lambda t_=t_: nc.gpsimd.memset(t_[:, :, :, 128:129], 1.0), writes=[b_])
                cosr = self.ring(st, "pj_cos", 2, [128, 512], F32)
                sinr = self.ring(st, "pj_sin", 2, [128, 512], F32)
                rt = [self.ring(st, f"pj_rt{i}", 2, [128, 512], F32) for i in range(4)]
            pi = 0
            for (toff, T) in self.seqs:
                for sbi in range(T // 512):
                    t0 = toff + 512 * sbi
                    blk0 = t0 // 128
                    h, hb = hT.next()
                    for b in range(4):
                        self.frontend(fe, src[t0 + 128 * b:t0 + 128 * b + 128, :], 128, h, hb, 128 * b)
                    if not na:
                        cs, csb = cosr.next()
                        sn, snb = sinr.next()
                        fw.dma(fw.sp, cs[:], self.rot_cos[:, 512 * sbi:512 * sbi + 512], writes=[csb])
                        fw.dma(fw.sp, sn[:], self.rot_sin[:, 512 * sbi:512 * sbi + 512], writes=[snb])
                    for qk in range(2):
                        sg, sgb = (stq if qk == 0 else stk).next()
                        if na:
                            for oc in range(8):
                                P_, Pb = pp[pi % 3], ppb[pi % 3]
                                pi += 1
                                col = qk * D + oc * 128
                                for kc in range(8):
                                    fw.op(fw.pe, lambda kc=kc, col=col, P_=P_: nc.tensor.matmul(P_[:, 0, :], w[:, kc, col:col + 128], h[:, kc, :], start=(kc == 0), stop=(kc == 7)),
                                          reads=[wb, hb], writes=[Pb[0]], inc=(kc == 7))
                                if oc % 2 == 0:
                                    fw.op(fw.act, lambda P_=P_, oc=oc, sg=sg: nc.scalar.activation(out=sg[:, :, oc, :], in_=P_[:, 0, :].rearrange("p (b t) -> p b t", b=4), func=AF.Copy),
                                          reads=[Pb[0]], writes=[sgb])
                                else:
                                    fw.op(fw.dve, lambda P_=P_, oc=oc, sg=sg: nc.vector.tensor_copy(out=sg[:, :, oc, :], in_=P_[:, 0, :].rearrange("p (b t) -> p b t", b=4)),
                                          reads=[Pb[0]], writes=[sgb])
                            dstT = (self.NQT if qk == 0 else self.NKT)
                            fw.dma(fw.sp, dstT[blk0:blk0 + 4].rearrange("b p o t -> p b o t"), sg[:], reads=[sgb])
                        else:
                            for m in range(4):
                                P_, Pb = pp[pi % 3], ppb[pi % 3]
                                pi += 1
                                for ab in range(2):
                                    col = qk * D + m * 256 + ab * 128
                                    for kc in range(8):
                                        fw.op(fw.pe, lambda kc=kc, col=col, P_=P_, ab=ab: nc.tensor.matmul(P_[:, ab, :], w[:, kc, col:col + 128], h[:, kc, :], start=(kc == 0), stop=(kc == 7)),
                                              reads=[wb, hb], writes=[Pb[ab]], inc=(kc == 7))
                                t1, t1b = rt[0].next()
                                t2, t2b = rt[1].next()
                                t3, t3b = rt[2].next()
                                t4, t4b = rt[3].next()
                                TT = nc.vector.tensor_tensor
                                fw.op(fw.dve, lambda P_=P_, t1=t1: TT(out=t1[:], in0=P_[:, 0, :], in1=cs[:], op=ALU.mult), reads=[Pb[0], csb], writes=[t1b])
                                fw.op(fw.dve, lambda P_=P_, t2=t2: TT(out=t2[:], in0=P_[:, 1, :], in1=sn[:], op=ALU.mult), reads=[Pb[1], snb], writes=[t2b])
                                fw.op(fw.dve, lambda P_=P_, t3=t3: TT(out=t3[:], in0=P_[:, 1, :], in1=cs[:], op=ALU.mult), reads=[Pb[1], csb], writes=[t3b])
                                fw.op(fw.dve, lambda P_=P_, t4=t4: TT(out=t4[:], in0=P_[:, 0, :], in1=sn[:], op=ALU.mult), reads=[Pb[0], snb], writes=[t4b])
                                fw.op(fw.pool, lambda t1=t1, t2=t2, m=m, sg=sg: nc.gpsimd.tensor_tensor(out=sg[:, 2 * m, :], in0=t1[:], in1=t2[:], op=ALU.subtract),
                                      reads=[t1b, t2b], writes=[sgb])
                                fw.op(fw.pool, lambda t3=t3, t4=t4, m=m, sg=sg: nc.gpsimd.tensor_tensor(out=sg[:, 2 * m + 1, :], in0=t3[:], in1=t4[:], op=ALU.add),
                                      reads=[t3b, t4b], writes=[sgb])
                            dstT = (self.DQT if qk == 0 else self.DKT)
                            fw.dma(fw.sp, dstT[:, :, t0:t0 + 512].rearrange("o p t -> p o t"), sg[:], reads=[sgb])
                    va, vab = vaug.next()
                    for b in range(4):
                        P_, Pb = pp[pi % 3], ppb[pi % 3]
                        pi += 1
                        for half in range(2):
                            col = 2 * D + half * 512
                            for kc in range(8):
                                fw.op(fw.pe, lambda kc=kc, col=col, P_=P_, half=half, b=b: nc.tensor.matmul(P_[:, half, :], h[:, kc, 128 * b:128 * b + 128], w[:, kc, col:col + 512],
                                                                                                            start=(kc == 0), stop=(kc == 7)),
                                      reads=[wb, hb], writes=[Pb[half]], inc=(kc == 7))
                            if na:
                                o_ap = va[:, b, 8 * half:8 * half + 8, 0:64]
                                i_ap = P_[:, half, :].rearrange("p (h d) -> p h d", h=8)
                            else:
                                o_ap = va[:, 4 * half:4 * half + 4, b, 0:128]
                                i_ap = P_[:, half, :].rearrange("p (h d) -> p h d", h=4)
                            if half == 0:
                                fw.op(fw.act, lambda o_ap=o_ap, i_ap=i_ap: nc.scalar.activation(out=o_ap, in_=i_ap, func=AF.Copy), reads=[Pb[half]], writes=[vab])
                            else:
                                fw.op(fw.dve, lambda o_ap=o_ap, i_ap=i_ap: nc.vector.tensor_copy(out=o_ap, in_=i_ap), reads=[Pb[half]], writes=[vab])
                    if na:
                        fw.dma(fw.sp, self.NV[:, blk0:blk0 + 4, :], va[:].rearrange("p b h d -> p b (h d)"), reads=[vab])
                    else:
                        fw.dma(fw.sp, self.DV[:, :, blk0:blk0 + 4, :].rearrange("h p b x -> p h b x"), va[:], reads=[vab])
            fw.barrier()

    @staticmethod
    def na_kc(j, NB):
        R = 2 * NB
        r0a = min(max(2 * j - 4, 0), R - 8)
        r0b = min(max(2 * j + 1 - 4, 0), R - 8)
        return list(range(r0a // 2, (r0b + 7) // 2 + 1))

    def phase_na2(self, dst):
        nc, fw = self.nc, self.fw
        with ExitStack() as st:
            c = self.consts(st)
            gbc, gbcb = self.load_gbc(st, "na_gbc", 2)
            wo = self.sb(st, "na_wo", [128, 8, D], BF16)
            wob = Buf()
            fw.dma(fw.pool, wo[:], self.na_wo.rearrange("(kc p) n -> p kc n", p=128), writes=[wob])
            bint = self.sb(st, "na_bint", [128, NA_H, 640], F32)
            bintb = Buf()
            for h0 in range(0, NA_H, 4):
                fw.dma(fw.sp, bint[:, h0:h0 + 4, :], self.bias_int[h0:h0 + 4].rearrange("h p n -> p h n"), writes=[bintb])
            bsp = self.ring(st, "na_bsp", 3, [128, 512], F32)
            kring = self.ring(st, "na_k", 8, [128, 8, 128], BF16)
            vring = self.ring(st, "na_v", 8, [128, NA_H * 65], BF16)
            qr = self.ring(st, "na_q", 3, [128, 8, 128], BF16)
            ssb_r = self.ring(st, "na_ssb", 3, [128, 640], F32)
            pT = self.ring(st, "na_pT", 3, [128, 640], BF16)
            on_r = self.ring(st, "na_on", 2, [128, NA_H, 64], BF16)
            oT_r = self.ring(st, "na_oT", 2, [128, 8, 128], BF16)
            rec_r = self.ring(st, "na_rec", 4, [128, 8], F32)
            be = self.make_backend(st, c, gbc, gbcb)
            S = [self.ps(st, f"na_S{i}", [128, 2, 512]) for i in range(2)]
            Sb = [Buf(), Buf()]
            O = [self.ps(st, f"na_O{i}", [128, 512]) for i in range(2)]
            Ob = [Buf(), Buf()]
            M = self.ps(st, "na_M", [128, 2, 512])
            Mb = Buf()
            si = 0
            for (toff, T) in self.seqs:
                NB = T // 128
                blkoff = toff // 128
                loaded = -1
                kslot = {}

                def ensure(upto):
                    nonlocal loaded
                    while loaded < min(upto, NB - 1):
                        loaded += 1
                        kt, ktb = kring.next()
                        vt, vtb = vring.next()
                        fw.dma(fw.sp, kt[:], self.NKT[blkoff + loaded], writes=[ktb])
                        fw.dma(fw.sp, vt[:], self.NV[:, blkoff + loaded, :], writes=[vtb])
                        kslot[loaded] = (kt, ktb, vt, vtb)

                for j in range(NB):
                    kcs = self.na_kc(j, NB)
                    ensure(max(kcs) + 1)
                    nt = len(kcs)
                    if j < 2:
                        sp_idx = j
                    elif j >= NB - 2:
                        sp_idx = 2 + (j - (NB - 2))
                    else:
                        sp_idx = None
                    q, qb = qr.next()
                    fw.dma(fw.sp, q[:], self.NQT[blkoff + j], writes=[qb])
                    tk = toff + 128 * j
                    xr, xrb = self.backend_prefetch(be, self.x[tk:tk + 128, :], 128)
                    on, onb = on_r.next()
                    for hg0, hg1, ob_i in ((0, 7, 0), (7, 14, 1), (14, 16, 0)):
                        Ot, Otb = O[ob_i], Ob[ob_i]
                        for h in range(hg0, hg1):
                            oc, r0 = h // 2, (h % 2) * 64
                            S_, S_b = S[si % 2], Sb[si % 2]
                            si += 1
                            Sf = S_[:, :, :].rearrange("p a b -> p (a b)")
                            for i, kc in enumerate(kcs):
                                kt, ktb, vt, vtb = kslot[kc]
                                fw.op(fw.pe, lambda i=i, kt=kt, Sf=Sf, oc=oc, r0=r0: nc.tensor.matmul(Sf[:, i * 128:(i + 1) * 128], kt[r0:r0 + 64, oc, :], q[r0:r0 + 64, oc, :],
                                                                                                      start=True, stop=True),
                                      reads=[ktb, qb], writes=[S_b], inc=(i == nt - 1))
                            if sp_idx is None:
                                b_ap, b_b = bint[:, h, 0:nt * 128], bintb
                            else:
                                bt, btb = bsp.next()
                                fw.dma(fw.sp, bt[:], self.bias_sp[sp_idx, h], writes=[btb])
                                b_ap, b_b = bt[:, 0:nt * 128], btb
                            ss_, ss_b = ssb_r.next()
                            fw.op(fw.dve, lambda Sf=Sf, ss_=ss_, b_ap=b_ap: nc.vector.scalar_tensor_tensor(out=ss_[:, 0:nt * 128], in0=Sf[:, 0:nt * 128], scalar=0.125, in1=b_ap,
                                                                                                        op0=ALU.mult, op1=ALU.add),
                                  reads=[S_b, b_b], writes=[ss_b])
                            p_, p_b = pT.next()
                            fw.op(fw.act, lambda ss_=ss_, p_=p_: nc.scalar.activation(out=p_[:, 0:nt * 128], in_=ss_[:, 0:nt * 128], func=AF.Exp), reads=[ss_b], writes=[p_b])
                            sl = h - hg0
                            for i, kc in enumerate(kcs):
                                kt, ktb, vt, vtb = kslot[kc]
                                fw.op(fw.pe, lambda i=i, vt=vt, p_=p_, sl=sl, h=h, Ot=Ot: nc.tensor.matmul(Ot[:, sl * 65:(sl + 1) * 65], p_[:, i * 128:(i + 1) * 128], vt[:, h * 65:(h + 1) * 65],
                                                                                                          start=(i == 0), stop=(i == nt - 1)),
                                      reads=[p_b, vtb], writes=[Otb], inc=(i == nt - 1))
                        nh_ = hg1 - hg0
                        rec, recb = rec_r.next()
                        Ov = Ot[:, 0:nh_ * 65].rearrange("p (h d) -> p h d", d=65)
                        fw.op(fw.dve, lambda Ov=Ov, rec=rec, nh_=nh_: nc.vector.reciprocal(out=rec[:, 0:nh_], in_=Ov[:, :, 64]), reads=[Otb], writes=[recb])
                        fw.op(fw.dve, lambda Ov=Ov, rec=rec, nh_=nh_, hg0=hg0, hg1=hg1, on=on: nc.vector.tensor_tensor(out=on[:, hg0:hg1, :], in0=Ov[:, :, 0:64],
                                                                                                               in1=rec[:, 0:nh_].unsqueeze(2).to_broadcast([128, nh_, 64]), op=ALU.mult),
                              reads=[Otb, recb], writes=[onb])
                    Mbf = M[:, 0, :].bitcast(BF16)
                    onf = on[:, :, :].rearrange("p h d -> p (h d)")
                    for k in range(8):
                        fw.op(fw.pe, lambda k=k: nc.tensor.transpose(out=Mbf[:, k * 128:(k + 1) * 128], in_=onf[:, k * 128:(k + 1) * 128], identity=c["idb"][:]),
                              reads=[onb, c["B"]], writes=[Mb], inc=(k == 7))
                    oT, oTb = oT_r.next()
                    fw.op(fw.dve, lambda oT=oT: nc.vector.tensor_copy(out=oT[:, :, :].rearrange("p a b -> p (a b)"), in_=Mbf[:, :]), reads=[Mb], writes=[oTb])
                    for half in range(2):
                        for k in range(8):
                            fw.op(fw.pe, lambda k=k, half=half, oT=oT: nc.tensor.matmul(M[:, half, :], oT[:, k, :], wo[:, k, half * 512:(half + 1) * 512], start=(k == 0), stop=(k == 7)),
                                  reads=[oTb, wob], writes=[Mb], inc=(k == 7 and half == 1))
                    self.backend(be, M, Mb, 128, xr, xrb, dst[tk:tk + 128, :])
            fw.barrier()

    def phase_da2(self):
        nc, fw = self.nc, self.fw
        with ExitStack() as st:
            c = self.consts(st)
            lam = self.sb(st, "da_lam", [128, 4, 64], F32)
            lamb = Buf()
            for i in range(4):
                fw.dma(fw.sp, lam[:, i, :], self.lam_in[i, :].partition_broadcast(128), writes=[lamb])
            lj = self.sb(st, "da_lj", [128, 64], F32)
            lv = self.sb(st, "da_lv", [128, 8], F32)
            lvb = Buf()
            for i in range(2):
                fw.op(fw.dve, lambda i=i: nc.vector.scalar_tensor_tensor(out=lj[:], in0=lam[:, 2 * i, :], scalar=1.0, in1=lam[:, 2 * i + 1, :], op0=ALU.mult, op1=ALU.mult,
                                                                       accum_out=lv[:, i:i + 1]), reads=[lamb], writes=[lvb])
            fw.op(fw.act, lambda: nc.scalar.activation(out=lv[:, 2:4], in_=lv[:, 0:2], func=AF.Exp), reads=[lvb], writes=[lvb])
            fw.op(fw.dve, lambda: nc.vector.scalar_tensor_tensor(out=lv[:, 4:5], in0=lv[:, 3:4], scalar=-LAMBDA_INIT, in1=lv[:, 2:3], op0=ALU.add, op1=ALU.subtract),
                  reads=[lvb], writes=[lvb])
            gsub = self.sb(st, "da_gsub", [128, 128], F32)
            gsubb = Buf()
            fw.dma(fw.sp, gsub[:], self.subln_g.partition_broadcast(128), writes=[gsubb])
            fw.op(fw.dve, lambda: nc.vector.tensor_scalar(out=gsub[:], in0=gsub[:], scalar1=(1.0 - LAMBDA_INIT), scalar2=None, op0=ALU.mult), reads=[gsubb], writes=[gsubb])

            TM = self.TMAX
            NCM = TM // 128
            KT = self.ring(st, "da_KT", 2, [128, TM], BF16)
            QT = self.ring(st, "da_QT", 2, [128, TM], BF16)
            VH = self.ring(st, "da_VH", 2, [128, NCM, 129], BF16)
            E = self.ring(st, "da_E", 3, [128, 2, 256], BF16)
            accs = self.ring(st, "da_accs", 2, [128, 2, 2, 129], F32)
            rec_r = self.ring(st, "da_rec", 2, [128, 2, 2], F32)
            o1_r = self.ring(st, "da_o1", 2, [128, 128], F32)
            o_r = self.ring(st, "da_o", 2, [128, 128], F32)
            oj = self.sb(st, "da_oj", [128, 128], F32)
            ojb = Buf()
            ss_r = self.ring(st, "da_ss", 4, [128, 4], F32)
            on_r = self.ring(st, "da_on", 2, [128, 128], BF16)
            oT_r = self.ring(st, "da_oT", 2, [128, 512], BF16)
            S = [self.ps(st, f"da_S{i}", [128, 2, 512]) for i in range(2)]
            Sb = [Buf(), Buf()]
            A = [[self.ps(st, f"da_A{i}{s}", [128, 512]) for s in range(2)] for i in range(2)]
            Ab = [[Buf(), Buf()], [Buf(), Buf()]]
            si = 0
            for (toff, T) in self.seqs:
                NC_ = T // 128
                blkoff = toff // 128
                for h in range(DA_H):
                    kt, ktb = KT.next()
                    qt_, qtb = QT.next()
                    vh, vhb = VH.next()
                    m, hh = h // 2, h % 2
                    for i in range(2):
                        sl = 2 * hh + i
                        for ab in range(2):
                            prt = i * 64 + ab * 32
                            fw.dma(fw.sp, kt[prt:prt + 32, 0:T], self.DKT[2 * m + ab, sl * 32:sl * 32 + 32, toff:toff + T], writes=[ktb])
                            fw.dma(fw.sp, qt_[prt:prt + 32, 0:T], self.DQT[2 * m + ab, sl * 32:sl * 32 + 32, toff:toff + T], writes=[qtb])
                    fw.dma(fw.sp, vh[:, 0:NC_, :], self.DV[h, :, blkoff:blkoff + NC_, :], writes=[vhb])
                    oT = oTb = None
                    for qi in range(T // 256):
                        q0 = qi * 256
                        for kb in range(NC_):
                            S_, S_b = S[si % 2], Sb[si % 2]
                            si += 1
                            for i in range(2):
                                fw.op(fw.pe, lambda i=i, S_=S_, kb=kb: nc.tensor.matmul(S_[:, i, 0:256], kt[64 * i:64 * i + 64, kb * 128:(kb + 1) * 128], qt_[64 * i:64 * i + 64, q0:q0 + 256],
                                                                                        start=True, stop=True),
                                      reads=[ktb, qtb], writes=[S_b], inc=(i == 1))
                            e, eb = E.next()
                            fw.op(fw.act, lambda e=e, S_=S_: nc.scalar.activation(out=e[:, :, :], in_=S_[:, :, 0:256], func=AF.Exp, scale=0.125), reads=[S_b], writes=[eb])
                            for i in range(2):
                                for s in range(2):
                                    fw.op(fw.pe, lambda i=i, s=s, e=e, kb=kb: nc.tensor.matmul(A[i][s][:, 0:129], e[:, i, 128 * s:128 * s + 128], vh[:, kb, :],
                                                                                               start=(kb == 0), stop=(kb == NC_ - 1)),
                                          reads=[eb, vhb], writes=[Ab[i][s]], inc=(i == 1 and s == 1))
                        ac, acb = accs.next()
                        for i in range(2):
                            for s in range(2):
                                fw.op(fw.dve, lambda i=i, s=s, ac=ac: nc.vector.tensor_copy(out=ac[:, i, s, :], in_=A[i][s][:, 0:129]), reads=[Ab[i][s]], writes=[acb])
                        rec, recb = rec_r.next()
                        fw.op(fw.dve, lambda ac=ac, rec=rec: nc.vector.reciprocal(out=rec[:, :, :], in_=ac[:, :, :, 128]), reads=[acb], writes=[recb])
                        fw.op(fw.dve, lambda rec=rec: nc.vector.tensor_scalar(out=rec[:, 1, :], in0=rec[:, 1, :], scalar1=lv[:, 4:5], scalar2=None, op0=ALU.mult),
                              reads=[recb, lvb], writes=[recb])
                        if qi % 2 == 0:
                            oT, oTb = oT_r.next()
                        for s in range(2):
                            o1, o1b = o1_r.next()
                            o_, o_b = o_r.next()
                            ss, ssb = ss_r.next()
                            on, onb = on_r.next()
                            fw.op(fw.dve, lambda s=s, ac=ac, rec=rec, o1=o1: nc.vector.tensor_scalar(out=o1[:], in0=ac[:, 0, s, 0:128], scalar1=rec[:, 0, s:s + 1], scalar2=None, op0=ALU.mult),
                                  reads=[acb, recb], writes=[o1b])
                            fw.op(fw.dve, lambda s=s, ac=ac, rec=rec, o1=o1, o_=o_: nc.vector.scalar_tensor_tensor(out=o_[:], in0=ac[:, 1, s, 0:128], scalar=rec[:, 1, s:s + 1], in1=o1[:],
                                                                                                               op0=ALU.mult, op1=ALU.add),
                                  reads=[acb, recb, o1b], writes=[o_b])
                            fw.op(fw.dve, lambda o_=o_, ss=ss: nc.vector.scalar_tensor_tensor(out=oj[:], in0=o_[:], scalar=1.0, in1=o_[:], op0=ALU.mult, op1=ALU.mult, accum_out=ss[:, 0:1]),
                                  reads=[o_b], writes=[ojb, ssb])
                            self.rstd(ss, ssb, 128, 1.0 / 128, SUBLN_EPS, c)
                            fw.op(fw.dve, lambda o_=o_, ss=ss, on=on: nc.vector.scalar_tensor_tensor(out=on[:], in0=o_[:], scalar=ss[:, 2:3], in1=gsub[:], op0=ALU.mult, op1=ALU.mult),
                                  reads=[o_b, ssb, gsubb], writes=[onb])
                            Abf = A[0][s][:, :].bitcast(BF16)
                            fw.op(fw.pe, lambda on=on, Abf=Abf: nc.tensor.transpose(out=Abf[:, 0:128], in_=on[:], identity=c["idb"][:]), reads=[onb, c["B"]], writes=[Ab[0][s]])
                            cc = (qi % 2) * 256 + 128 * s
                            fw.op(fw.dve, lambda oT=oT, Abf=Abf, cc=cc: nc.vector.tensor_copy(out=oT[:, cc:cc + 128], in_=Abf[:, 0:128]), reads=[Ab[0][s]], writes=[oTb])
                        if qi % 2 == 1:
                            tq = toff + (qi - 1) * 256
                            fw.dma(fw.sp, self.DOT[h, :, tq:tq + 512], oT[:], reads=[oTb])
            fw.barrier()

    def phase_da3(self, src, dst):
        nc, fw = self.nc, self.fw
        with ExitStack() as st:
            c = self.consts(st)
            gbc, gbcb = self.load_gbc(st, "d3_gbc", 3)
            wo = self.sb(st, "d3_wo", [128, 8, D], BF16)
            wob = Buf()
            fw.dma(fw.pool, wo[:], self.da_wo.rearrange("(kc p) n -> p kc n", p=128), writes=[wob])
            be = self.make_backend(st, c, gbc, gbcb)
            oT = self.ring(st, "d3_oT", 2, [128, 8, 512], BF16)
            M = [self.ps(st, f"d3_M{i}", [128, 2, 512]) for i in range(2)]
            Mb = [Buf(), Buf()]
            mi = 0
            for (toff, T) in self.seqs:
                for sbi in range(T // 512):
                    t0 = toff + 512 * sbi
                    o, ob = oT.next()
                    fw.dma(fw.sp, o[:], self.DOT[:, :, t0:t0 + 512].rearrange("h p t -> p h t"), writes=[ob])
                    for b in range(4):
                        tk = t0 + 128 * b
                        xr, xrb = self.backend_prefetch(be, src[tk:tk + 128, :], 128)
                        M_, M_b = M[mi % 2], Mb[mi % 2]
                        mi += 1
                        for half in range(2):
                            for k in range(8):
                                fw.op(fw.pe, lambda k=k, half=half, M_=M_, b=b, o=o: nc.tensor.matmul(M_[:, half, :], o[:, k, 128 * b:128 * b + 128], wo[:, k, half * 512:(half + 1) * 512],
                                                                                                      start=(k == 0), stop=(k == 7)),
                                      reads=[ob, wob], writes=[M_b], inc=(k == 7 and half == 1))
                        self.backend(be, M_, M_b, 128, xr, xrb, dst[tk:tk + 128, :])
            fw.barrier()

    def build(self, phases="all"):
        fw = self.fw
        P = phases
        if P == "all" or "p0" in P:
            self.phase_p0()
        if P == "all" or "na1" in P:
            self.phase_proj("na", self.x)
        if P == "all" or "na2" in P:
            self.phase_na2(self.XA)
        if P == "all" or "ffn0" in P:
            self.phase_ffn(0, self.XA, self.XB)
        if P == "all" or "da1" in P:
            self.phase_proj("da", self.XB)
        if P == "all" or "da2" in P:
            self.phase_da2()
        if P == "all" or "da3" in P:
            self.phase_da3(self.XB, self.XC)
        if P == "all" or "ffn1" in P:
            self.phase_ffn(1, self.XC, self.y)
        fw.finish()
        return self.nc


def _na_bias_tables(rpb):
    R = 16
    NB = R // 2
    W = 64

    def table(j):
        kcs = Builder.na_kc(j, NB)
        out = np.full((NA_H, 128, len(kcs), 128), NEG, np.float32)
        rq = 2 * j + np.arange(128) // 64
        cq = np.arange(128) % 64
        r0 = np.clip(rq - 4, 0, R - 8)
        c0 = np.clip(cq - 8, 0, W - 16)
        for i, kc in enumerate(kcs):
            rk = 2 * kc + np.arange(128) // 64
            ck = np.arange(128) % 64
            valid = ((rk[:, None] >= r0[None, :]) & (rk[:, None] < r0[None, :] + 8) &
                     (ck[:, None] >= c0[None, :]) & (ck[:, None] < c0[None, :] + 16))
            dr = np.clip(rk[:, None] - rq[None, :] + 7, 0, 14)
            dc = np.clip(ck[:, None] - cq[None, :] + 15, 0, 30)
            vals = rpb[:, dr, dc]
            out[:, :, i, :] = np.where(valid[None], vals, np.float32(NEG))
        return out.reshape(NA_H, 128, len(kcs) * 128)

    interior = table(3)
    assert interior.shape[2] == 640
    sp = np.stack([table(0), table(1), table(NB - 2), table(NB - 1)], 0)
    assert sp.shape[3] == 512
    return np.ascontiguousarray(interior), np.ascontiguousarray(sp)


def _rot_tables(T):
    inv = (1.0 / (ROPE_THETA ** (np.arange(0, 64, 2, dtype=np.float32) / np.float32(64)))).astype(np.float32)
    ang = np.arange(T, dtype=np.float32)[None, :] * inv[np.arange(128) % 32][:, None]
    return np.cos(ang).astype(np.float32), np.sin(ang).astype(np.float32)


def _perm_cols():
    cols = []
    for m in range(4):
        for ab in range(2):
            for sl in range(4):
                for dl in range(32):
                    cols.append((4 * m + sl) * 64 + ab * 32 + dl)
    return np.array(cols)


def prepare_shared(inp, TMAX):
    f = lambda a: np.ascontiguousarray(np.asarray(a, dtype=np.float32))
    perm = _perm_cols()
    bi, bsp = _na_bias_tables(f(inp["na_rpb"])[0])
    cos, sin = _rot_tables(TMAX)
    sh = {
        "gvec": f(np.concatenate([inp["attn_pre_g"], inp["attn_post_g"], inp["ffn_pre_g"], inp["ffn_post_g"]], 0)),
        "w_qkv": f(inp["na_w_qkv"][0]), "na_wo": f(inp["na_w_o"][0]),
        "da_wq": f(np.asarray(inp["da_w_q"][0])[:, perm]), "da_wk": f(np.asarray(inp["da_w_k"][0])[:, perm]),
        "da_wv": f(inp["da_w_v"][0]), "da_wo": f(inp["da_w_o"][0]),
        "w_in": f(inp["ffn_w_in"]), "w_out": f(inp["ffn_w_out"]),
        "conv_w": f(inp["ffn_conv_w"]), "conv_b": f(inp["ffn_conv_b"]),
        "bias_int": bi, "bias_sp": bsp, "rot_cos": cos, "rot_sin": sin,
        "lam_in": f(np.stack([inp["da_lambda_q1"][0], inp["da_lambda_k1"][0], inp["da_lambda_q2"][0], inp["da_lambda_k2"][0]], 0)),
        "subln_g": f(inp["da_subln_g"][0]), "ident": np.eye(128, dtype=np.float32),
    }
    return sh


def kernel(**inp):
    xs = np.asarray(inp["x_sample"], dtype=np.float32)
    xp = np.asarray(inp["x_prompt"], dtype=np.float32)
    NS, TS, _ = xs.shape
    NP_, TP, _ = xp.shape
    ncore = 8
    assert NS == ncore and NP_ * 2 == ncore
    seqs = [(0, TS), (TS, TP)]
    bld = Builder(seqs)
    nc = bld.build()
    sh = prepare_shared(inp, max(TS, TP))
    in_maps = []
    for cidx in range(ncore):
        m = dict(sh)
        m["x"] = np.ascontiguousarray(np.concatenate([xs[cidx], xp[cidx // 2]], 0))
        in_maps.append(m)
    res = run_bass_kernel_spmd(nc, in_maps, core_ids=list(range(ncore)))
    y_s = np.stack([res.results[cidx]["y"][:TS] for cidx in range(ncore)], 0)
    y_p = np.stack([res.results[2 * p]["y"][TS:] for p in range(NP_)], 0)
    return (y_p.astype(np.float32), y_s.astype(np.float32))
```

```python
import math
from contextlib import ExitStack
import numpy as np
import ml_dtypes
import concourse.bass as bass
import concourse.mybir as mybir
from concourse.bass_utils import run_bass_kernel_spmd

F32 = mybir.dt.float32
BF16 = mybir.dt.bfloat16
AF = mybir.ActivationFunctionType
ALU = mybir.AluOpType

D = 1024
DFF = 2816
NFC = 22
NA_H = 16
DA_H = 8
NEG = -30000.0
NORM_EPS = 1e-6
SUBLN_EPS = 1e-5
LAMBDA_INIT = 0.8 - 0.6 * math.exp(-0.3 * 1)
ROPE_THETA = 10000.0


class Buf:
    __slots__ = ("name", "w", "r")

    def __init__(self, name=""):
        self.name = name
        self.w = None
        self.r = {}


class Eng:
    def __init__(self, nc, h, name):
        self.h = h
        self.name = name
        self.sem = nc.alloc_semaphore(name="s_" + name)
        self.count = 0
        self.known = {}


class FW:
    NDMASEM = 48

    def __init__(self, nc):
        self.nc = nc
        self.pe = Eng(nc, nc.tensor, "pe")
        self.act = Eng(nc, nc.scalar, "act")
        self.dve = Eng(nc, nc.vector, "dve")
        self.pool = Eng(nc, nc.gpsimd, "pool")
        self.sp = Eng(nc, nc.sync, "sp")
        self.engs = [self.pe, self.act, self.dve, self.pool, self.sp]
        self.dsems = [nc.alloc_semaphore(name=f"d{i}") for i in range(self.NDMASEM)]
        self.dval = [0] * self.NDMASEM
        self.dnext = 0
        self.nops = 0
        self.ndma = 0
        self.rec = None

    def wait(self, eng, tok):
        if tok is None:
            return
        sem, val = tok
        if sem is eng.sem:
            if eng is self.pe:
                return
            if val > eng.count:
                return
        k = id(sem)
        if eng.known.get(k, 0) >= val:
            return
        eng.h.wait_ge(sem, val)
        eng.known[k] = val
        if self.rec is not None:
            self.rec[eng.name].append(("w", k, val))

    def _deps(self, eng, reads, writes):
        for b in reads:
            self.wait(eng, b.w)
        for b in writes:
            self.wait(eng, b.w)
            for tok in list(b.r.values()):
                self.wait(eng, tok)

    def _commit(self, tok, reads, writes):
        k = id(tok[0])
        for b in reads:
            if k not in b.r or b.r[k][1] < tok[1]:
                b.r[k] = tok
        for b in writes:
            b.w = tok
            b.r = {}

    def op(self, eng, fn, reads=(), writes=(), inc=True):
        self._deps(eng, reads, writes)
        ins = fn()
        self.nops += 1
        if inc:
            eng.count += 1
            ins.then_inc(eng.sem, 1)
            tok = (eng.sem, eng.count)
            if self.rec is not None:
                self.rec[eng.name].append(("i", id(eng.sem), 1))
        else:
            tok = (eng.sem, eng.count + 1)
        self._commit(tok, reads, writes)
        return tok

    def dma(self, eng, out, in_, reads=(), writes=(), **kw):
        self._deps(eng, reads, writes)
        i = self.dnext
        self.dnext = (self.dnext + 1) % self.NDMASEM
        sem = self.dsems[i]
        if self.dval[i] > 0:
            self.wait(eng, (sem, self.dval[i]))
        self.dval[i] += 16
        eng.h.dma_start(out=out, in_=in_, **kw).then_inc(sem, 16)
        self.ndma += 1
        if self.rec is not None:
            self.rec[eng.name].append(("i", id(sem), 16))
        tok = (sem, self.dval[i])
        self._commit(tok, reads, writes)
        return tok

    def barrier(self):
        for e in self.engs:
            for o in self.engs:
                if o is not e and o.count > 0:
                    self.wait(e, (o.sem, o.count))
            for i, sem in enumerate(self.dsems):
                if self.dval[i] > 0:
                    self.wait(e, (sem, self.dval[i]))

    def finish(self):
        for i, sem in enumerate(self.dsems):
            if self.dval[i] > 0:
                self.wait(self.sp, (sem, self.dval[i]))
        for e in (self.pe, self.act, self.dve, self.pool):
            if e.count > 0:
                self.wait(self.sp, (e.sem, e.count))


class Ring:
    def __init__(self, tiles):
        self.tiles = tiles
        self.bufs = [Buf() for _ in tiles]
        self.i = -1

    def next(self):
        self.i = (self.i + 1) % len(self.tiles)
        return self.tiles[self.i], self.bufs[self.i]

    def cur(self):
        return self.tiles[self.i], self.bufs[self.i]


class Builder:
    def __init__(self, seqs, debug=False):
        self.seqs = seqs
        self.NT = sum(T for _, T in seqs)
        self.TMAX = max(T for _, T in seqs)
        self.debug = debug
        self.nc = nc = bass.Bass("TRN2", target_bir_lowering=False)
        self.fw = FW(nc)
        NT = self.NT
        dt = nc.dram_tensor
        I = "ExternalInput"
        self.x = dt("x", [NT, D], F32, kind=I).ap()
        self.gvec = dt("gvec", [8, D], F32, kind=I).ap()
        self.w_qkv = dt("w_qkv", [D, 3 * D], F32, kind=I).ap()
        self.na_wo = dt("na_wo", [D, D], F32, kind=I).ap()
        self.da_wq = dt("da_wq", [D, D], F32, kind=I).ap()
        self.da_wk = dt("da_wk", [D, D], F32, kind=I).ap()
        self.da_wv = dt("da_wv", [D, D], F32, kind=I).ap()
        self.da_wo = dt("da_wo", [D, D], F32, kind=I).ap()
        self.w_in = dt("w_in", [2, D, 2 * DFF], F32, kind=I).ap()
        self.w_out = dt("w_out", [2, DFF, D], F32, kind=I).ap()
        self.conv_w = dt("conv_w", [2, 3, 2 * DFF], F32, kind=I).ap()
        self.conv_b = dt("conv_b", [2, 2 * DFF], F32, kind=I).ap()
        self.bias_int = dt("bias_int", [NA_H, 128, 640], F32, kind=I).ap()
        self.bias_sp = dt("bias_sp", [4, NA_H, 128, 512], F32, kind=I).ap()
        self.rot_cos = dt("rot_cos", [128, self.TMAX], F32, kind=I).ap()
        self.rot_sin = dt("rot_sin", [128, self.TMAX], F32, kind=I).ap()
        self.lam_in = dt("lam_in", [4, 64], F32, kind=I).ap()
        self.subln_g = dt("subln_g", [128], F32, kind=I).ap()
        self.ident = dt("ident", [128, 128], F32, kind=I).ap()
        self.y = dt("y", [NT, D], F32, kind="ExternalOutput").ap()
        S = "ExternalOutput" if debug else "Internal"
        self.XA = dt("XA", [NT, D], F32, kind=S).ap()
        self.XB = dt("XB", [NT, D], F32, kind=S).ap()
        self.WINB = dt("WINB", [2, 11, 128, 8, 512], BF16, kind="Internal").ap()
        NBLK = NT // 128
        self.NQT = dt("NQT", [NBLK, 128, 8, 128], BF16, kind="Internal").ap()
        self.NKT = dt("NKT", [NBLK, 128, 8, 128], BF16, kind="Internal").ap()
        self.NV = dt("NV", [128, NBLK, NA_H * 65], BF16, kind="Internal").ap()
        self.DQT = dt("DQT", [8, 128, NT], BF16, kind="Internal").ap()
        self.DKT = dt("DKT", [8, 128, NT], BF16, kind="Internal").ap()
        self.DV = dt("DV", [DA_H, 128, NBLK, 129], BF16, kind="Internal").ap()
        self.DOT = dt("DOT", [DA_H, 128, NT], BF16, kind="Internal").ap()
        self.XC = self.XA

    _uid = 0

    def _nm(self, name):
        Builder._uid += 1
        return f"{name}_{Builder._uid}"

    def sb(self, st, name, shape, dtype):
        return st.enter_context(self.nc.sbuf_tensor(self._nm(name), shape, dtype))

    def ps(self, st, name, shape, dtype=F32):
        return st.enter_context(self.nc.psum_tensor(self._nm(name), shape, dtype))

    def ring(self, st, name, n, shape, dtype):
        return Ring([self.sb(st, f"{name}{i}", shape, dtype) for i in range(n)])

    def consts(self, st):
        nc, fw = self.nc, self.fw
        c = {}
        c["idf"] = self.sb(st, "idf", [128, 128], F32)
        c["idb"] = self.sb(st, "idb", [128, 128], BF16)
        c["nh"] = self.sb(st, "nhalf", [128, 1], F32)
        c["B"] = Buf("consts")
        fw.dma(fw.sp, c["idf"][:], self.ident[:, :], writes=[c["B"]])
        fw.dma(fw.pool, c["idb"][:], self.ident[:, :], writes=[c["B"]])
        fw.op(fw.pool, lambda: nc.gpsimd.memset(c["nh"][:], -0.5), writes=[c["B"]])
        return c

    def load_gfm(self, st, name, row):
        nc, fw = self.nc, self.fw
        t = self.sb(st, name, [128, 8], F32)
        b = Buf(name)
        with nc.allow_non_contiguous_dma(reason="tiny gain vector"):
            fw.dma(fw.sp, t[:], self.gvec[row, :].rearrange("(c p) -> p c", p=128), writes=[b])
        return t, b

    def load_gbc(self, st, name, row):
        fw = self.fw
        t = self.sb(st, name, [128, D], F32)
        b = Buf(name)
        fw.dma(fw.sp, t[:], self.gvec[row, :].partition_broadcast(128), writes=[b])
        return t, b

    def rstd(self, ss, ssb, n, inv_dim, eps, c):
        nc, fw = self.nc, self.fw
        fw.op(fw.pool, lambda: nc.gpsimd.tensor_scalar(out=ss[:n, 1:2], in0=ss[:n, 0:1], scalar1=inv_dim, scalar2=eps,
                                                        op0=ALU.mult, op1=ALU.add), reads=[ssb], writes=[ssb])
        fw.op(fw.pool, lambda: nc.gpsimd.tensor_tensor(out=ss[:n, 2:3], in0=ss[:n, 1:2], in1=c["nh"][:n, :], op=ALU.pow),
              reads=[ssb, c["B"]], writes=[ssb])

    def make_frontend(self, st, c, gfm, gfmb, tp, tpb):
        fe = {"xt": self.ring(st, "fe_xt", 3, [128, D], F32), "xs": self.ring(st, "fe_xs", 2, [128, D], F32),
              "junk": self.sb(st, "fe_junk", [128, D], BF16), "junkb": Buf(),
              "ss": self.ring(st, "fe_ss", 4, [128, 4], F32), "c": c, "gfm": gfm, "gfmb": gfmb, "tp": tp, "tpb": tpb}
        return fe

    def frontend(self, fe, src, n, hT, hTb, col0):
        nc, fw = self.nc, self.fw
        c = fe["c"]
        xt, xtb = fe["xt"].next()
        xs, xsb = fe["xs"].next()
        ss, ssb = fe["ss"].next()
        tp, tpb = fe["tp"], fe["tpb"]
        fw.dma(fw.sp, xt[:n, :], src, writes=[xtb])
        fw.op(fw.act, lambda: nc.scalar.activation(out=fe["junk"][:n, :], in_=xt[:n, :], func=AF.Square, accum_out=ss[:n, 0:1]),
              reads=[xtb], writes=[fe["junkb"], ssb])
        self.rstd(ss, ssb, n, 1.0 / D, NORM_EPS, c)
        fw.op(fw.act, lambda: nc.scalar.activation(out=xs[:n, :], in_=xt[:n, :], func=AF.Copy, scale=ss[:n, 2:3]),
              reads=[xtb, ssb], writes=[xsb])
        for k in range(8):
            fw.op(fw.pe, lambda k=k: nc.tensor.transpose(out=tp[:, k, :n], in_=xs[:n, k * 128:(k + 1) * 128], identity=c["idf"][:n, :n]),
                  reads=[xsb, c["B"]], writes=[tpb], inc=(k == 7))
        fw.op(fw.dve, lambda: nc.vector.tensor_tensor(out=hT[:, :, col0:col0 + n], in0=tp[:, :, :n],
                                                      in1=fe["gfm"][:, :].unsqueeze(2).to_broadcast([128, 8, n]), op=ALU.mult),
              reads=[tpb, fe["gfmb"]], writes=[hTb])

    def make_backend(self, st, c, gbc, gbcb):
        be = {"xr": self.ring(st, "be_xr", 3, [128, D], F32), "t": self.ring(st, "be_t", 2, [128, D], F32),
              "o": self.ring(st, "be_o", 3, [128, D], F32), "junk": self.sb(st, "be_junk", [128, D], BF16), "junkb": Buf(),
              "ss": self.ring(st, "be_ss", 4, [128, 4], F32), "c": c, "gbc": gbc, "gbcb": gbcb}
        return be

    def backend_prefetch(self, be, res_src, n):
        xr, xrb = be["xr"].next()
        self.fw.dma(self.fw.sp, xr[:n, :], res_src, writes=[xrb])
        return xr, xrb

    def backend(self, be, m, mb, n, xr, xrb, dst):
        nc, fw = self.nc, self.fw
        ss, ssb = be["ss"].next()
        t, tb = be["t"].next()
        o, ob = be["o"].next()
        m2 = m[:n, :, :].rearrange("p a b -> p (a b)")
        fw.op(fw.act, lambda: nc.scalar.activation(out=be["junk"][:n, :], in_=m2, func=AF.Square, accum_out=ss[:n, 0:1]),
              reads=[mb], writes=[be["junkb"], ssb])
        self.rstd(ss, ssb, n, 1.0 / D, NORM_EPS, be["c"])
        fw.op(fw.dve, lambda: nc.vector.scalar_tensor_tensor(out=t[:n, :], in0=m2, scalar=ss[:n, 2:3], in1=be["gbc"][:n, :],
                                                             op0=ALU.mult, op1=ALU.mult), reads=[mb, ssb, be["gbcb"]], writes=[tb])
        fw.op(fw.pool, lambda: nc.gpsimd.tensor_tensor(out=o[:n, :], in0=t[:n, :], in1=xr[:n, :], op=ALU.add),
              reads=[tb, xrb], writes=[ob])
        fw.dma(fw.sp, dst, o[:n, :], reads=[ob])

    def phase_p0(self):
        nc, fw = self.nc, self.fw
        with ExitStack() as st:
            stg = self.ring(st, "p0_stg", 3, [128, 8, 512], BF16)
            for l in range(2):
                for g in range(11):
                    t, tb = stg.next()
                    src = self.w_in[l]
                    fw.dma(fw.pool, t[:, :, 0:256], src[:, 256 * g:256 * g + 256].rearrange("(kc p) n -> p kc n", p=128), writes=[tb])
                    fw.dma(fw.pool, t[:, :, 256:512], src[:, DFF + 256 * g:DFF + 256 * g + 256].rearrange("(kc p) n -> p kc n", p=128), writes=[tb])
                    fw.dma(fw.sp, self.WINB[l, g], t[:], reads=[tb])
            fw.barrier()

    def phase_ffn(self, l, src, dst):
        nc, fw = self.nc, self.fw
        with ExitStack() as st:
            c = self.consts(st)
            gfm, gfmb = self.load_gfm(st, "ffn_gfm", 4 + l)
            gbc, gbcb = self.load_gbc(st, "ffn_gbc", 6 + l)
            wout = self.sb(st, "ffn_wout", [128, NFC, D], BF16)
            woutb = Buf()
            for f0 in range(0, NFC, 6):
                f1 = min(NFC, f0 + 6)
                fw.dma(fw.pool, wout[:, f0:f1, :], self.w_out[l, f0 * 128:f1 * 128, :].rearrange("(f p) n -> p f n", p=128), writes=[woutb])
            cw = self.sb(st, "ffn_cw", [128, 3, 2 * NFC], F32)
            cb = self.sb(st, "ffn_cb", [128, 2 * NFC], F32)
            cwb = Buf()
            with nc.allow_non_contiguous_dma(reason="small conv params"):
                for k in range(3):
                    fw.dma(fw.sp, cw[:, k, :], self.conv_w[l, k, :].rearrange("(c p) -> p c", p=128), writes=[cwb])
                fw.dma(fw.sp, cb[:, :], self.conv_b[l, :].rearrange("(c p) -> p c", p=128), writes=[cwb])
            tp = self.ps(st, "ffn_tp", [128, 8, 128])
            tpb = Buf()
            pgu = [self.ps(st, f"ffn_pgu{i}", [128, 2, 512]) for i in range(2)]
            pgub = [[Buf(), Buf()] for _ in range(2)]
            pm = self.ps(st, "ffn_pm", [128, 2, 512])
            pmb = Buf()
            fe = self.make_frontend(st, c, gfm, gfmb, tp, tpb)
            be = self.make_backend(st, c, gbc, gbcb)
            hT = self.ring(st, "ffn_hT", 2, [128, 8, 512], BF16)
            wg = self.ring(st, "ffn_wg", 3, [128, 8, 512], BF16)
            aT = self.sb(st, "ffn_aT", [128, NFC, 513], BF16)
            aTb = Buf()
            carry = self.sb(st, "ffn_carry", [128, 2 * NFC, 2], F32)
            carryb = [Buf() for _ in range(2 * NFC)]
            R = self.ring(st, "ffn_R", 4, [128, 515], F32)
            T1 = self.ring(st, "ffn_T1", 3, [128, 513], F32)
            T2 = self.ring(st, "ffn_T2", 3, [128, 513], F32)
            TG = self.ring(st, "ffn_TG", 2, [128, 513], F32)
            TU = self.ring(st, "ffn_TU", 2, [128, 513], F32)
            GG = self.ring(st, "ffn_GG", 2, [128, 513], F32)
            for t_, b_ in zip(R.tiles, R.bufs):
                fw.op(fw.pool, lambda t_=t_: nc.gpsimd.memset(t_[:, :], 0.0), writes=[b_])

            for (toff, T) in self.seqs:
                ntile = T // 512
                fw.op(fw.pool, lambda: nc.gpsimd.memset(carry[:, :, :], 0.0), writes=carryb)

                def do_frontend(i):
                    h, hb = hT.next()
                    for b in range(4):
                        t0 = toff + 512 * i + 128 * b
                        self.frontend(fe, src[t0:t0 + 128, :], 128, h, hb, 128 * b)
                    return h, hb

                nxt = do_frontend(0)
                for i in range(ntile):
                    h, hb = nxt
                    last = (i == ntile - 1)
                    W = 513 if last else 512
                    gi = 0
                    wq = []
                    for g in range(2):
                        w_, wb_ = wg.next()
                        fw.dma(fw.sp, w_[:], self.WINB[l, g], writes=[wb_])
                        wq.append((w_, wb_))
                    for g in range(11):
                        if g + 2 < 11:
                            w_, wb_ = wg.next()
                            fw.dma(fw.sp, w_[:], self.WINB[l, g + 2], writes=[wb_])
                            wq.append((w_, wb_))
                        w_, wb_ = wq[g]
                        for jf in range(2):
                            f = 2 * g + jf
                            pi = f % 2
                            P_ = pgu[pi]
                            Pb = pgub[pi]
                            tg = tu = None
                            for gu in range(2):
                                cidx = f + NFC * gu
                                col = 256 * gu + 128 * jf
                                for kc in range(8):
                                    fw.op(fw.pe, lambda kc=kc, col=col, gu=gu: nc.tensor.matmul(P_[:, gu, :], w_[:, kc, col:col + 128], h[:, kc, :],
                                                                                             start=(kc == 0), stop=(kc == 7)),
                                          reads=[wb_, hb], writes=[Pb[gu]], inc=(kc == 7))
                                r, rb = R.next()
                                fw.op(fw.act, lambda r=r, gu=gu: nc.scalar.activation(out=r[:, 2:514], in_=P_[:, gu, :], func=AF.Copy),
                                      reads=[Pb[gu]], writes=[rb])
                                fw.op(fw.pool, lambda r=r, cidx=cidx: nc.gpsimd.tensor_copy(out=r[:, 0:2], in_=carry[:, cidx, :]),
                                      reads=[carryb[cidx]], writes=[rb])
                                fw.op(fw.pool, lambda r=r, cidx=cidx: nc.gpsimd.tensor_copy(out=carry[:, cidx, :], in_=r[:, 512:514]),
                                      reads=[rb], writes=[carryb[cidx]])
                                t1, t1b = T1.next()
                                t2, t2b = T2.next()
                                t3, t3b = (TG if gu == 0 else TU).next()
                                fw.op(fw.act, lambda r=r, t1=t1, cidx=cidx: nc.scalar.activation(out=t1[:, 0:W], in_=r[:, 1:1 + W], func=AF.Identity,
                                                                                                 scale=cw[:, 1, cidx:cidx + 1], bias=cb[:, cidx:cidx + 1]),
                                      reads=[rb, cwb], writes=[t1b])
                                fw.op(fw.dve, lambda r=r, t1=t1, t2=t2, cidx=cidx: nc.vector.scalar_tensor_tensor(out=t2[:, 0:W], in0=r[:, 0:W], scalar=cw[:, 0, cidx:cidx + 1],
                                                                                                               in1=t1[:, 0:W], op0=ALU.mult, op1=ALU.add),
                                      reads=[rb, t1b, cwb], writes=[t2b])
                                fw.op(fw.dve, lambda r=r, t2=t2, t3=t3, cidx=cidx: nc.vector.scalar_tensor_tensor(out=t3[:, 0:W], in0=r[:, 2:2 + W], scalar=cw[:, 2, cidx:cidx + 1],
                                                                                                               in1=t2[:, 0:W], op0=ALU.mult, op1=ALU.add),
                                      reads=[rb, t2b, cwb], writes=[t3b])
                                if gu == 0:
                                    tg, tgb = t3, t3b
                                else:
                                    tu, tub = t3, t3b
                            gg, ggb = GG.next()
                            fw.op(fw.act, lambda gg=gg, tg=tg: nc.scalar.activation(out=gg[:, 0:W], in_=tg[:, 0:W], func=AF.Gelu_apprx_tanh),
                                  reads=[tgb], writes=[ggb])
                            fw.op(fw.pool, lambda gg=gg, tu=tu, f=f: nc.gpsimd.tensor_tensor(out=aT[:, f, 0:W], in0=gg[:, 0:W], in1=tu[:, 0:W], op=ALU.mult),
                                  reads=[ggb, tub], writes=[aTb])
                    if not last:
                        nxt = do_frontend(i + 1)
                    j0 = 1 if i == 0 else 0
                    blocks = []
                    j = j0
                    while j < 512:
                        n = min(128, 512 - j)
                        blocks.append((j, n))
                        j += n
                    if W == 513:
                        blocks.append((512, 1))
                    for (j, n) in blocks:
                        tk = toff + 512 * i - 1 + j
                        xr, xrb = self.backend_prefetch(be, src[tk:tk + n, :], n)
                        for half in range(2):
                            for f in range(NFC):
                                fw.op(fw.pe, lambda f=f, half=half, j=j, n=n: nc.tensor.matmul(pm[:n, half, :], aT[:, f, j:j + n], wout[:, f, half * 512:(half + 1) * 512],
                                                                                              start=(f == 0), stop=(f == NFC - 1)),
                                      reads=[aTb, woutb], writes=[pmb], inc=(f == NFC - 1 and half == 1))
                        self.backend(be, pm, pmb, n, xr, xrb, dst[tk:tk + n, :])
            fw.barrier()

    def phase_proj(self, mode, src):
        nc, fw = self.nc, self.fw
        na = (mode == "na")
        with ExitStack() as st:
            c = self.consts(st)
            gfm, gfmb = self.load_gfm(st, "pj_gfm", 0 if na else 1)
            w = self.sb(st, "pj_w", [128, 8, 3 * D], BF16)
            wb = Buf()
            if na:
                for k3 in range(3):
                    fw.dma(fw.pool, w[:, :, k3 * D:(k3 + 1) * D], self.w_qkv[:, k3 * D:(k3 + 1) * D].rearrange("(kc p) n -> p kc n", p=128), writes=[wb])
            else:
                for k3, ww in enumerate((self.da_wq, self.da_wk, self.da_wv)):
                    fw.dma(fw.pool, w[:, :, k3 * D:(k3 + 1) * D], ww.rearrange("(kc p) n -> p kc n", p=128), writes=[wb])
            tp = self.ps(st, "pj_tp", [128, 8, 128])
            tpb = Buf()
            pp = [self.ps(st, f"pj_pp{i}", [128, 2, 512]) for i in range(3)]
            ppb = [[Buf(), Buf()] for _ in range(3)]
            fe = self.make_frontend(st, c, gfm, gfmb, tp, tpb)
            hT = self.ring(st, "pj_hT", 2, [128, 8, 512], BF16)
            if na:
                stq = self.ring(st, "pj_stq", 2, [128, 4, 8, 128], BF16)
                stk = self.ring(st, "pj_stk", 2, [128, 4, 8, 128], BF16)
                vaug = self.ring(st, "pj_vaug", 2, [128, 4, NA_H, 65], BF16)
                for t_, b_ in zip(vaug.tiles, vaug.bufs):
                    fw.op(fw.pool, lambda t_=t_: nc.gpsimd.memset(t_[:, :, :, 64:65], 1.0), writes=[b_])
            else:
                stq = self.ring(st, "pj_stq", 2, [128, 8, 512], BF16)
                stk = self.ring(st, "pj_stk", 2, [128, 8, 512], BF16)
                vaug = self.ring(st, "pj_vaug", 2, [128, DA_H, 4, 129], BF16)
                for t_, b_ in zip(vaug.tiles, vaug.bufs):
                    fw.op(fw.pool, lambda t_=t_: nc.gpsimd.memset(t_[:, :, :, 128:129], 1.0), writes=[b_])
                cosr = self.ring(st, "pj_cos", 2, [128, 512], F32)
                sinr = self.ring(st, "pj_sin", 2, [128, 512], F32)
                rt = [self.ring(st, f"pj_rt{i}", 2, [128, 512], F32) for i in range(4)]
            pi = 0
            for (toff, T) in self.seqs:
                for sbi in range(T // 512):
                    t0 = toff + 512 * sbi
                    blk0 = t0 // 128
                    h, hb = hT.next()
                    for b in range(4):
                        self.frontend(fe, src[t0 + 128 * b:t0 + 128 * b + 128, :], 128, h, hb, 128 * b)
                    if not na:
                        cs, csb = cosr.next()
                        sn, snb = sinr.next()
                        fw.dma(fw.sp, cs[:], self.rot_cos[:, 512 * sbi:512 * sbi + 512], writes=[csb])
                        fw.dma(fw.sp, sn[:], self.rot_sin[:, 512 * sbi:512 * sbi + 512], writes=[snb])
                    for qk in range(2):
                        sg, sgb = (stq if qk == 0 else stk).next()
                        if na:
                            for oc in range(8):
                                P_, Pb = pp[pi % 3], ppb[pi % 3]
                                pi += 1
                                col = qk * D + oc * 128
                                for kc in range(8):
                                    fw.op(fw.pe, lambda kc=kc, col=col, P_=P_: nc.tensor.matmul(P_[:, 0, :], w[:, kc, col:col + 128], h[:, kc, :], start=(kc == 0), stop=(kc == 7)),
                                          reads=[wb, hb], writes=[Pb[0]], inc=(kc == 7))
                                if oc % 2 == 0:
                                    fw.op(fw.act, lambda P_=P_, oc=oc, sg=sg: nc.scalar.activation(out=sg[:, :, oc, :], in_=P_[:, 0, :].rearrange("p (b t) -> p b t", b=4), func=AF.Copy),
                                          reads=[Pb[0]], writes=[sgb])
                                else:
                                    fw.op(fw.dve, lambda P_=P_, oc=oc, sg=sg: nc.vector.tensor_copy(out=sg[:, :, oc, :], in_=P_[:, 0, :].rearrange("p (b t) -> p b t", b=4)),
                                          reads=[Pb[0]], writes=[sgb])
                            dstT = (self.NQT if qk == 0 else self.NKT)
                            fw.dma(fw.sp, dstT[blk0:blk0 + 4].rearrange("b p o t -> p b o t"), sg[:], reads=[sgb])
                        else:
                            for m in range(4):
                                P_, Pb = pp[pi % 3], ppb[pi % 3]
                                pi += 1
                                for ab in range(2):
                                    col = qk * D + m * 256 + ab * 128
                                    for kc in range(8):
                                        fw.op(fw.pe, lambda kc=kc, col=col, P_=P_, ab=ab: nc.tensor.matmul(P_[:, ab, :], w[:, kc, col:col + 128], h[:, kc, :], start=(kc == 0), stop=(kc == 7)),
                                              reads=[wb, hb], writes=[Pb[ab]], inc=(kc == 7))
                                t1, t1b = rt[0].next()
                                t2, t2b = rt[1].next()
                                t3, t3b = rt[2].next()
                                t4, t4b = rt[3].next()
                                TT = nc.vector.tensor_tensor
                                fw.op(fw.dve, lambda P_=P_, t1=t1: TT(out=t1[:], in0=P_[:, 0, :], in1=cs[:], op=ALU.mult), reads=[Pb[0], csb], writes=[t1b])
                                fw.op(fw.dve, lambda P_=P_, t2=t2: TT(out=t2[:], in0=P_[:, 1, :], in1=sn[:], op=ALU.mult), reads=[Pb[1], snb], writes=[t2b])
                                fw.op(fw.dve, lambda P_=P_, t3=t3: TT(out=t3[:], in0=P_[:, 1, :], in1=cs[:], op=ALU.mult), reads=[Pb[1], csb], writes=[t3b])
                                fw.op(fw.dve, lambda P_=P_, t4=t4: TT(out=t4[:], in0=P_[:, 0, :], in1=sn[:], op=ALU.mult), reads=[Pb[0], snb], writes=[t4b])
                                fw.op(fw.pool, lambda t1=t1, t2=t2, m=m, sg=sg: nc.gpsimd.tensor_tensor(out=sg[:, 2 * m, :], in0=t1[:], in1=t2[:], op=ALU.subtract),
                                      reads=[t1b, t2b], writes=[sgb])
                                fw.op(fw.pool, lambda t3=t3, t4=t4, m=m, sg=sg: nc.gpsimd.tensor_tensor(out=sg[:, 2 * m + 1, :], in0=t3[:], in1=t4[:], op=ALU.add),
                                      reads=[t3b, t4b], writes=[sgb])
                            dstT = (self.DQT if qk == 0 else self.DKT)
                            fw.dma(fw.sp, dstT[:, :, t0:t0 + 512].rearrange("o p t -> p o t"), sg[:], reads=[sgb])
                    va, vab = vaug.next()
                    for b in range(4):
                        P_, Pb = pp[pi % 3], ppb[pi % 3]
                        pi += 1
                        for half in range(2):
                            col = 2 * D + half * 512
                            for kc in range(8):
                                fw.op(fw.pe, lambda kc=kc, col=col, P_=P_, half=half, b=b: nc.tensor.matmul(P_[:, half, :], h[:, kc, 128 * b:128 * b + 128], w[:, kc, col:col + 512],
                                                                                                            start=(kc == 0), stop=(kc == 7)),
                                      reads=[wb, hb], writes=[Pb[half]], inc=(kc == 7))
                            if na:
                                o_ap = va[:, b, 8 * half:8 * half + 8, 0:64]
                                i_ap = P_[:, half, :].rearrange("p (h d) -> p h d", h=8)
                            else:
                                o_ap = va[:, 4 * half:4 * half + 4, b, 0:128]
                                i_ap = P_[:, half, :].rearrange("p (h d) -> p h d", h=4)
                            if half == 0:
                                fw.op(fw.act, lambda o_ap=o_ap, i_ap=i_ap: nc.scalar.activation(out=o_ap, in_=i_ap, func=AF.Copy), reads=[Pb[half]], writes=[vab])
                            else:
                                fw.op(fw.dve, lambda o_ap=o_ap, i_ap=i_ap: nc.vector.tensor_copy(out=o_ap, in_=i_ap), reads=[Pb[half]], writes=[vab])
                    if na:
                        fw.dma(fw.sp, self.NV[:, blk0:blk0 + 4, :], va[:].rearrange("p b h d -> p b (h d)"), reads=[vab])
                    else:
                        fw.dma(fw.sp, self.DV[:, :, blk0:blk0 + 4, :].rearrange("h p b x -> p h b x"), va[:], reads=[vab])
            fw.barrier()

    @staticmethod
    def na_kc(j, NB):
        R = 2 * NB
        r0a = min(max(2 * j - 4, 0), R - 8)
        r0b = min(max(2 * j + 1 - 4, 0), R - 8)
        return list(range(r0a // 2, (r0b + 7) // 2 + 1))

    def phase_na2(self, dst):
        nc, fw = self.nc, self.fw
        with ExitStack() as st:
            c = self.consts(st)
            gbc, gbcb = self.load_gbc(st, "na_gbc", 2)
            wo = self.sb(st, "na_wo", [128, 8, D], BF16)
            wob = Buf()
            ORDER = [0, 2, 4, 6, 8, 10, 12, 14, 1, 3, 5, 7, 9, 11, 13, 15]
            for pos, hh_ in enumerate(ORDER):
                fw.dma(fw.pool, wo[64 * (pos % 2):64 * (pos % 2) + 64, pos // 2, :], self.na_wo[hh_ * 64:hh_ * 64 + 64, :], writes=[wob])
            bint = self.sb(st, "na_bint", [128, NA_H, 640], F32)
            bintb = Buf()
            for h0 in range(0, NA_H, 4):
                fw.dma(fw.sp, bint[:, h0:h0 + 4, :], self.bias_int[h0:h0 + 4].rearrange("h p n -> p h n"), writes=[bintb])
            bsp = self.ring(st, "na_bsp", 3, [128, 512], F32)
            kring = self.ring(st, "na_k", 10, [128, 8, 128], BF16)
            vring = self.ring(st, "na_v", 10, [128, NA_H * 65], BF16)
            qr = self.ring(st, "na_q", 3, [128, 8, 128], BF16)
            ssb_r = self.ring(st, "na_ssb", 3, [128, 640], F32)
            pT = self.ring(st, "na_pT", 3, [128, 640], BF16)
            on_r = self.ring(st, "na_on", 2, [128, NA_H, 64], BF16)
            oT_r = self.ring(st, "na_oT", 2, [128, 8, 128], BF16)
            rec_r = self.ring(st, "na_rec", 4, [128, 8], F32)
            be = self.make_backend(st, c, gbc, gbcb)
            Sall = self.ps(st, "na_S", [128, 4, 512])
            Sflat = Sall[:, :, :].rearrange("p a b -> p (a b)")
            Sb = [Buf(), Buf()]
            NREG, RSTR = 2, 1024
            O = [self.ps(st, f"na_O{i}", [128, 512]) for i in range(2)]
            Ob = [Buf(), Buf()]
            M = self.ps(st, "na_M", [128, 2, 512])
            Mb = Buf()
            Mbf = M[:, 0, :].bitcast(BF16)
            GROUPS = ((0, 7, 0), (7, 14, 1), (14, 16, 0))
            units = []
            for si_, (toff, T) in enumerate(self.seqs):
                for j in range(T // 128):
                    for h in range(NA_H):
                        units.append((si_, j, h))
            blocks = {}
            seqstate = {}
            pend = []

            def block_setup(si_, j):
                toff, T = self.seqs[si_]
                NB = T // 128
                blkoff = toff // 128
                if j == 0:
                    seqstate[si_] = {"loaded": -1, "kslot": {}}
                ss_ = seqstate[si_]
                kcs = self.na_kc(j, NB)
                upto = min(max(kcs) + 1, NB - 1)
                while ss_["loaded"] < upto:
                    ss_["loaded"] += 1
                    kt, ktb = kring.next()
                    vt, vtb = vring.next()
                    fw.dma(fw.sp, kt[:], self.NKT[blkoff + ss_["loaded"]], writes=[ktb])
                    fw.dma(fw.sp, vt[:], self.NV[:, blkoff + ss_["loaded"], :], writes=[vtb])
                    ss_["kslot"][ss_["loaded"]] = (kt, ktb, vt, vtb)
                if j < 2:
                    sp_idx = j
                elif j >= NB - 2:
                    sp_idx = 2 + (j - (NB - 2))
                else:
                    sp_idx = None
                q, qb = qr.next()
                fw.dma(fw.sp, q[:], self.NQT[blkoff + j], writes=[qb])
                tk = toff + 128 * j
                xr, xrb = self.backend_prefetch(be, self.x[tk:tk + 128, :], 128)
                on, onb = on_r.next()
                blocks[(si_, j)] = dict(kcs=kcs, sp_idx=sp_idx, q=q, qb=qb, tk=tk, xr=xr, xrb=xrb, on=on, onb=onb,
                                        ks=[ss_["kslot"][kc] for kc in kcs])

            def emit_qk(ui):
                si_, j, pos = units[ui]
                h = ORDER[pos]
                if pos == 0:
                    block_setup(si_, j)
                B_ = blocks[(si_, j)]
                oc, r0 = h // 2, (h % 2) * 64
                reg = ui % NREG
                nt = len(B_["kcs"])
                for i in range(nt):
                    kt, ktb, vt, vtb = B_["ks"][i]
                    fw.op(fw.pe, lambda: nc.tensor.matmul(Sflat[:, RSTR * reg + i * 128:RSTR * reg + (i + 1) * 128], kt[r0:r0 + 64, oc, :], B_["q"][r0:r0 + 64, oc, :],
                                                          start=True, stop=True),
                          reads=[ktb, B_["qb"]], writes=[Sb[reg]], inc=(i == nt - 1))

            def emit_rest(ui):
                si_, j, pos = units[ui]
                h = ORDER[pos]
                B_ = blocks[(si_, j)]
                reg = ui % NREG
                nt = len(B_["kcs"])
                Sf = Sflat[:, RSTR * reg:RSTR * reg + nt * 128]
                if B_["sp_idx"] is None:
                    b_ap, b_b = bint[:, h, 0:nt * 128], bintb
                else:
                    bt, btb = bsp.next()
                    fw.dma(fw.sp, bt[:], self.bias_sp[B_["sp_idx"], h], writes=[btb])
                    b_ap, b_b = bt[:, 0:nt * 128], btb
                ss_, ss_b = ssb_r.next()
                fw.op(fw.dve, lambda: nc.vector.scalar_tensor_tensor(out=ss_[:, 0:nt * 128], in0=Sf, scalar=0.125, in1=b_ap, op0=ALU.mult, op1=ALU.add),
                      reads=[Sb[reg], b_b], writes=[ss_b])
                p_, p_b = pT.next()
                fw.op(fw.act, lambda: nc.scalar.activation(out=p_[:, 0:nt * 128], in_=ss_[:, 0:nt * 128], func=AF.Exp), reads=[ss_b], writes=[p_b])
                hg0, hg1, ob_i = [g for g in GROUPS if g[0] <= pos < g[1]][0]
                Ot, Otb = O[ob_i], Ob[ob_i]
                sl = pos - hg0
                for i in range(nt):
                    kt, ktb, vt, vtb = B_["ks"][i]
                    fw.op(fw.pe, lambda: nc.tensor.matmul(Ot[:, sl * 65:(sl + 1) * 65], p_[:, i * 128:(i + 1) * 128], vt[:, h * 65:(h + 1) * 65],
                                                          start=(i == 0), stop=(i == nt - 1)),
                          reads=[p_b, vtb], writes=[Otb], inc=(i == nt - 1))
                if pos == hg1 - 1:
                    nh_ = hg1 - hg0
                    rec, recb = rec_r.next()
                    on, onb = B_["on"], B_["onb"]
                    Ov = Ot[:, 0:nh_ * 65].rearrange("p (h d) -> p h d", d=65)
                    fw.op(fw.dve, lambda: nc.vector.reciprocal(out=rec[:, 0:nh_], in_=Ov[:, :, 64]), reads=[Otb], writes=[recb])
                    fw.op(fw.dve, lambda: nc.vector.tensor_tensor(out=on[:, hg0:hg1, :], in0=Ov[:, :, 0:64],
                                                                  in1=rec[:, 0:nh_].unsqueeze(2).to_broadcast([128, nh_, 64]), op=ALU.mult),
                          reads=[Otb, recb], writes=[onb])
                if pos == NA_H - 1:
                    def tail(B_=B_):
                        on, onb = B_["on"], B_["onb"]
                        onf = on[:, :, :].rearrange("p h d -> p (h d)")
                        for k in range(8):
                            fw.op(fw.pe, lambda: nc.tensor.transpose(out=Mbf[:, k * 128:(k + 1) * 128], in_=onf[:, k * 128:(k + 1) * 128], identity=c["idb"][:]),
                                  reads=[onb, c["B"]], writes=[Mb], inc=(k == 7))
                        oT, oTb = oT_r.next()
                        fw.op(fw.dve, lambda: nc.vector.tensor_copy(out=oT[:, :, :].rearrange("p a b -> p (a b)"), in_=Mbf[:, :]), reads=[Mb], writes=[oTb])
                        for half in range(2):
                            for k in range(8):
                                fw.op(fw.pe, lambda: nc.tensor.matmul(M[:, half, :], oT[:, k, :], wo[:, k, half * 512:(half + 1) * 512], start=(k == 0), stop=(k == 7)),
                                      reads=[oTb, wob], writes=[Mb], inc=(k == 7 and half == 1))
                        tk = B_["tk"]
                        self.backend(be, M, Mb, 128, B_["xr"], B_["xrb"], dst[tk:tk + 128, :])
                    pend.append(tail)
                if pos == 3:
                    while pend:
                        pend.pop(0)()

            LA = 1
            for ui in range(len(units) + LA):
                if ui < len(units):
                    emit_qk(ui)
                if ui - LA >= 0:
                    emit_rest(ui - LA)
            while pend:
                pend.pop(0)()
            fw.barrier()

    def phase_da2(self):
        nc, fw = self.nc, self.fw
        with ExitStack() as st:
            c = self.consts(st)
            lam = self.sb(st, "da_lam", [128, 4, 64], F32)
            lamb = Buf()
            for i in range(4):
                fw.dma(fw.sp, lam[:, i, :], self.lam_in[i, :].partition_broadcast(128), writes=[lamb])
            lj = self.sb(st, "da_lj", [128, 64], F32)
            lv = self.sb(st, "da_lv", [128, 8], F32)
            lvb = Buf()
            for i in range(2):
                fw.op(fw.dve, lambda i=i: nc.vector.scalar_tensor_tensor(out=lj[:], in0=lam[:, 2 * i, :], scalar=1.0, in1=lam[:, 2 * i + 1, :], op0=ALU.mult, op1=ALU.mult,
                                                                       accum_out=lv[:, i:i + 1]), reads=[lamb], writes=[lvb])
            fw.op(fw.act, lambda: nc.scalar.activation(out=lv[:, 2:4], in_=lv[:, 0:2], func=AF.Exp), reads=[lvb], writes=[lvb])
            fw.op(fw.dve, lambda: nc.vector.scalar_tensor_tensor(out=lv[:, 4:5], in0=lv[:, 3:4], scalar=-LAMBDA_INIT, in1=lv[:, 2:3], op0=ALU.add, op1=ALU.subtract),
                  reads=[lvb], writes=[lvb])
            gsub = self.sb(st, "da_gsub", [128, 128], F32)
            gsubb = Buf()
            fw.dma(fw.sp, gsub[:], self.subln_g.partition_broadcast(128), writes=[gsubb])
            fw.op(fw.dve, lambda: nc.vector.tensor_scalar(out=gsub[:], in0=gsub[:], scalar1=(1.0 - LAMBDA_INIT), scalar2=None, op0=ALU.mult), reads=[gsubb], writes=[gsubb])

            TM = self.TMAX
            NCM = TM // 128
            KT = self.ring(st, "da_KT", 2, [128, TM], BF16)
            QT = self.ring(st, "da_QT", 2, [128, TM], BF16)
            VH = self.ring(st, "da_VH", 2, [128, NCM, 129], BF16)
            E = self.ring(st, "da_E", 3, [128, 2, 512], BF16)
            accs = self.ring(st, "da_accs", 2, [128, 2, 2, 129], F32)
            rec_r = self.ring(st, "da_rec", 2, [128, 2, 2], F32)
            o1_r = self.ring(st, "da_o1", 2, [128, 128], F32)
            o_r = self.ring(st, "da_o", 2, [128, 128], F32)
            oj = self.sb(st, "da_oj", [128, 128], F32)
            ojb = Buf()
            ss_r = self.ring(st, "da_ss", 4, [128, 4], F32)
            on_r = self.ring(st, "da_on", 4, [128, 128], BF16)
            oT_r = self.ring(st, "da_oT", 2, [128, 512], BF16)
            S = [self.ps(st, f"da_S{i}", [128, 2, 512]) for i in range(2)]
            Sb = [Buf(), Buf()]
            A = [self.ps(st, f"da_A{i}", [128, 512]) for i in range(2)]
            Ab = [Buf(), Buf()]
            TPp = self.ps(st, "da_TP", [128, 512])
            TPb = Buf()
            TPbf = TPp[:, :].bitcast(BF16)
            pend = []

            def flush_pend():
                while pend:
                    pend.pop(0)()

            for (toff, T) in self.seqs:
                NC_ = T // 128
                NPAIR = NC_ // 2
                blkoff = toff // 128
                for h in range(DA_H):
                    kt, ktb = KT.next()
                    qt_, qtb = QT.next()
                    vh, vhb = VH.next()
                    m, hh = h // 2, h % 2
                    for i in range(2):
                        sl = 2 * hh + i
                        for ab in range(2):
                            prt = i * 64 + ab * 32
                            fw.dma(fw.sp, kt[prt:prt + 32, 0:T], self.DKT[2 * m + ab, sl * 32:sl * 32 + 32, toff:toff + T], writes=[ktb])
                            fw.dma(fw.sp, qt_[prt:prt + 32, 0:T], self.DQT[2 * m + ab, sl * 32:sl * 32 + 32, toff:toff + T], writes=[qtb])
                    fw.dma(fw.sp, vh[:, 0:NC_, :], self.DV[h, :, blkoff:blkoff + NC_, :], writes=[vhb])
                    oT = oTb = None
                    for qi in range(T // 256):
                        q0 = qi * 256

                        def qk_pair(pp):
                            S_, S_b = S[pp % 2], Sb[pp % 2]
                            for hf in range(2):
                                kb = 2 * pp + hf
                                for i in range(2):
                                    fw.op(fw.pe, lambda: nc.tensor.matmul(S_[:, i, hf * 256:hf * 256 + 256], kt[64 * i:64 * i + 64, kb * 128:(kb + 1) * 128],
                                                                          qt_[64 * i:64 * i + 64, q0:q0 + 256], start=True, stop=True),
                                          reads=[ktb, qtb], writes=[S_b], inc=(i == 1 and hf == 1))

                        qk_pair(0)
                        for pp in range(NPAIR):
                            if pp + 1 < NPAIR:
                                qk_pair(pp + 1)
                            S_, S_b = S[pp % 2], Sb[pp % 2]
                            e, eb = E.next()
                            fw.op(fw.act, lambda: nc.scalar.activation(out=e[:, :, :], in_=S_[:, :, :], func=AF.Exp, scale=0.125), reads=[S_b], writes=[eb])
                            for hf in range(2):
                                kb = 2 * pp + hf
                                for i in range(2):
                                    for s in range(2):
                                        first = (kb == 0 and s == 0)
                                        fw.op(fw.pe, lambda: nc.tensor.matmul(A[i][:, 256 * s:256 * s + 129], e[:, i, hf * 256 + 128 * s:hf * 256 + 128 * s + 128], vh[:, kb, :],
                                                                              start=first, stop=(kb == NC_ - 1), skip_group_check=True),
                                              reads=[eb, vhb], writes=[Ab[i]], inc=(i == 1 and s == 1 and hf == 1))
                            if pp == 2:
                                flush_pend()
                        flush_pend()
                        ac, acb = accs.next()
                        for i in range(2):
                            fw.op(fw.dve, lambda: nc.vector.tensor_copy(out=ac[:, i, :, :], in_=A[i][:, :].rearrange("p (s x) -> p s x", s=2)[:, :, 0:129]),
                                  reads=[Ab[i]], writes=[acb])
                        rec, recb = rec_r.next()
                        fw.op(fw.dve, lambda: nc.vector.reciprocal(out=rec[:, :, :], in_=ac[:, :, :, 128]), reads=[acb], writes=[recb])
                        fw.op(fw.dve, lambda: nc.vector.tensor_scalar(out=rec[:, 1, :], in0=rec[:, 1, :], scalar1=lv[:, 4:5], scalar2=None, op0=ALU.mult),
                              reads=[recb, lvb], writes=[recb])
                        if qi % 2 == 0:
                            oT, oTb = oT_r.next()
                        ons = []
                        for s in range(2):
                            o1, o1b = o1_r.next()
                            o_, o_b = o_r.next()
                            ss, ssb = ss_r.next()
                            on, onb = on_r.next()
                            fw.op(fw.dve, lambda: nc.vector.tensor_scalar(out=o1[:], in0=ac[:, 0, s, 0:128], scalar1=rec[:, 0, s:s + 1], scalar2=None, op0=ALU.mult),
                                  reads=[acb, recb], writes=[o1b])
                            fw.op(fw.dve, lambda: nc.vector.scalar_tensor_tensor(out=o_[:], in0=ac[:, 1, s, 0:128], scalar=rec[:, 1, s:s + 1], in1=o1[:], op0=ALU.mult, op1=ALU.add),
                                  reads=[acb, recb, o1b], writes=[o_b])
                            fw.op(fw.dve, lambda: nc.vector.scalar_tensor_tensor(out=oj[:], in0=o_[:], scalar=1.0, in1=o_[:], op0=ALU.mult, op1=ALU.mult, accum_out=ss[:, 0:1]),
                                  reads=[o_b], writes=[ojb, ssb])
                            self.rstd(ss, ssb, 128, 1.0 / 128, SUBLN_EPS, c)
                            fw.op(fw.dve, lambda: nc.vector.scalar_tensor_tensor(out=on[:], in0=o_[:], scalar=ss[:, 2:3], in1=gsub[:], op0=ALU.mult, op1=ALU.mult),
                                  reads=[o_b, ssb, gsubb], writes=[onb])
                            ons.append((on, onb))

                        def tail(ons=ons, oT=oT, oTb=oTb, qi=qi, h=h, toff=toff):
                            for s, (on, onb) in enumerate(ons):
                                fw.op(fw.pe, lambda: nc.tensor.transpose(out=TPbf[:, 128 * s:128 * s + 128], in_=on[:], identity=c["idb"][:]),
                                      reads=[onb, c["B"]], writes=[TPb], inc=(s == 1))
                            cc = (qi % 2) * 256
                            fw.op(fw.dve, lambda: nc.vector.tensor_copy(out=oT[:, cc:cc + 256], in_=TPbf[:, 0:256]), reads=[TPb], writes=[oTb])
                            if qi % 2 == 1:
                                tq = toff + (qi - 1) * 256
                                fw.dma(fw.sp, self.DOT[h, :, tq:tq + 512], oT[:], reads=[oTb])
                        pend.append(tail)
            flush_pend()
            fw.barrier()

    def phase_da3(self, src, dst):
        nc, fw = self.nc, self.fw
        with ExitStack() as st:
            c = self.consts(st)
            gbc, gbcb = self.load_gbc(st, "d3_gbc", 3)
            wo = self.sb(st, "d3_wo", [128, 8, D], BF16)
            wob = Buf()
            fw.dma(fw.pool, wo[:], self.da_wo.rearrange("(kc p) n -> p kc n", p=128), writes=[wob])
            be = self.make_backend(st, c, gbc, gbcb)
            oT = self.ring(st, "d3_oT", 2, [128, 8, 512], BF16)
            M = [self.ps(st, f"d3_M{i}", [128, 2, 512]) for i in range(2)]
            Mb = [Buf(), Buf()]
            mi = 0
            for (toff, T) in self.seqs:
                for sbi in range(T // 512):
                    t0 = toff + 512 * sbi
                    o, ob = oT.next()
                    fw.dma(fw.sp, o[:], self.DOT[:, :, t0:t0 + 512].rearrange("h p t -> p h t"), writes=[ob])
                    for b in range(4):
                        tk = t0 + 128 * b
                        xr, xrb = self.backend_prefetch(be, src[tk:tk + 128, :], 128)
                        M_, M_b = M[mi % 2], Mb[mi % 2]
                        mi += 1
                        for half in range(2):
                            for k in range(8):
                                fw.op(fw.pe, lambda k=k, half=half, M_=M_, b=b, o=o: nc.tensor.matmul(M_[:, half, :], o[:, k, 128 * b:128 * b + 128], wo[:, k, half * 512:(half + 1) * 512],
                                                                                                      start=(k == 0), stop=(k == 7)),
                                      reads=[ob, wob], writes=[M_b], inc=(k == 7 and half == 1))
                        self.backend(be, M_, M_b, 128, xr, xrb, dst[tk:tk + 128, :])
            fw.barrier()

    def build(self, phases="all"):
        fw = self.fw
        P = phases
        if P == "all" or "p0" in P:
            self.phase_p0()
        if P == "all" or "na1" in P:
            self.phase_proj("na", self.x)
        if P == "all" or "na2" in P:
            self.phase_na2(self.XA)
        if P == "all" or "ffn0" in P:
            self.phase_ffn(0, self.XA, self.XB)
        if P == "all" or "da1" in P:
            self.phase_proj("da", self.XB)
        if P == "all" or "da2" in P:
            self.phase_da2()
        if P == "all" or "da3" in P:
            self.phase_da3(self.XB, self.XC)
        if P == "all" or "ffn1" in P:
            self.phase_ffn(1, self.XC, self.y)
        fw.finish()
        return self.nc


def _na_bias_tables(rpb):
    R = 16
    NB = R // 2
    W = 64

    def table(j):
        kcs = Builder.na_kc(j, NB)
        out = np.full((NA_H, 128, len(kcs), 128), NEG, np.float32)
        rq = 2 * j + np.arange(128) // 64
        cq = np.arange(128) % 64
        r0 = np.clip(rq - 4, 0, R - 8)
        c0 = np.clip(cq - 8, 0, W - 16)
        for i, kc in enumerate(kcs):
            rk = 2 * kc + np.arange(128) // 64
            ck = np.arange(128) % 64
            valid = ((rk[:, None] >= r0[None, :]) & (rk[:, None] < r0[None, :] + 8) &
                     (ck[:, None] >= c0[None, :]) & (ck[:, None] < c0[None, :] + 16))
            dr = np.clip(rk[:, None] - rq[None, :] + 7, 0, 14)
            dc = np.clip(ck[:, None] - cq[None, :] + 15, 0, 30)
            vals = rpb[:, dr, dc]
            out[:, :, i, :] = np.where(valid[None], vals, np.float32(NEG))
        return out.reshape(NA_H, 128, len(kcs) * 128)

    interior = table(3)
    assert interior.shape[2] == 640
    sp = np.stack([table(0), table(1), table(NB - 2), table(NB - 1)], 0)
    assert sp.shape[3] == 512
    return np.ascontiguousarray(interior), np.ascontiguousarray(sp)


def _rot_tables(T):
    inv = (1.0 / (ROPE_THETA ** (np.arange(0, 64, 2, dtype=np.float32) / np.float32(64)))).astype(np.float32)
    ang = np.arange(T, dtype=np.float32)[None, :] * inv[np.arange(128) % 32][:, None]
    return np.cos(ang).astype(np.float32), np.sin(ang).astype(np.float32)


def _perm_cols():
    cols = []
    for m in range(4):
        for ab in range(2):
            for sl in range(4):
                for dl in range(32):
                    cols.append((4 * m + sl) * 64 + ab * 32 + dl)
    return np.array(cols)


def prepare_shared(inp, TMAX):
    f = lambda a: np.ascontiguousarray(np.asarray(a, dtype=np.float32))
    perm = _perm_cols()
    bi, bsp = _na_bias_tables(f(inp["na_rpb"])[0])
    cos, sin = _rot_tables(TMAX)
    sh = {
        "gvec": f(np.concatenate([inp["attn_pre_g"], inp["attn_post_g"], inp["ffn_pre_g"], inp["ffn_post_g"]], 0)),
        "w_qkv": f(inp["na_w_qkv"][0]), "na_wo": f(inp["na_w_o"][0]),
        "da_wq": f(np.asarray(inp["da_w_q"][0])[:, perm]), "da_wk": f(np.asarray(inp["da_w_k"][0])[:, perm]),
        "da_wv": f(inp["da_w_v"][0]), "da_wo": f(inp["da_w_o"][0]),
        "w_in": f(inp["ffn_w_in"]), "w_out": f(inp["ffn_w_out"]),
        "conv_w": f(inp["ffn_conv_w"]), "conv_b": f(inp["ffn_conv_b"]),
        "bias_int": bi, "bias_sp": bsp, "rot_cos": cos, "rot_sin": sin,
        "lam_in": f(np.stack([inp["da_lambda_q1"][0], inp["da_lambda_k1"][0], inp["da_lambda_q2"][0], inp["da_lambda_k2"][0]], 0)),
        "subln_g": f(inp["da_subln_g"][0]), "ident": np.eye(128, dtype=np.float32),
    }
    return sh


def kernel(**inp):
    xs = np.asarray(inp["x_sample"], dtype=np.float32)
    xp = np.asarray(inp["x_prompt"], dtype=np.float32)
    NS, TS, _ = xs.shape
    NP_, TP, _ = xp.shape
    ncore = 8
    assert NS == ncore and NP_ * 2 == ncore
    seqs = [(0, TS), (TS, TP)]
    bld = Builder(seqs)
    nc = bld.build()
    sh = prepare_shared(inp, max(TS, TP))
    in_maps = []
    for cidx in range(ncore):
        m = dict(sh)
        m["x"] = np.ascontiguousarray(np.concatenate([xs[cidx], xp[cidx // 2]], 0))
        in_maps.append(m)
    res = run_bass_kernel_spmd(nc, in_maps, core_ids=list(range(ncore)))
    y_s = np.stack([res.results[cidx]["y"][:TS] for cidx in range(ncore)], 0)
    y_p = np.stack([res.results[2 * p]["y"][TS:] for p in range(NP_)], 0)
    return (y_p.astype(np.float32), y_s.astype(np.float32))
```

```python
import math
from contextlib import ExitStack
import numpy as np
import ml_dtypes
import concourse.bass as bass
import concourse.mybir as mybir
from concourse.bass_utils import run_bass_kernel_spmd

F32 = mybir.dt.float32
BF16 = mybir.dt.bfloat16
AF = mybir.ActivationFunctionType
ALU = mybir.AluOpType

D = 1024
DFF = 2816
NFC = 22
NA_H = 16
DA_H = 8
NEG = -30000.0
NORM_EPS = 1e-6
SUBLN_EPS = 1e-5
LAMBDA_INIT = 0.8 - 0.6 * math.exp(-0.3 * 1)
ROPE_THETA = 10000.0
import os as _os
SEQ_FFN = bool(int(_os.environ.get('SEQ_FFN', '0')))


class Buf:
    __slots__ = ("name", "w", "r")

    def __init__(self, name=""):
        self.name = name
        self.w = None
        self.r = {}


class Eng:
    def __init__(self, nc, h, name):
        self.h = h
        self.name = name
        self.sem = nc.alloc_semaphore(name="s_" + name)
        self.count = 0
        self.known = {}


class FW:
    NDMASEM = 48

    def __init__(self, nc):
        self.nc = nc
        self.pe = Eng(nc, nc.tensor, "pe")
        self.act = Eng(nc, nc.scalar, "act")
        self.dve = Eng(nc, nc.vector, "dve")
        self.pool = Eng(nc, nc.gpsimd, "pool")
        self.sp = Eng(nc, nc.sync, "sp")
        self.engs = [self.pe, self.act, self.dve, self.pool, self.sp]
        self.dsems = [nc.alloc_semaphore(name=f"d{i}") for i in range(self.NDMASEM)]
        self.dval = [0] * self.NDMASEM
        self.dnext = 0
        self.nops = 0
        self.ndma = 0
        self.rec = None

    def wait(self, eng, tok):
        if tok is None:
            return
        sem, val = tok
        if sem is eng.sem:
            if eng is self.pe:
                return
            if val > eng.count:
                return
        k = id(sem)
        if eng.known.get(k, 0) >= val:
            return
        eng.h.wait_ge(sem, val)
        eng.known[k] = val
        if self.rec is not None:
            self.rec[eng.name].append(("w", k, val))

    def _deps(self, eng, reads, writes):
        for b in reads:
            self.wait(eng, b.w)
        for b in writes:
            self.wait(eng, b.w)
            for tok in list(b.r.values()):
                self.wait(eng, tok)

    def _commit(self, tok, reads, writes):
        k = id(tok[0])
        for b in reads:
            if k not in b.r or b.r[k][1] < tok[1]:
                b.r[k] = tok
        for b in writes:
            b.w = tok
            b.r = {}

    def op(self, eng, fn, reads=(), writes=(), inc=True):
        self._deps(eng, reads, writes)
        ins = fn()
        self.nops += 1
        if inc:
            eng.count += 1
            ins.then_inc(eng.sem, 1)
            tok = (eng.sem, eng.count)
            if self.rec is not None:
                self.rec[eng.name].append(("i", id(eng.sem), 1))
        else:
            tok = (eng.sem, eng.count + 1)
        self._commit(tok, reads, writes)
        return tok

    def dma(self, eng, out, in_, reads=(), writes=(), **kw):
        self._deps(eng, reads, writes)
        i = self.dnext
        self.dnext = (self.dnext + 1) % self.NDMASEM
        sem = self.dsems[i]
        if self.dval[i] > 0:
            self.wait(eng, (sem, self.dval[i]))
        self.dval[i] += 16
        eng.h.dma_start(out=out, in_=in_, **kw).then_inc(sem, 16)
        self.ndma += 1
        if self.rec is not None:
            self.rec[eng.name].append(("i", id(sem), 16))
        tok = (sem, self.dval[i])
        self._commit(tok, reads, writes)
        return tok

    def barrier(self):
        for e in self.engs:
            for o in self.engs:
                if o is not e and o.count > 0:
                    self.wait(e, (o.sem, o.count))
            for i, sem in enumerate(self.dsems):
                if self.dval[i] > 0:
                    self.wait(e, (sem, self.dval[i]))

    def finish(self):
        for i, sem in enumerate(self.dsems):
            if self.dval[i] > 0:
                self.wait(self.sp, (sem, self.dval[i]))
        for e in (self.pe, self.act, self.dve, self.pool):
            if e.count > 0:
                self.wait(self.sp, (e.sem, e.count))


class Ring:
    def __init__(self, tiles):
        self.tiles = tiles
        self.bufs = [Buf() for _ in tiles]
        self.i = -1

    def next(self):
        self.i = (self.i + 1) % len(self.tiles)
        return self.tiles[self.i], self.bufs[self.i]

    def cur(self):
        return self.tiles[self.i], self.bufs[self.i]


class Builder:
    def __init__(self, seqs, debug=False):
        self.seqs = seqs
        self.NT = sum(T for _, T in seqs)
        self.TMAX = max(T for _, T in seqs)
        self.debug = debug
        self.nc = nc = bass.Bass("TRN2", target_bir_lowering=False)
        self.fw = FW(nc)
        NT = self.NT
        dt = nc.dram_tensor
        I = "ExternalInput"
        self.x = dt("x", [NT, D], F32, kind=I).ap()
        self.gvec = dt("gvec", [8, D], F32, kind=I).ap()
        self.w_qkv = dt("w_qkv", [D, 3 * D], F32, kind=I).ap()
        self.na_wo = dt("na_wo", [D, D], F32, kind=I).ap()
        self.da_wq = dt("da_wq", [D, D], F32, kind=I).ap()
        self.da_wk = dt("da_wk", [D, D], F32, kind=I).ap()
        self.da_wv = dt("da_wv", [D, D], F32, kind=I).ap()
        self.da_wo = dt("da_wo", [D, D], F32, kind=I).ap()
        self.w_in = dt("w_in", [2, D, 2 * DFF], F32, kind=I).ap()
        self.w_out = dt("w_out", [2, DFF, D], F32, kind=I).ap()
        self.conv_w = dt("conv_w", [2, 3, 2 * DFF], F32, kind=I).ap()
        self.conv_b = dt("conv_b", [2, 2 * DFF], F32, kind=I).ap()
        self.bias_int = dt("bias_int", [NA_H, 128, 640], F32, kind=I).ap()
        self.bias_sp = dt("bias_sp", [4, NA_H, 128, 512], F32, kind=I).ap()
        self.rot_cos = dt("rot_cos", [128, self.TMAX], F32, kind=I).ap()
        self.rot_sin = dt("rot_sin", [128, self.TMAX], F32, kind=I).ap()
        self.lam_in = dt("lam_in", [4, 64], F32, kind=I).ap()
        self.subln_g = dt("subln_g", [128], F32, kind=I).ap()
        self.ident = dt("ident", [128, 128], F32, kind=I).ap()
        self.y = dt("y", [NT, D], F32, kind="ExternalOutput").ap()
        S = "ExternalOutput" if debug else "Internal"
        self.XA = dt("XA", [NT, D], F32, kind=S).ap()
        self.XB = dt("XB", [NT, D], F32, kind=S).ap()
        self.WINB = dt("WINB", [2, 11, 128, 8, 512], BF16, kind="Internal").ap()
        NBLK = NT // 128
        self.NQT = dt("NQT", [NBLK, 128, 8, 128], BF16, kind="Internal").ap()
        self.NKT = dt("NKT", [NBLK, 128, 8, 128], BF16, kind="Internal").ap()
        self.NV = dt("NV", [128, NBLK, NA_H * 65], BF16, kind="Internal").ap()
        self.DQT = dt("DQT", [8, 128, NT], BF16, kind="Internal").ap()
        self.DKT = dt("DKT", [8, 128, NT], BF16, kind="Internal").ap()
        self.DV = dt("DV", [DA_H, 128, NBLK, 129], BF16, kind="Internal").ap()
        self.DOT = dt("DOT", [DA_H, 128, NT], BF16, kind="Internal").ap()
        self.XC = self.XA

    _uid = 0

    def _nm(self, name):
        Builder._uid += 1
        return f"{name}_{Builder._uid}"

    def sb(self, st, name, shape, dtype):
        return st.enter_context(self.nc.sbuf_tensor(self._nm(name), shape, dtype))

    def ps(self, st, name, shape, dtype=F32):
        return st.enter_context(self.nc.psum_tensor(self._nm(name), shape, dtype))

    def ring(self, st, name, n, shape, dtype):
        return Ring([self.sb(st, f"{name}{i}", shape, dtype) for i in range(n)])

    def consts(self, st):
        nc, fw = self.nc, self.fw
        c = {}
        c["idf"] = self.sb(st, "idf", [128, 128], F32)
        c["idb"] = self.sb(st, "idb", [128, 128], BF16)
        c["nh"] = self.sb(st, "nhalf", [128, 1], F32)
        c["B"] = Buf("consts")
        fw.dma(fw.sp, c["idf"][:], self.ident[:, :], writes=[c["B"]])
        fw.dma(fw.pool, c["idb"][:], self.ident[:, :], writes=[c["B"]])
        fw.op(fw.pool, lambda: nc.gpsimd.memset(c["nh"][:], -0.5), writes=[c["B"]])
        return c

    def load_gfm(self, st, name, row):
        nc, fw = self.nc, self.fw
        t = self.sb(st, name, [128, 8], F32)
        b = Buf(name)
        with nc.allow_non_contiguous_dma(reason="tiny gain vector"):
            fw.dma(fw.sp, t[:], self.gvec[row, :].rearrange("(c p) -> p c", p=128), writes=[b])
        return t, b

    def load_gbc(self, st, name, row):
        fw = self.fw
        t = self.sb(st, name, [128, D], F32)
        b = Buf(name)
        fw.dma(fw.sp, t[:], self.gvec[row, :].partition_broadcast(128), writes=[b])
        return t, b

    def rstd(self, ss, ssb, n, inv_dim, eps, c):
        nc, fw = self.nc, self.fw
        fw.op(fw.pool, lambda: nc.gpsimd.tensor_scalar(out=ss[:n, 1:2], in0=ss[:n, 0:1], scalar1=inv_dim, scalar2=eps,
                                                        op0=ALU.mult, op1=ALU.add), reads=[ssb], writes=[ssb])
        fw.op(fw.pool, lambda: nc.gpsimd.tensor_tensor(out=ss[:n, 2:3], in0=ss[:n, 1:2], in1=c["nh"][:n, :], op=ALU.pow),
              reads=[ssb, c["B"]], writes=[ssb])

    def make_frontend(self, st, c, gfm, gfmb, tp, tpb):
        fe = {"xt": self.ring(st, "fe_xt", 2, [128, D], F32), "xs": self.ring(st, "fe_xs", 3, [128, D], F32),
              "junk": self.sb(st, "fe_junk", [128, D], BF16), "junkb": Buf(),
              "ss": self.ring(st, "fe_ss", 4, [128, 4], F32), "c": c, "gfm": gfm, "gfmb": gfmb, "tp": tp, "tpb": tpb}
        return fe

    def frontend_a(self, fe, src, n):
        nc, fw = self.nc, self.fw
        c = fe["c"]
        xt, xtb = fe["xt"].next()
        xs, xsb = fe["xs"].next()
        ss, ssb = fe["ss"].next()
        fw.dma(fw.sp, xt[:n, :], src, writes=[xtb])
        fw.op(fw.act, lambda: nc.scalar.activation(out=fe["junk"][:n, :], in_=xt[:n, :], func=AF.Square, accum_out=ss[:n, 0:1]),
              reads=[xtb], writes=[fe["junkb"], ssb])
        self.rstd(ss, ssb, n, 1.0 / D, NORM_EPS, c)
        fw.op(fw.act, lambda: nc.scalar.activation(out=xs[:n, :], in_=xt[:n, :], func=AF.Copy, scale=ss[:n, 2:3]),
              reads=[xtb, ssb], writes=[xsb])
        return xs, xsb

    def frontend_b(self, fe, xs, xsb, n, hT, hTb, col0):
        nc, fw = self.nc, self.fw
        c = fe["c"]
        tp, tpb = fe["tp"], fe["tpb"]
        for k in range(8):
            fw.op(fw.pe, lambda: nc.tensor.transpose(out=tp[:, k, :n], in_=xs[:n, k * 128:(k + 1) * 128], identity=c["idf"][:n, :n]),
                  reads=[xsb, c["B"]], writes=[tpb], inc=(k == 7))
        fw.op(fw.dve, lambda: nc.vector.tensor_tensor(out=hT[:, :, col0:col0 + n], in0=tp[:, :, :n],
                                                      in1=fe["gfm"][:, :].unsqueeze(2).to_broadcast([128, 8, n]), op=ALU.mult),
              reads=[tpb, fe["gfmb"]], writes=[hTb])

    def frontend(self, fe, src, n, hT, hTb, col0):
        xs, xsb = self.frontend_a(fe, src, n)
        self.frontend_b(fe, xs, xsb, n, hT, hTb, col0)

    def make_backend(self, st, c, gbc, gbcb):
        be = {"xr": self.ring(st, "be_xr", 2, [128, D], F32), "t": self.ring(st, "be_t", 2, [128, D], F32),
              "o": self.ring(st, "be_o", 2, [128, D], F32), "junk": self.sb(st, "be_junk", [128, D], BF16), "junkb": Buf(),
              "ss": self.ring(st, "be_ss", 4, [128, 4], F32), "c": c, "gbc": gbc, "gbcb": gbcb}
        return be

    def backend_prefetch(self, be, res_src, n):
        xr, xrb = be["xr"].next()
        self.fw.dma(self.fw.sp, xr[:n, :], res_src, writes=[xrb])
        return xr, xrb

    def backend(self, be, m, mb, n, xr, xrb, dst):
        nc, fw = self.nc, self.fw
        ss, ssb = be["ss"].next()
        t, tb = be["t"].next()
        o, ob = be["o"].next()
        m2 = m[:n, :, :].rearrange("p a b -> p (a b)")
        fw.op(fw.act, lambda: nc.scalar.activation(out=be["junk"][:n, :], in_=m2, func=AF.Square, accum_out=ss[:n, 0:1]),
              reads=[mb], writes=[be["junkb"], ssb])
        self.rstd(ss, ssb, n, 1.0 / D, NORM_EPS, be["c"])
        fw.op(fw.dve, lambda: nc.vector.scalar_tensor_tensor(out=t[:n, :], in0=m2, scalar=ss[:n, 2:3], in1=be["gbc"][:n, :],
                                                             op0=ALU.mult, op1=ALU.mult), reads=[mb, ssb, be["gbcb"]], writes=[tb])
        fw.op(fw.pool, lambda: nc.gpsimd.tensor_tensor(out=o[:n, :], in0=t[:n, :], in1=xr[:n, :], op=ALU.add),
              reads=[tb, xrb], writes=[ob])
        fw.dma(fw.sp, dst, o[:n, :], reads=[ob])

    def phase_p0(self):
        nc, fw = self.nc, self.fw
        with ExitStack() as st:
            stg = self.ring(st, "p0_stg", 3, [128, 8, 512], BF16)
            for l in range(2):
                for g in range(11):
                    t, tb = stg.next()
                    src = self.w_in[l]
                    fw.dma(fw.pool, t[:, :, 0:256], src[:, 256 * g:256 * g + 256].rearrange("(kc p) n -> p kc n", p=128), writes=[tb])
                    fw.dma(fw.pool, t[:, :, 256:512], src[:, DFF + 256 * g:DFF + 256 * g + 256].rearrange("(kc p) n -> p kc n", p=128), writes=[tb])
                    fw.dma(fw.sp, self.WINB[l, g], t[:], reads=[tb])
            fw.barrier()

    def phase_ffn(self, l, src, dst):
        nc, fw = self.nc, self.fw
        with ExitStack() as st:
            c = self.consts(st)
            gfm, gfmb = self.load_gfm(st, "ffn_gfm", 4 + l)
            gbc, gbcb = self.load_gbc(st, "ffn_gbc", 6 + l)
            wout = self.sb(st, "ffn_wout", [128, NFC, D], BF16)
            woutb = Buf()
            for f0 in range(0, NFC, 6):
                f1 = min(NFC, f0 + 6)
                fw.dma(fw.pool, wout[:, f0:f1, :], self.w_out[l, f0 * 128:f1 * 128, :].rearrange("(f p) n -> p f n", p=128), writes=[woutb])
            cw = self.sb(st, "ffn_cw", [128, 3, 2 * NFC], F32)
            cb = self.sb(st, "ffn_cb", [128, 2 * NFC], F32)
            cwb = Buf()
            with nc.allow_non_contiguous_dma(reason="small conv params"):
                for k in range(3):
                    fw.dma(fw.sp, cw[:, k, :], self.conv_w[l, k, :].rearrange("(c p) -> p c", p=128), writes=[cwb])
                fw.dma(fw.sp, cb[:, :], self.conv_b[l, :].rearrange("(c p) -> p c", p=128), writes=[cwb])
            tp = self.ps(st, "ffn_tp", [128, 8, 128])
            tpb = Buf()
            pgu = [self.ps(st, f"ffn_pgu{i}", [128, 2, 512]) for i in range(2)]
            pgub = [[Buf(), Buf()] for _ in range(2)]
            pm = self.ps(st, "ffn_pm", [128, 2, 512])
            pmb = Buf()
            fe = self.make_frontend(st, c, gfm, gfmb, tp, tpb)
            be = self.make_backend(st, c, gbc, gbcb)
            hT = self.ring(st, "ffn_hT", 2, [128, 8, 512], BF16)
            wg = self.ring(st, "ffn_wg", 3, [128, 8, 512], BF16)
            aT = self.ring(st, "ffn_aT", 2, [128, NFC, 513], BF16)
            carry = self.sb(st, "ffn_carry", [128, 2 * NFC, 2], F32)
            carryb = [Buf() for _ in range(2 * NFC)]
            T1 = self.ring(st, "ffn_T1", 2, [128, 513], F32)
            T2 = self.ring(st, "ffn_T2", 2, [128, 513], F32)
            TG = self.ring(st, "ffn_TG", 2, [128, 513], F32)
            TU = self.ring(st, "ffn_TU", 2, [128, 513], F32)
            GG = self.ring(st, "ffn_GG", 2, [128, 513], F32)

            tiles = []
            for (toff, T) in self.seqs:
                nt_ = T // 512
                for i in range(nt_):
                    tiles.append(dict(toff=toff, i=i, first=(i == 0), last=(i == nt_ - 1)))
            NTL = len(tiles)
            for n, tl in enumerate(tiles):
                tl["h"], tl["hb"] = None, None
                W = 513 if tl["last"] else 512
                j = 1 if tl["first"] else 0
                blocks = []
                while j < 512:
                    nn = min(128, 512 - j)
                    blocks.append((j, nn))
                    j += nn
                if W == 513:
                    blocks.append((512, 1))
                tl["W"], tl["blocks"] = W, blocks
            xs_pend = {}

            def A1(n, b):
                if n >= NTL:
                    return
                tl = tiles[n]
                if b == 0:
                    tl["h"], tl["hb"] = hT.next()
                t0 = tl["toff"] + 512 * tl["i"] + 128 * b
                xs_pend[(n, b)] = self.frontend_a(fe, src[t0:t0 + 128, :], 128)

            def A2(n, b):
                if n >= NTL:
                    return
                tl = tiles[n]
                xs, xsb = xs_pend.pop((n, b))
                self.frontend_b(fe, xs, xsb, 128, tl["h"], tl["hb"], 128 * b)

            def C(n, bi):
                if n < 0 or bi >= len(tiles[n]["blocks"]):
                    return
                tl = tiles[n]
                a_, a_b = tl["aT"], tl["aTb"]
                j, nn = tl["blocks"][bi]
                tk = tl["toff"] + 512 * tl["i"] - 1 + j
                xr, xrb = self.backend_prefetch(be, src[tk:tk + nn, :], nn)
                for half in range(2):
                    for f in range(NFC):
                        fw.op(fw.pe, lambda: nc.tensor.matmul(pm[:nn, half, :], a_[:, f, j:j + nn], wout[:, f, half * 512:(half + 1) * 512],
                                                              start=(f == 0), stop=(f == NFC - 1)),
                              reads=[a_b, woutb], writes=[pmb], inc=(f == NFC - 1 and half == 1))
                self.backend(be, pm, pmb, nn, xr, xrb, dst[tk:tk + nn, :])

            def Bgroup(n, g, wtile):
                tl = tiles[n]
                h, hb = tl["h"], tl["hb"]
                a_, a_b = tl["aT"], tl["aTb"]
                W = tl["W"]
                w_, wb_ = wtile
                for jf in range(2):
                    f = 2 * g + jf
                    P_ = pgu[f % 2]
                    Pb = pgub[f % 2]
                    t3s = []
                    for gu in range(2):
                        cidx = f + NFC * gu
                        col = 256 * gu + 128 * jf
                        for kc in range(8):
                            fw.op(fw.pe, lambda: nc.tensor.matmul(P_[:, gu, :], w_[:, kc, col:col + 128], h[:, kc, :], start=(kc == 0), stop=(kc == 7)),
                                  reads=[wb_, hb], writes=[Pb[gu]], inc=(kc == 7))
                        Pg = P_[:, gu, :]
                        t1, t1b = T1.next()
                        t2, t2b = T2.next()
                        t3, t3b = (TG if gu == 0 else TU).next()
                        sc1, sc0, sc2, bi_ = cw[:, 1, cidx:cidx + 1], cw[:, 0, cidx:cidx + 1], cw[:, 2, cidx:cidx + 1], cb[:, cidx:cidx + 1]
                        fw.op(fw.act, lambda: nc.scalar.activation(out=t1[:, 1:W], in_=Pg[:, 0:W - 1], func=AF.Identity, scale=sc1, bias=bi_),
                              reads=[Pb[gu], cwb], writes=[t1b])
                        fw.op(fw.act, lambda: nc.scalar.activation(out=t1[:, 0:1], in_=carry[:, cidx, 1:2], func=AF.Identity, scale=sc1, bias=bi_),
                              reads=[carryb[cidx], cwb], writes=[t1b])
                        fw.op(fw.dve, lambda: nc.vector.scalar_tensor_tensor(out=t2[:, 2:W], in0=Pg[:, 0:W - 2], scalar=sc0, in1=t1[:, 2:W], op0=ALU.mult, op1=ALU.add),
                              reads=[Pb[gu], t1b, cwb], writes=[t2b])
                        fw.op(fw.dve, lambda: nc.vector.scalar_tensor_tensor(out=t2[:, 0:2], in0=carry[:, cidx, :], scalar=sc0, in1=t1[:, 0:2], op0=ALU.mult, op1=ALU.add),
                              reads=[carryb[cidx], t1b, cwb], writes=[t2b])
                        fw.op(fw.dve, lambda: nc.vector.scalar_tensor_tensor(out=t3[:, 0:512], in0=Pg[:, 0:512], scalar=sc2, in1=t2[:, 0:512], op0=ALU.mult, op1=ALU.add),
                              reads=[Pb[gu], t2b, cwb], writes=[t3b])
                        if W == 513:
                            fw.op(fw.dve, lambda: nc.vector.tensor_copy(out=t3[:, 512:513], in_=t2[:, 512:513]), reads=[t2b], writes=[t3b])
                        fw.op(fw.dve, lambda: nc.vector.tensor_copy(out=carry[:, cidx, :], in_=Pg[:, 510:512]), reads=[Pb[gu]], writes=[carryb[cidx]])
                        t3s.append((t3, t3b))
                    (tg, tgb), (tu, tub) = t3s
                    gg, ggb = GG.next()
                    fw.op(fw.act, lambda: nc.scalar.activation(out=gg[:, 0:W], in_=tg[:, 0:W], func=AF.Gelu_apprx_tanh), reads=[tgb], writes=[ggb])
                    fw.op(fw.pool, lambda: nc.gpsimd.tensor_tensor(out=a_[:, f, 0:W], in0=gg[:, 0:W], in1=tu[:, 0:W], op=ALU.mult),
                          reads=[ggb, tub], writes=[a_b])

            for b in range(4):
                A1(0, b)
                A2(0, b)
            for n, tl in enumerate(tiles):
                tl["aT"], tl["aTb"] = aT.next()
                if tl["first"]:
                    fw.op(fw.pool, lambda: nc.gpsimd.memset(carry[:, :, :], 0.0), writes=carryb)
                wq = []
                for g in range(2):
                    w_, wb_ = wg.next()
                    fw.dma(fw.sp, w_[:], self.WINB[l, g], writes=[wb_])
                    wq.append((w_, wb_))
                for g in range(11):
                    if g + 2 < 11:
                        w_, wb_ = wg.next()
                        fw.dma(fw.sp, w_[:], self.WINB[l, g + 2], writes=[wb_])
                        wq.append((w_, wb_))
                    Bgroup(n, g, wq[g])
                    if SEQ_FFN:
                        if g == 10:
                            for b in range(4):
                                A1(n + 1, b)
                                A2(n + 1, b)
                            for bi in range(5):
                                C(n, bi)
                        continue
                    if g in (0, 2, 4, 6):
                        A1(n + 1, g // 2)
                    if g in (1, 3, 5, 7):
                        C(n - 1, (g - 1) // 2)
                        A2(n + 1, (g - 1) // 2)
                    if g == 8:
                        C(n - 1, 4)
            if not SEQ_FFN:
                for bi in range(5):
                    C(NTL - 1, bi)
            fw.barrier()

    def phase_proj(self, mode, src):
        nc, fw = self.nc, self.fw
        na = (mode == "na")
        with ExitStack() as st:
            c = self.consts(st)
            gfm, gfmb = self.load_gfm(st, "pj_gfm", 0 if na else 1)
            w = self.sb(st, "pj_w", [128, 8, 3 * D], BF16)
            wb = Buf()
            if na:
                for k3 in range(3):
                    fw.dma(fw.pool, w[:, :, k3 * D:(k3 + 1) * D], self.w_qkv[:, k3 * D:(k3 + 1) * D].rearrange("(kc p) n -> p kc n", p=128), writes=[wb])
            else:
                for k3, ww in enumerate((self.da_wq, self.da_wk, self.da_wv)):
                    fw.dma(fw.pool, w[:, :, k3 * D:(k3 + 1) * D], ww.rearrange("(kc p) n -> p kc n", p=128), writes=[wb])
            tp = self.ps(st, "pj_tp", [128, 8, 128])
            tpb = Buf()
            pp = [self.ps(st, f"pj_pp{i}", [128, 2, 512]) for i in range(3)]
            ppb = [[Buf(), Buf()] for _ in range(3)]
            fe = self.make_frontend(st, c, gfm, gfmb, tp, tpb)
            hT = self.ring(st, "pj_hT", 2, [128, 8, 512], BF16)
            if na:
                stq = self.ring(st, "pj_stq", 2, [128, 4, 8, 128], BF16)
                stk = self.ring(st, "pj_stk", 2, [128, 4, 8, 128], BF16)
                vaug = self.ring(st, "pj_vaug", 2, [128, 4, NA_H, 65], BF16)
                for t_, b_ in zip(vaug.tiles, vaug.bufs):
                    fw.op(fw.pool, lambda t_=t_: nc.gpsimd.memset(t_[:, :, :, 64:65], 1.0), writes=[b_])
            else:
                stq = self.ring(st, "pj_stq", 2, [128, 8, 512], BF16)
                stk = self.ring(st, "pj_stk", 2, [128, 8, 512], BF16)
                vaug = self.ring(st, "pj_vaug", 2, [128, DA_H, 4, 129], BF16)
                for t_, b_ in zip(vaug.tiles, vaug.bufs):
                    fw.op(fw.pool, lambda t_=t_: nc.gpsimd.memset(t_[:, :, :, 128:129], 1.0), writes=[b_])
                cosr = self.ring(st, "pj_cos", 2, [128, 512], F32)
                sinr = self.ring(st, "pj_sin", 2, [128, 512], F32)
                rt = [self.ring(st, f"pj_rt{i}", 2, [128, 512], F32) for i in range(4)]
            pi = 0
            for (toff, T) in self.seqs:
                for sbi in range(T // 512):
                    t0 = toff + 512 * sbi
                    blk0 = t0 // 128
                    h, hb = hT.next()
                    for b in range(4):
                        self.frontend(fe, src[t0 + 128 * b:t0 + 128 * b + 128, :], 128, h, hb, 128 * b)
                    if not na:
                        cs, csb = cosr.next()
                        sn, snb = sinr.next()
                        fw.dma(fw.sp, cs[:], self.rot_cos[:, 512 * sbi:512 * sbi + 512], writes=[csb])
                        fw.dma(fw.sp, sn[:], self.rot_sin[:, 512 * sbi:512 * sbi + 512], writes=[snb])
                    for qk in range(2):
                        sg, sgb = (stq if qk == 0 else stk).next()
                        if na:
                            for oc in range(8):
                                P_, Pb = pp[pi % 3], ppb[pi % 3]
                                pi += 1
                                col = qk * D + oc * 128
                                for kc in range(8):
                                    fw.op(fw.pe, lambda kc=kc, col=col, P_=P_: nc.tensor.matmul(P_[:, 0, :], w[:, kc, col:col + 128], h[:, kc, :], start=(kc == 0), stop=(kc == 7)),
                                          reads=[wb, hb], writes=[Pb[0]], inc=(kc == 7))
                                if oc % 2 == 0:
                                    fw.op(fw.act, lambda P_=P_, oc=oc, sg=sg: nc.scalar.activation(out=sg[:, :, oc, :], in_=P_[:, 0, :].rearrange("p (b t) -> p b t", b=4), func=AF.Copy),
                                          reads=[Pb[0]], writes=[sgb])
                                else:
                                    fw.op(fw.dve, lambda P_=P_, oc=oc, sg=sg: nc.vector.tensor_copy(out=sg[:, :, oc, :], in_=P_[:, 0, :].rearrange("p (b t) -> p b t", b=4)),
                                          reads=[Pb[0]], writes=[sgb])
                            dstT = (self.NQT if qk == 0 else self.NKT)
                            fw.dma(fw.sp, dstT[blk0:blk0 + 4].rearrange("b p o t -> p b o t"), sg[:], reads=[sgb])
                        else:
                            for m in range(4):
                                P_, Pb = pp[pi % 3], ppb[pi % 3]
                                pi += 1
                                for ab in range(2):
                                    col = qk * D + m * 256 + ab * 128
                                    for kc in range(8):
                                        fw.op(fw.pe, lambda kc=kc, col=col, P_=P_, ab=ab: nc.tensor.matmul(P_[:, ab, :], w[:, kc, col:col + 128], h[:, kc, :], start=(kc == 0), stop=(kc == 7)),
                                              reads=[wb, hb], writes=[Pb[ab]], inc=(kc == 7))
                                t1, t1b = rt[0].next()
                                t2, t2b = rt[1].next()
                                t3, t3b = rt[2].next()
                                t4, t4b = rt[3].next()
                                TT = nc.vector.tensor_tensor
                                fw.op(fw.dve, lambda P_=P_, t1=t1: TT(out=t1[:], in0=P_[:, 0, :], in1=cs[:], op=ALU.mult), reads=[Pb[0], csb], writes=[t1b])
                                fw.op(fw.dve, lambda P_=P_, t2=t2: TT(out=t2[:], in0=P_[:, 1, :], in1=sn[:], op=ALU.mult), reads=[Pb[1], snb], writes=[t2b])
                                fw.op(fw.dve, lambda P_=P_, t3=t3: TT(out=t3[:], in0=P_[:, 1, :], in1=cs[:], op=ALU.mult), reads=[Pb[1], csb], writes=[t3b])
                                fw.op(fw.dve, lambda P_=P_, t4=t4: TT(out=t4[:], in0=P_[:, 0, :], in1=sn[:], op=ALU.mult), reads=[Pb[0], snb], writes=[t4b])
                                fw.op(fw.pool, lambda t1=t1, t2=t2, m=m, sg=sg: nc.gpsimd.tensor_tensor(out=sg[:, 2 * m, :], in0=t1[:], in1=t2[:], op=ALU.subtract),
                                      reads=[t1b, t2b], writes=[sgb])
                                fw.op(fw.pool, lambda t3=t3, t4=t4, m=m, sg=sg: nc.gpsimd.tensor_tensor(out=sg[:, 2 * m + 1, :], in0=t3[:], in1=t4[:], op=ALU.add),
                                      reads=[t3b, t4b], writes=[sgb])
                            dstT = (self.DQT if qk == 0 else self.DKT)
                            fw.dma(fw.sp, dstT[:, :, t0:t0 + 512].rearrange("o p t -> p o t"), sg[:], reads=[sgb])
                    va, vab = vaug.next()
                    for b in range(4):
                        P_, Pb = pp[pi % 3], ppb[pi % 3]
                        pi += 1
                        for half in range(2):
                            col = 2 * D + half * 512
                            for kc in range(8):
                                fw.op(fw.pe, lambda kc=kc, col=col, P_=P_, half=half, b=b: nc.tensor.matmul(P_[:, half, :], h[:, kc, 128 * b:128 * b + 128], w[:, kc, col:col + 512],
                                                                                                            start=(kc == 0), stop=(kc == 7)),
                                      reads=[wb, hb], writes=[Pb[half]], inc=(kc == 7))
                            if na:
                                o_ap = va[:, b, 8 * half:8 * half + 8, 0:64]
                                i_ap = P_[:, half, :].rearrange("p (h d) -> p h d", h=8)
                            else:
                                o_ap = va[:, 4 * half:4 * half + 4, b, 0:128]
                                i_ap = P_[:, half, :].rearrange("p (h d) -> p h d", h=4)
                            if half == 0:
                                fw.op(fw.act, lambda o_ap=o_ap, i_ap=i_ap: nc.scalar.activation(out=o_ap, in_=i_ap, func=AF.Copy), reads=[Pb[half]], writes=[vab])
                            else:
                                fw.op(fw.dve, lambda o_ap=o_ap, i_ap=i_ap: nc.vector.tensor_copy(out=o_ap, in_=i_ap), reads=[Pb[half]], writes=[vab])
                    if na:
                        fw.dma(fw.sp, self.NV[:, blk0:blk0 + 4, :], va[:].rearrange("p b h d -> p b (h d)"), reads=[vab])
                    else:
                        fw.dma(fw.sp, self.DV[:, :, blk0:blk0 + 4, :].rearrange("h p b x -> p h b x"), va[:], reads=[vab])
            fw.barrier()

    @staticmethod
    def na_kc(j, NB):
        R = 2 * NB
        r0a = min(max(2 * j - 4, 0), R - 8)
        r0b = min(max(2 * j + 1 - 4, 0), R - 8)
        return list(range(r0a // 2, (r0b + 7) // 2 + 1))

    def phase_na2(self, dst):
        nc, fw = self.nc, self.fw
        with ExitStack() as st:
            c = self.consts(st)
            gbc, gbcb = self.load_gbc(st, "na_gbc", 2)
            wo = self.sb(st, "na_wo", [128, 8, D], BF16)
            wob = Buf()
            ORDER = [0, 2, 4, 6, 8, 10, 12, 14, 1, 3, 5, 7, 9, 11, 13, 15]
            for pos, hh_ in enumerate(ORDER):
                fw.dma(fw.pool, wo[64 * (pos % 2):64 * (pos % 2) + 64, pos // 2, :], self.na_wo[hh_ * 64:hh_ * 64 + 64, :], writes=[wob])
            bint = self.sb(st, "na_bint", [128, NA_H, 640], F32)
            bintb = Buf()
            for h0 in range(0, NA_H, 4):
                fw.dma(fw.sp, bint[:, h0:h0 + 4, :], self.bias_int[h0:h0 + 4].rearrange("h p n -> p h n"), writes=[bintb])
            bsp = self.ring(st, "na_bsp", 3, [128, 512], F32)
            kring = self.ring(st, "na_k", 10, [128, 8, 128], BF16)
            vring = self.ring(st, "na_v", 10, [128, NA_H * 65], BF16)
            qr = self.ring(st, "na_q", 3, [128, 8, 128], BF16)
            ssb_r = self.ring(st, "na_ssb", 3, [128, 640], F32)
            pT = self.ring(st, "na_pT", 4, [128, 640], BF16)
            on_r = self.ring(st, "na_on", 2, [128, NA_H, 64], BF16)
            oT_r = self.ring(st, "na_oT", 2, [128, 8, 128], BF16)
            rec_r = self.ring(st, "na_rec", 4, [128, 8], F32)
            be = self.make_backend(st, c, gbc, gbcb)
            Sall = self.ps(st, "na_S", [128, 4, 512])
            Sflat = Sall[:, :, :].rearrange("p a b -> p (a b)")
            Sb = [Buf(), Buf()]
            NREG, RSTR = 2, 1024
            O = [self.ps(st, f"na_O{i}", [128, 512]) for i in range(2)]
            Ob = [Buf(), Buf()]
            M = self.ps(st, "na_M", [128, 2, 512])
            Mb = Buf()
            Mbf = M[:, 0, :].bitcast(BF16)
            GROUPS = ((0, 7, 0), (7, 14, 1), (14, 16, 0))
            units = []
            for si_, (toff, T) in enumerate(self.seqs):
                for j in range(T // 128):
                    for h in range(NA_H):
                        units.append((si_, j, h))
            blocks = {}
            seqstate = {}
            pend = []

            def block_setup(si_, j):
                toff, T = self.seqs[si_]
                NB = T // 128
                blkoff = toff // 128
                if j == 0:
                    seqstate[si_] = {"loaded": -1, "kslot": {}}
                ss_ = seqstate[si_]
                kcs = self.na_kc(j, NB)
                upto = min(max(kcs) + 1, NB - 1)
                while ss_["loaded"] < upto:
                    ss_["loaded"] += 1
                    kt, ktb = kring.next()
                    vt, vtb = vring.next()
                    fw.dma(fw.sp, kt[:], self.NKT[blkoff + ss_["loaded"]], writes=[ktb])
                    fw.dma(fw.sp, vt[:], self.NV[:, blkoff + ss_["loaded"], :], writes=[vtb])
                    ss_["kslot"][ss_["loaded"]] = (kt, ktb, vt, vtb)
                if j < 2:
                    sp_idx = j
                elif j >= NB - 2:
                    sp_idx = 2 + (j - (NB - 2))
                else:
                    sp_idx = None
                q, qb = qr.next()
                fw.dma(fw.sp, q[:], self.NQT[blkoff + j], writes=[qb])
                tk = toff + 128 * j
                xr, xrb = self.backend_prefetch(be, self.x[tk:tk + 128, :], 128)
                on, onb = on_r.next()
                blocks[(si_, j)] = dict(kcs=kcs, sp_idx=sp_idx, q=q, qb=qb, tk=tk, xr=xr, xrb=xrb, on=on, onb=onb,
                                        ks=[ss_["kslot"][kc] for kc in kcs])

            def emit_qk(ui):
                si_, j, pos = units[ui]
                h = ORDER[pos]
                if pos == 0:
                    block_setup(si_, j)
                B_ = blocks[(si_, j)]
                oc, r0 = h // 2, (h % 2) * 64
                reg = ui % NREG
                nt = len(B_["kcs"])
                for i in range(nt):
                    kt, ktb, vt, vtb = B_["ks"][i]
                    fw.op(fw.pe, lambda: nc.tensor.matmul(Sflat[:, RSTR * reg + i * 128:RSTR * reg + (i + 1) * 128], kt[r0:r0 + 64, oc, :], B_["q"][r0:r0 + 64, oc, :],
                                                          start=True, stop=True),
                          reads=[ktb, B_["qb"]], writes=[Sb[reg]], inc=(i == nt - 1))

            def emit_rest(ui):
                si_, j, pos = units[ui]
                h = ORDER[pos]
                B_ = blocks[(si_, j)]
                reg = ui % NREG
                nt = len(B_["kcs"])
                Sf = Sflat[:, RSTR * reg:RSTR * reg + nt * 128]
                if B_["sp_idx"] is None:
                    b_ap, b_b = bint[:, h, 0:nt * 128], bintb
                else:
                    bt, btb = bsp.next()
                    fw.dma(fw.sp, bt[:], self.bias_sp[B_["sp_idx"], h], writes=[btb])
                    b_ap, b_b = bt[:, 0:nt * 128], btb
                ss_, ss_b = ssb_r.next()
                fw.op(fw.dve, lambda: nc.vector.scalar_tensor_tensor(out=ss_[:, 0:nt * 128], in0=Sf, scalar=0.125, in1=b_ap, op0=ALU.mult, op1=ALU.add),
                      reads=[Sb[reg], b_b], writes=[ss_b])
                p_, p_b = pT.next()
                fw.op(fw.act, lambda: nc.scalar.activation(out=p_[:, 0:nt * 128], in_=ss_[:, 0:nt * 128], func=AF.Exp), reads=[ss_b], writes=[p_b])
                pts[ui] = (p_, p_b)

            def emit_pv(ui):
                si_, j, pos = units[ui]
                h = ORDER[pos]
                B_ = blocks[(si_, j)]
                nt = len(B_["kcs"])
                p_, p_b = pts.pop(ui)
                hg0, hg1, ob_i = [g for g in GROUPS if g[0] <= pos < g[1]][0]
                Ot, Otb = O[ob_i], Ob[ob_i]
                sl = pos - hg0
                for i in range(nt):
                    kt, ktb, vt, vtb = B_["ks"][i]
                    fw.op(fw.pe, lambda: nc.tensor.matmul(Ot[:, sl * 65:(sl + 1) * 65], p_[:, i * 128:(i + 1) * 128], vt[:, h * 65:(h + 1) * 65],
                                                          start=(i == 0), stop=(i == nt - 1)),
                          reads=[p_b, vtb], writes=[Otb], inc=(i == nt - 1))
                if pos == hg1 - 1:
                    nh_ = hg1 - hg0
                    rec, recb = rec_r.next()
                    on, onb = B_["on"], B_["onb"]
                    Ov = Ot[:, 0:nh_ * 65].rearrange("p (h d) -> p h d", d=65)
                    fw.op(fw.dve, lambda: nc.vector.reciprocal(out=rec[:, 0:nh_], in_=Ov[:, :, 64]), reads=[Otb], writes=[recb])
                    fw.op(fw.dve, lambda: nc.vector.tensor_tensor(out=on[:, hg0:hg1, :], in0=Ov[:, :, 0:64],
                                                                  in1=rec[:, 0:nh_].unsqueeze(2).to_broadcast([128, nh_, 64]), op=ALU.mult),
                          reads=[Otb, recb], writes=[onb])
                if pos == NA_H - 1:
                    def tail(B_=B_):
                        on, onb = B_["on"], B_["onb"]
                        onf = on[:, :, :].rearrange("p h d -> p (h d)")
                        for k in range(8):
                            fw.op(fw.pe, lambda: nc.tensor.transpose(out=Mbf[:, k * 128:(k + 1) * 128], in_=onf[:, k * 128:(k + 1) * 128], identity=c["idb"][:]),
                                  reads=[onb, c["B"]], writes=[Mb], inc=(k == 7))
                        oT, oTb = oT_r.next()
                        fw.op(fw.dve, lambda: nc.vector.tensor_copy(out=oT[:, :, :].rearrange("p a b -> p (a b)"), in_=Mbf[:, :]), reads=[Mb], writes=[oTb])
                        for half in range(2):
                            for k in range(8):
                                fw.op(fw.pe, lambda: nc.tensor.matmul(M[:, half, :], oT[:, k, :], wo[:, k, half * 512:(half + 1) * 512], start=(k == 0), stop=(k == 7)),
                                      reads=[oTb, wob], writes=[Mb], inc=(k == 7 and half == 1))
                        tk = B_["tk"]
                        self.backend(be, M, Mb, 128, B_["xr"], B_["xrb"], dst[tk:tk + 128, :])
                    pend.append(tail)
                if pos == 3:
                    while pend:
                        pend.pop(0)()

            pts = {}
            for ui in range(len(units) + 2):
                if ui < len(units):
                    emit_qk(ui)
                if 0 <= ui - 1 < len(units):
                    emit_rest(ui - 1)
                if ui - 2 >= 0:
                    emit_pv(ui - 2)
            while pend:
                pend.pop(0)()
            fw.barrier()

    def phase_da2(self):
        nc, fw = self.nc, self.fw
        with ExitStack() as st:
            c = self.consts(st)
            lam = self.sb(st, "da_lam", [128, 4, 64], F32)
            lamb = Buf()
            for i in range(4):
                fw.dma(fw.sp, lam[:, i, :], self.lam_in[i, :].partition_broadcast(128), writes=[lamb])
            lj = self.sb(st, "da_lj", [128, 64], F32)
            lv = self.sb(st, "da_lv", [128, 8], F32)
            lvb = Buf()
            for i in range(2):
                fw.op(fw.dve, lambda i=i: nc.vector.scalar_tensor_tensor(out=lj[:], in0=lam[:, 2 * i, :], scalar=1.0, in1=lam[:, 2 * i + 1, :], op0=ALU.mult, op1=ALU.mult,
                                                                       accum_out=lv[:, i:i + 1]), reads=[lamb], writes=[lvb])
            fw.op(fw.act, lambda: nc.scalar.activation(out=lv[:, 2:4], in_=lv[:, 0:2], func=AF.Exp), reads=[lvb], writes=[lvb])
            fw.op(fw.dve, lambda: nc.vector.scalar_tensor_tensor(out=lv[:, 4:5], in0=lv[:, 3:4], scalar=-LAMBDA_INIT, in1=lv[:, 2:3], op0=ALU.add, op1=ALU.subtract),
                  reads=[lvb], writes=[lvb])
            gsub = self.sb(st, "da_gsub", [128, 128], F32)
            gsubb = Buf()
            fw.dma(fw.sp, gsub[:], self.subln_g.partition_broadcast(128), writes=[gsubb])
            fw.op(fw.dve, lambda: nc.vector.tensor_scalar(out=gsub[:], in0=gsub[:], scalar1=(1.0 - LAMBDA_INIT), scalar2=None, op0=ALU.mult), reads=[gsubb], writes=[gsubb])

            TM = self.TMAX
            NCM = TM // 128
            KT = self.ring(st, "da_KT", 2, [128, TM], BF16)
            QT = self.ring(st, "da_QT", 2, [128, TM], BF16)
            VH = self.ring(st, "da_VH", 2, [128, NCM, 129], BF16)
            E = self.ring(st, "da_E", 3, [128, 2, 512], BF16)
            accs = self.ring(st, "da_accs", 2, [128, 2, 2, 129], F32)
            rec_r = self.ring(st, "da_rec", 2, [128, 2, 2], F32)
            o1_r = self.ring(st, "da_o1", 2, [128, 128], F32)
            o_r = self.ring(st, "da_o", 2, [128, 128], F32)
            oj = self.sb(st, "da_oj", [128, 128], F32)
            ojb = Buf()
            ss_r = self.ring(st, "da_ss", 4, [128, 4], F32)
            on_r = self.ring(st, "da_on", 4, [128, 128], BF16)
            oT_r = self.ring(st, "da_oT", 2, [128, 512], BF16)
            S = [self.ps(st, f"da_S{i}", [128, 2, 512]) for i in range(2)]
            Sb = [Buf(), Buf()]
            A = [self.ps(st, f"da_A{i}", [128, 512]) for i in range(2)]
            Ab = [Buf(), Buf()]
            TPp = self.ps(st, "da_TP", [128, 512])
            TPb = Buf()
            TPbf = TPp[:, :].bitcast(BF16)
            pend = []

            def flush_pend():
                while pend:
                    pend.pop(0)()

            for (toff, T) in self.seqs:
                NC_ = T // 128
                NPAIR = NC_ // 2
                blkoff = toff // 128
                for h in range(DA_H):
                    kt, ktb = KT.next()
                    qt_, qtb = QT.next()
                    vh, vhb = VH.next()
                    m, hh = h // 2, h % 2
                    for i in range(2):
                        sl = 2 * hh + i
                        for ab in range(2):
                            prt = i * 64 + ab * 32
                            fw.dma(fw.sp, kt[prt:prt + 32, 0:T], self.DKT[2 * m + ab, sl * 32:sl * 32 + 32, toff:toff + T], writes=[ktb])
                            fw.dma(fw.sp, qt_[prt:prt + 32, 0:T], self.DQT[2 * m + ab, sl * 32:sl * 32 + 32, toff:toff + T], writes=[qtb])
                    fw.dma(fw.sp, vh[:, 0:NC_, :], self.DV[h, :, blkoff:blkoff + NC_, :], writes=[vhb])
                    oT = oTb = None
                    for qi in range(T // 256):
                        q0 = qi * 256

                        def qk_pair(pp):
                            S_, S_b = S[pp % 2], Sb[pp % 2]
                            for hf in range(2):
                                kb = 2 * pp + hf
                                for i in range(2):
                                    fw.op(fw.pe, lambda: nc.tensor.matmul(S_[:, i, hf * 256:hf * 256 + 256], kt[64 * i:64 * i + 64, kb * 128:(kb + 1) * 128],
                                                                          qt_[64 * i:64 * i + 64, q0:q0 + 256], start=True, stop=True),
                                          reads=[ktb, qtb], writes=[S_b], inc=(i == 1 and hf == 1))

                        qk_pair(0)
                        for pp in range(NPAIR):
                            if pp + 1 < NPAIR:
                                qk_pair(pp + 1)
                            S_, S_b = S[pp % 2], Sb[pp % 2]
                            e, eb = E.next()
                            fw.op(fw.act, lambda: nc.scalar.activation(out=e[:, :, :], in_=S_[:, :, :], func=AF.Exp, scale=0.125), reads=[S_b], writes=[eb])
                            for hf in range(2):
                                kb = 2 * pp + hf
                                for i in range(2):
                                    for s in range(2):
                                        first = (kb == 0 and s == 0)
                                        fw.op(fw.pe, lambda: nc.tensor.matmul(A[i][:, 256 * s:256 * s + 129], e[:, i, hf * 256 + 128 * s:hf * 256 + 128 * s + 128], vh[:, kb, :],
                                                                              start=first, stop=(kb == NC_ - 1), skip_group_check=True),
                                              reads=[eb, vhb], writes=[Ab[i]], inc=(i == 1 and s == 1 and hf == 1))
                            if pp == 2:
                                flush_pend()
                        flush_pend()
                        ac, acb = accs.next()
                        for i in range(2):
                            fw.op(fw.dve, lambda: nc.vector.tensor_copy(out=ac[:, i, :, :], in_=A[i][:, :].rearrange("p (s x) -> p s x", s=2)[:, :, 0:129]),
                                  reads=[Ab[i]], writes=[acb])
                        rec, recb = rec_r.next()
                        fw.op(fw.dve, lambda: nc.vector.reciprocal(out=rec[:, :, :], in_=ac[:, :, :, 128]), reads=[acb], writes=[recb])
                        fw.op(fw.dve, lambda: nc.vector.tensor_scalar(out=rec[:, 1, :], in0=rec[:, 1, :], scalar1=lv[:, 4:5], scalar2=None, op0=ALU.mult),
                              reads=[recb, lvb], writes=[recb])
                        if qi % 2 == 0:
                            oT, oTb = oT_r.next()
                        ons = []
                        for s in range(2):
                            o1, o1b = o1_r.next()
                            o_, o_b = o_r.next()
                            ss, ssb = ss_r.next()
                            on, onb = on_r.next()
                            fw.op(fw.dve, lambda: nc.vector.tensor_scalar(out=o1[:], in0=ac[:, 0, s, 0:128], scalar1=rec[:, 0, s:s + 1], scalar2=None, op0=ALU.mult),
                                  reads=[acb, recb], writes=[o1b])
                            fw.op(fw.dve, lambda: nc.vector.scalar_tensor_tensor(out=o_[:], in0=ac[:, 1, s, 0:128], scalar=rec[:, 1, s:s + 1], in1=o1[:], op0=ALU.mult, op1=ALU.add),
                                  reads=[acb, recb, o1b], writes=[o_b])
                            fw.op(fw.dve, lambda: nc.vector.scalar_tensor_tensor(out=oj[:], in0=o_[:], scalar=1.0, in1=o_[:], op0=ALU.mult, op1=ALU.mult, accum_out=ss[:, 0:1]),
                                  reads=[o_b], writes=[ojb, ssb])
                            self.rstd(ss, ssb, 128, 1.0 / 128, SUBLN_EPS, c)
                            fw.op(fw.dve, lambda: nc.vector.scalar_tensor_tensor(out=on[:], in0=o_[:], scalar=ss[:, 2:3], in1=gsub[:], op0=ALU.mult, op1=ALU.mult),
                                  reads=[o_b, ssb, gsubb], writes=[onb])
                            ons.append((on, onb))

                        def tail(ons=ons, oT=oT, oTb=oTb, qi=qi, h=h, toff=toff):
                            for s, (on, onb) in enumerate(ons):
                                fw.op(fw.pe, lambda: nc.tensor.transpose(out=TPbf[:, 128 * s:128 * s + 128], in_=on[:], identity=c["idb"][:]),
                                      reads=[onb, c["B"]], writes=[TPb], inc=(s == 1))
                            cc = (qi % 2) * 256
                            fw.op(fw.dve, lambda: nc.vector.tensor_copy(out=oT[:, cc:cc + 256], in_=TPbf[:, 0:256]), reads=[TPb], writes=[oTb])
                            if qi % 2 == 1:
                                tq = toff + (qi - 1) * 256
                                fw.dma(fw.sp, self.DOT[h, :, tq:tq + 512], oT[:], reads=[oTb])
                        pend.append(tail)
            flush_pend()
            fw.barrier()

    def phase_da3(self, src, dst):
        nc, fw = self.nc, self.fw
        with ExitStack() as st:
            c = self.consts(st)
            gbc, gbcb = self.load_gbc(st, "d3_gbc", 3)
            wo = self.sb(st, "d3_wo", [128, 8, D], BF16)
            wob = Buf()
            fw.dma(fw.pool, wo[:], self.da_wo.rearrange("(kc p) n -> p kc n", p=128), writes=[wob])
            be = self.make_backend(st, c, gbc, gbcb)
            oT = self.ring(st, "d3_oT", 2, [128, 8, 512], BF16)
            M = [self.ps(st, f"d3_M{i}", [128, 2, 512]) for i in range(2)]
            Mb = [Buf(), Buf()]
            mi = 0
            for (toff, T) in self.seqs:
                for sbi in range(T // 512):
                    t0 = toff + 512 * sbi
                    o, ob = oT.next()
                    fw.dma(fw.sp, o[:], self.DOT[:, :, t0:t0 + 512].rearrange("h p t -> p h t"), writes=[ob])
                    for b in range(4):
                        tk = t0 + 128 * b
                        xr, xrb = self.backend_prefetch(be, src[tk:tk + 128, :], 128)
                        M_, M_b = M[mi % 2], Mb[mi % 2]
                        mi += 1
                        for half in range(2):
                            for k in range(8):
                                fw.op(fw.pe, lambda k=k, half=half, M_=M_, b=b, o=o: nc.tensor.matmul(M_[:, half, :], o[:, k, 128 * b:128 * b + 128], wo[:, k, half * 512:(half + 1) * 512],
                                                                                                      start=(k == 0), stop=(k == 7)),
                                      reads=[ob, wob], writes=[M_b], inc=(k == 7 and half == 1))
                        self.backend(be, M_, M_b, 128, xr, xrb, dst[tk:tk + 128, :])
            fw.barrier()

    def build(self, phases="all"):
        fw = self.fw
        P = phases
        if P == "all" or "p0" in P:
            self.phase_p0()
        if P == "all" or "na1" in P:
            self.phase_proj("na", self.x)
        if P == "all" or "na2" in P:
            self.phase_na2(self.XA)
        if P == "all" or "ffn0" in P:
            self.phase_ffn(0, self.XA, self.XB)
        if P == "all" or "da1" in P:
            self.phase_proj("da", self.XB)
        if P == "all" or "da2" in P:
            self.phase_da2()
        if P == "all" or "da3" in P:
            self.phase_da3(self.XB, self.XC)
        if P == "all" or "ffn1" in P:
            self.phase_ffn(1, self.XC, self.y)
        fw.finish()
        return self.nc


def _na_bias_tables(rpb):
    R = 16
    NB = R // 2
    W = 64

    def table(j):
        kcs = Builder.na_kc(j, NB)
        out = np.full((NA_H, 128, len(kcs), 128), NEG, np.float32)
        rq = 2 * j + np.arange(128) // 64
        cq = np.arange(128) % 64
        r0 = np.clip(rq - 4, 0, R - 8)
        c0 = np.clip(cq - 8, 0, W - 16)
        for i, kc in enumerate(kcs):
            rk = 2 * kc + np.arange(128) // 64
            ck = np.arange(128) % 64
            valid = ((rk[:, None] >= r0[None, :]) & (rk[:, None] < r0[None, :] + 8) &
                     (ck[:, None] >= c0[None, :]) & (ck[:, None] < c0[None, :] + 16))
            dr = np.clip(rk[:, None] - rq[None, :] + 7, 0, 14)
            dc = np.clip(ck[:, None] - cq[None, :] + 15, 0, 30)
            vals = rpb[:, dr, dc]
            out[:, :, i, :] = np.where(valid[None], vals, np.float32(NEG))
        return out.reshape(NA_H, 128, len(kcs) * 128)

    interior = table(3)
    assert interior.shape[2] == 640
    sp = np.stack([table(0), table(1), table(NB - 2), table(NB - 1)], 0)
    assert sp.shape[3] == 512
    return np.ascontiguousarray(interior), np.ascontiguousarray(sp)


def _rot_tables(T):
    inv = (1.0 / (ROPE_THETA ** (np.arange(0, 64, 2, dtype=np.float32) / np.float32(64)))).astype(np.float32)
    ang = np.arange(T, dtype=np.float32)[None, :] * inv[np.arange(128) % 32][:, None]
    return np.cos(ang).astype(np.float32), np.sin(ang).astype(np.float32)


def _perm_cols():
    cols = []
    for m in range(4):
        for ab in range(2):
            for sl in range(4):
                for dl in range(32):
                    cols.append((4 * m + sl) * 64 + ab * 32 + dl)
    return np.array(cols)


def prepare_shared(inp, TMAX):
    f = lambda a: np.ascontiguousarray(np.asarray(a, dtype=np.float32))
    perm = _perm_cols()
    bi, bsp = _na_bias_tables(f(inp["na_rpb"])[0])
    cos, sin = _rot_tables(TMAX)
    sh = {
        "gvec": f(np.concatenate([inp["attn_pre_g"], inp["attn_post_g"], inp["ffn_pre_g"], inp["ffn_post_g"]], 0)),
        "w_qkv": f(inp["na_w_qkv"][0]), "na_wo": f(inp["na_w_o"][0]),
        "da_wq": f(np.asarray(inp["da_w_q"][0])[:, perm]), "da_wk": f(np.asarray(inp["da_w_k"][0])[:, perm]),
        "da_wv": f(inp["da_w_v"][0]), "da_wo": f(inp["da_w_o"][0]),
        "w_in": f(inp["ffn_w_in"]), "w_out": f(inp["ffn_w_out"]),
        "conv_w": f(inp["ffn_conv_w"]), "conv_b": f(inp["ffn_conv_b"]),
        "bias_int": bi, "bias_sp": bsp, "rot_cos": cos, "rot_sin": sin,
        "lam_in": f(np.stack([inp["da_lambda_q1"][0], inp["da_lambda_k1"][0], inp["da_lambda_q2"][0], inp["da_lambda_k2"][0]], 0)),
        "subln_g": f(inp["da_subln_g"][0]), "ident": np.eye(128, dtype=np.float32),
    }
    return sh


def kernel(**inp):
    xs = np.asarray(inp["x_sample"], dtype=np.float32)
    xp = np.asarray(inp["x_prompt"], dtype=np.float32)
    NS, TS, _ = xs.shape
    NP_, TP, _ = xp.shape
    ncore = 8
    assert NS == ncore and NP_ * 2 == ncore
    seqs = [(0, TS), (TS, TP)]
    bld = Builder(seqs)
    nc = bld.build()
    sh = prepare_shared(inp, max(TS, TP))
    in_maps = []
    for cidx in range(ncore):
        m = dict(sh)
        m["x"] = np.ascontiguousarray(np.concatenate([xs[cidx], xp[cidx // 2]], 0))
        in_maps.append(m)
    res = run_bass_kernel_spmd(nc, in_maps, core_ids=list(range(ncore)))
    y_s = np.stack([res.results[cidx]["y"][:TS] for cidx in range(ncore)], 0)
    y_p = np.stack([res.results[2 * p]["y"][TS:] for p in range(NP_)], 0)
    return (y_p.astype(np.float32), y_s.astype(np.float32))
```

```python
import math
from contextlib import ExitStack
import numpy as np
import ml_dtypes
import concourse.bass as bass
import concourse.mybir as mybir
from concourse.bass_utils import run_bass_kernel_spmd

F32 = mybir.dt.float32
BF16 = mybir.dt.bfloat16
AF = mybir.ActivationFunctionType
ALU = mybir.AluOpType

D = 1024
DFF = 2816
NFC = 22
NA_H = 16
DA_H = 8
NEG = -30000.0
NORM_EPS = 1e-6
SUBLN_EPS = 1e-5
LAMBDA_INIT = 0.8 - 0.6 * math.exp(-0.3 * 1)
ROPE_THETA = 10000.0
import os as _os
SEQ_FFN = bool(int(_os.environ.get('SEQ_FFN', '0')))


class Buf:
    __slots__ = ("name", "w", "r")

    def __init__(self, name=""):
        self.name = name
        self.w = None
        self.r = {}


class Eng:
    def __init__(self, nc, h, name):
        self.h = h
        self.name = name
        self.sem = nc.alloc_semaphore(name="s_" + name)
        self.count = 0
        self.known = {}


class FW:
    NDMASEM = 48

    def __init__(self, nc):
        self.nc = nc
        self.pe = Eng(nc, nc.tensor, "pe")
        self.act = Eng(nc, nc.scalar, "act")
        self.dve = Eng(nc, nc.vector, "dve")
        self.pool = Eng(nc, nc.gpsimd, "pool")
        self.sp = Eng(nc, nc.sync, "sp")
        self.engs = [self.pe, self.act, self.dve, self.pool, self.sp]
        self.dsems = [nc.alloc_semaphore(name=f"d{i}") for i in range(self.NDMASEM)]
        self.dval = [0] * self.NDMASEM
        self.dnext = 0
        self.nops = 0
        self.ndma = 0
        self.rec = None

    def wait(self, eng, tok):
        if tok is None:
            return
        sem, val = tok
        if sem is eng.sem:
            if eng is self.pe:
                return
            if val > eng.count:
                return
        k = id(sem)
        if eng.known.get(k, 0) >= val:
            return
        eng.h.wait_ge(sem, val)
        eng.known[k] = val
        if self.rec is not None:
            self.rec[eng.name].append(("w", k, val))

    def _deps(self, eng, reads, writes):
        for b in reads:
            self.wait(eng, b.w)
        for b in writes:
            self.wait(eng, b.w)
            for tok in list(b.r.values()):
                self.wait(eng, tok)

    def _commit(self, tok, reads, writes):
        k = id(tok[0])
        for b in reads:
            if k not in b.r or b.r[k][1] < tok[1]:
                b.r[k] = tok
        for b in writes:
            b.w = tok
            b.r = {}

    def op(self, eng, fn, reads=(), writes=(), inc=True):
        self._deps(eng, reads, writes)
        ins = fn()
        self.nops += 1
        if inc:
            eng.count += 1
            ins.then_inc(eng.sem, 1)
            tok = (eng.sem, eng.count)
            if self.rec is not None:
                self.rec[eng.name].append(("i", id(eng.sem), 1))
        else:
            tok = (eng.sem, eng.count + 1)
        self._commit(tok, reads, writes)
        return tok

    def dma(self, eng, out, in_, reads=(), writes=(), **kw):
        self._deps(eng, reads, writes)
        i = self.dnext
        self.dnext = (self.dnext + 1) % self.NDMASEM
        sem = self.dsems[i]
        if self.dval[i] > 0:
            self.wait(eng, (sem, self.dval[i]))
        self.dval[i] += 16
        eng.h.dma_start(out=out, in_=in_, **kw).then_inc(sem, 16)
        self.ndma += 1
        if self.rec is not None:
            self.rec[eng.name].append(("i", id(sem), 16))
        tok = (sem, self.dval[i])
        self._commit(tok, reads, writes)
        return tok

    def barrier(self):
        for e in self.engs:
            for o in self.engs:
                if o is not e and o.count > 0:
                    self.wait(e, (o.sem, o.count))
            for i, sem in enumerate(self.dsems):
                if self.dval[i] > 0:
                    self.wait(e, (sem, self.dval[i]))

    def finish(self):
        for i, sem in enumerate(self.dsems):
            if self.dval[i] > 0:
                self.wait(self.sp, (sem, self.dval[i]))
        for e in (self.pe, self.act, self.dve, self.pool):
            if e.count > 0:
                self.wait(self.sp, (e.sem, e.count))


class Ring:
    def __init__(self, tiles):
        self.tiles = tiles
        self.bufs = [Buf() for _ in tiles]
        self.i = -1

    def next(self):
        self.i = (self.i + 1) % len(self.tiles)
        return self.tiles[self.i], self.bufs[self.i]

    def cur(self):
        return self.tiles[self.i], self.bufs[self.i]


class Builder:
    def __init__(self, seqs, debug=False):
        self.seqs = seqs
        self.NT = sum(T for _, T in seqs)
        self.TMAX = max(T for _, T in seqs)
        self.debug = debug
        self.nc = nc = bass.Bass("TRN2", target_bir_lowering=False)
        self.fw = FW(nc)
        NT = self.NT
        dt = nc.dram_tensor
        I = "ExternalInput"
        self.x = dt("x", [NT, D], F32, kind=I).ap()
        self.gvec = dt("gvec", [8, D], F32, kind=I).ap()
        self.w_qkv = dt("w_qkv", [D, 3 * D], F32, kind=I).ap()
        self.na_wo = dt("na_wo", [D, D], F32, kind=I).ap()
        self.da_wq = dt("da_wq", [D, D], F32, kind=I).ap()
        self.da_wk = dt("da_wk", [D, D], F32, kind=I).ap()
        self.da_wv = dt("da_wv", [D, D], F32, kind=I).ap()
        self.da_wo = dt("da_wo", [D, D], F32, kind=I).ap()
        self.w_in = dt("w_in", [2, D, 2 * DFF], F32, kind=I).ap()
        self.w_out = dt("w_out", [2, DFF, D], F32, kind=I).ap()
        self.conv_w = dt("conv_w", [2, 3, 2 * DFF], F32, kind=I).ap()
        self.conv_b = dt("conv_b", [2, 2 * DFF], F32, kind=I).ap()
        self.bias_int = dt("bias_int", [NA_H, 128, 640], F32, kind=I).ap()
        self.bias_sp = dt("bias_sp", [4, NA_H, 128, 512], F32, kind=I).ap()
        self.rot_cos = dt("rot_cos", [128, self.TMAX], F32, kind=I).ap()
        self.rot_sin = dt("rot_sin", [128, self.TMAX], F32, kind=I).ap()
        self.lam_in = dt("lam_in", [4, 64], F32, kind=I).ap()
        self.subln_g = dt("subln_g", [128], F32, kind=I).ap()
        self.ident = dt("ident", [128, 128], F32, kind=I).ap()
        self.y = dt("y", [seqs[0][1] + seqs[1][1] // 2 + 512, D], F32, kind="ExternalOutput").ap()
        S = "ExternalOutput" if debug else "Internal"
        self.XA = dt("XA", [NT, D], F32, kind=S).ap()
        self.XB = dt("XB", [NT, D], F32, kind=S).ap()
        self.WINB = dt("WINB", [2, 11, 128, 8, 512], BF16, kind="Internal").ap()
        NBLK = NT // 128
        self.NQT = dt("NQT", [NBLK, 128, 8, 128], BF16, kind="Internal").ap()
        self.NKT = dt("NKT", [NBLK, 128, 8, 128], BF16, kind="Internal").ap()
        self.NV = dt("NV", [128, NBLK, NA_H * 65], BF16, kind="Internal").ap()
        self.DQT = dt("DQT", [8, 128, NT], BF16, kind="Internal").ap()
        self.DKT = dt("DKT", [8, 128, NT], BF16, kind="Internal").ap()
        self.DV = dt("DV", [DA_H, 128, NBLK, 129], BF16, kind="Internal").ap()
        self.DOT = dt("DOT", [DA_H, 128, NT], BF16, kind="Internal").ap()
        self.XC = self.XA
        (self.TS_off, self.TS), (self.TP_off, self.TP) = seqs
        self.OWN = self.TP // 2 + 512
        self.OFF1 = self.TP - self.OWN
        self.fsel = dt("fsel", [128, 2], F32, kind=I).ap()
        self.DQTo = dt("DQTo", [8, 128, self.OWN], BF16, kind="Internal").ap()
        self.DOTo = dt("DOTo", [DA_H, 128, self.OWN], BF16, kind="Internal").ap()
        self.XBo = dt("XBo", [self.OWN, D], F32, kind="Internal").ap()
        self.XCo = dt("XCo", [self.OWN, D], F32, kind="Internal").ap()

    _uid = 0

    def _nm(self, name):
        Builder._uid += 1
        return f"{name}_{Builder._uid}"

    def sb(self, st, name, shape, dtype):
        return st.enter_context(self.nc.sbuf_tensor(self._nm(name), shape, dtype))

    def ps(self, st, name, shape, dtype=F32):
        return st.enter_context(self.nc.psum_tensor(self._nm(name), shape, dtype))

    def ring(self, st, name, n, shape, dtype):
        return Ring([self.sb(st, f"{name}{i}", shape, dtype) for i in range(n)])

    def consts(self, st):
        nc, fw = self.nc, self.fw
        c = {}
        c["idf"] = self.sb(st, "idf", [128, 128], F32)
        c["idb"] = self.sb(st, "idb", [128, 128], BF16)
        c["nh"] = self.sb(st, "nhalf", [128, 1], F32)
        c["B"] = Buf("consts")
        fw.dma(fw.sp, c["idf"][:], self.ident[:, :], writes=[c["B"]])
        fw.dma(fw.pool, c["idb"][:], self.ident[:, :], writes=[c["B"]])
        fw.op(fw.pool, lambda: nc.gpsimd.memset(c["nh"][:], -0.5), writes=[c["B"]])
        return c

    def load_gfm(self, st, name, row):
        nc, fw = self.nc, self.fw
        t = self.sb(st, name, [128, 8], F32)
        b = Buf(name)
        with nc.allow_non_contiguous_dma(reason="tiny gain vector"):
            fw.dma(fw.sp, t[:], self.gvec[row, :].rearrange("(c p) -> p c", p=128), writes=[b])
        return t, b

    def load_gbc(self, st, name, row):
        fw = self.fw
        t = self.sb(st, name, [128, D], F32)
        b = Buf(name)
        fw.dma(fw.sp, t[:], self.gvec[row, :].partition_broadcast(128), writes=[b])
        return t, b

    def rstd(self, ss, ssb, n, inv_dim, eps, c):
        nc, fw = self.nc, self.fw
        fw.op(fw.pool, lambda: nc.gpsimd.tensor_scalar(out=ss[:n, 1:2], in0=ss[:n, 0:1], scalar1=inv_dim, scalar2=eps,
                                                        op0=ALU.mult, op1=ALU.add), reads=[ssb], writes=[ssb])
        fw.op(fw.pool, lambda: nc.gpsimd.tensor_tensor(out=ss[:n, 2:3], in0=ss[:n, 1:2], in1=c["nh"][:n, :], op=ALU.pow),
              reads=[ssb, c["B"]], writes=[ssb])

    def make_frontend(self, st, c, gfm, gfmb, tp, tpb):
        fe = {"xt": self.ring(st, "fe_xt", 2, [128, D], F32), "xs": self.ring(st, "fe_xs", 3, [128, D], F32),
              "junk": self.sb(st, "fe_junk", [128, D], BF16), "junkb": Buf(),
              "ss": self.ring(st, "fe_ss", 4, [128, 4], F32), "c": c, "gfm": gfm, "gfmb": gfmb, "tp": tp, "tpb": tpb}
        return fe

    def frontend_a(self, fe, src, n):
        nc, fw = self.nc, self.fw
        c = fe["c"]
        xt, xtb = fe["xt"].next()
        xs, xsb = fe["xs"].next()
        ss, ssb = fe["ss"].next()
        fw.dma(fw.sp, xt[:n, :], src, writes=[xtb])
        fw.op(fw.act, lambda: nc.scalar.activation(out=fe["junk"][:n, :], in_=xt[:n, :], func=AF.Square, accum_out=ss[:n, 0:1]),
              reads=[xtb], writes=[fe["junkb"], ssb])
        self.rstd(ss, ssb, n, 1.0 / D, NORM_EPS, c)
        fw.op(fw.act, lambda: nc.scalar.activation(out=xs[:n, :], in_=xt[:n, :], func=AF.Copy, scale=ss[:n, 2:3]),
              reads=[xtb, ssb], writes=[xsb])
        return xs, xsb

    def frontend_b(self, fe, xs, xsb, n, hT, hTb, col0):
        nc, fw = self.nc, self.fw
        c = fe["c"]
        tp, tpb = fe["tp"], fe["tpb"]
        for k in range(8):
            fw.op(fw.pe, lambda: nc.tensor.transpose(out=tp[:, k, :n], in_=xs[:n, k * 128:(k + 1) * 128], identity=c["idf"][:n, :n]),
                  reads=[xsb, c["B"]], writes=[tpb], inc=(k == 7))
        fw.op(fw.dve, lambda: nc.vector.tensor_tensor(out=hT[:, :, col0:col0 + n], in0=tp[:, :, :n],
                                                      in1=fe["gfm"][:, :].unsqueeze(2).to_broadcast([128, 8, n]), op=ALU.mult),
              reads=[tpb, fe["gfmb"]], writes=[hTb])

    def frontend(self, fe, src, n, hT, hTb, col0):
        xs, xsb = self.frontend_a(fe, src, n)
        self.frontend_b(fe, xs, xsb, n, hT, hTb, col0)

    def make_backend(self, st, c, gbc, gbcb):
        be = {"xr": self.ring(st, "be_xr", 2, [128, D], F32), "t": self.ring(st, "be_t", 2, [128, D], F32),
              "o": self.ring(st, "be_o", 2, [128, D], F32), "junk": self.sb(st, "be_junk", [128, D], BF16), "junkb": Buf(),
              "ss": self.ring(st, "be_ss", 4, [128, 4], F32), "c": c, "gbc": gbc, "gbcb": gbcb}
        return be

    def backend_prefetch(self, be, res_src, n):
        xr, xrb = be["xr"].next()
        self.fw.dma(self.fw.sp, xr[:n, :], res_src, writes=[xrb])
        return xr, xrb

    def backend(self, be, m, mb, n, xr, xrb, dst):
        nc, fw = self.nc, self.fw
        ss, ssb = be["ss"].next()
        t, tb = be["t"].next()
        o, ob = be["o"].next()
        m2 = m[:n, :, :].rearrange("p a b -> p (a b)")
        fw.op(fw.act, lambda: nc.scalar.activation(out=be["junk"][:n, :], in_=m2, func=AF.Square, accum_out=ss[:n, 0:1]),
              reads=[mb], writes=[be["junkb"], ssb])
        self.rstd(ss, ssb, n, 1.0 / D, NORM_EPS, be["c"])
        fw.op(fw.dve, lambda: nc.vector.scalar_tensor_tensor(out=t[:n, :], in0=m2, scalar=ss[:n, 2:3], in1=be["gbc"][:n, :],
                                                             op0=ALU.mult, op1=ALU.mult), reads=[mb, ssb, be["gbcb"]], writes=[tb])
        fw.op(fw.pool, lambda: nc.gpsimd.tensor_tensor(out=o[:n, :], in0=t[:n, :], in1=xr[:n, :], op=ALU.add),
              reads=[tb, xrb], writes=[ob])
        fw.dma(fw.sp, dst, o[:n, :], reads=[ob])

    def phase_p0(self):
        nc, fw = self.nc, self.fw
        with ExitStack() as st:
            stg = self.ring(st, "p0_stg", 3, [128, 8, 512], BF16)
            for l in range(2):
                for g in range(11):
                    t, tb = stg.next()
                    src = self.w_in[l]
                    fw.dma(fw.pool, t[:, :, 0:256], src[:, 256 * g:256 * g + 256].rearrange("(kc p) n -> p kc n", p=128), writes=[tb])
                    fw.dma(fw.pool, t[:, :, 256:512], src[:, DFF + 256 * g:DFF + 256 * g + 256].rearrange("(kc p) n -> p kc n", p=128), writes=[tb])
                    fw.dma(fw.sp, self.WINB[l, g], t[:], reads=[tb])
            fw.barrier()

    def phase_ffn(self, l, seqlist):
        nc, fw = self.nc, self.fw
        with ExitStack() as st:
            c = self.consts(st)
            gfm, gfmb = self.load_gfm(st, "ffn_gfm", 4 + l)
            gbc, gbcb = self.load_gbc(st, "ffn_gbc", 6 + l)
            wout = self.sb(st, "ffn_wout", [128, NFC, D], BF16)
            woutb = Buf()
            for f0 in range(0, NFC, 6):
                f1 = min(NFC, f0 + 6)
                fw.dma(fw.pool, wout[:, f0:f1, :], self.w_out[l, f0 * 128:f1 * 128, :].rearrange("(f p) n -> p f n", p=128), writes=[woutb])
            cw = self.sb(st, "ffn_cw", [128, 3, 2 * NFC], F32)
            cb = self.sb(st, "ffn_cb", [128, 2 * NFC], F32)
            cwb = Buf()
            with nc.allow_non_contiguous_dma(reason="small conv params"):
                for k in range(3):
                    fw.dma(fw.sp, cw[:, k, :], self.conv_w[l, k, :].rearrange("(c p) -> p c", p=128), writes=[cwb])
                fw.dma(fw.sp, cb[:, :], self.conv_b[l, :].rearrange("(c p) -> p c", p=128), writes=[cwb])
            tp = self.ps(st, "ffn_tp", [128, 8, 128])
            tpb = Buf()
            pgu = [self.ps(st, f"ffn_pgu{i}", [128, 2, 512]) for i in range(2)]
            pgub = [[Buf(), Buf()] for _ in range(2)]
            pm = self.ps(st, "ffn_pm", [128, 2, 512])
            pmb = Buf()
            fe = self.make_frontend(st, c, gfm, gfmb, tp, tpb)
            be = self.make_backend(st, c, gbc, gbcb)
            hT = self.ring(st, "ffn_hT", 2, [128, 8, 512], BF16)
            wg = self.ring(st, "ffn_wg", 3, [128, 8, 512], BF16)
            aT = self.ring(st, "ffn_aT", 2, [128, NFC, 513], BF16)
            carry = self.sb(st, "ffn_carry", [128, 2 * NFC, 2], F32)
            carryb = [Buf() for _ in range(2 * NFC)]
            T1 = self.ring(st, "ffn_T1", 2, [128, 513], F32)
            T2 = self.ring(st, "ffn_T2", 2, [128, 513], F32)
            TG = self.ring(st, "ffn_TG", 2, [128, 513], F32)
            TU = self.ring(st, "ffn_TU", 2, [128, 513], F32)
            GG = self.ring(st, "ffn_GG", 2, [128, 513], F32)

            tiles = []
            for (src_, dst_, T) in seqlist:
                nt_ = T // 512
                for i in range(nt_):
                    tiles.append(dict(toff=0, src=src_, dst=dst_, i=i, first=(i == 0), last=(i == nt_ - 1)))
            NTL = len(tiles)
            for n, tl in enumerate(tiles):
                tl["h"], tl["hb"] = None, None
                W = 513 if tl["last"] else 512
                j = 1 if tl["first"] else 0
                blocks = []
                while j < 512:
                    nn = min(128, 512 - j)
                    blocks.append((j, nn))
                    j += nn
                if W == 513:
                    blocks.append((512, 1))
                tl["W"], tl["blocks"] = W, blocks
            xs_pend = {}

            def A1(n, b):
                if n >= NTL:
                    return
                tl = tiles[n]
                if b == 0:
                    tl["h"], tl["hb"] = hT.next()
                t0 = tl["toff"] + 512 * tl["i"] + 128 * b
                xs_pend[(n, b)] = self.frontend_a(fe, tl["src"][t0:t0 + 128, :], 128)

            def A2(n, b):
                if n >= NTL:
                    return
                tl = tiles[n]
                xs, xsb = xs_pend.pop((n, b))
                self.frontend_b(fe, xs, xsb, 128, tl["h"], tl["hb"], 128 * b)

            def C(n, bi):
                if n < 0 or bi >= len(tiles[n]["blocks"]):
                    return
                tl = tiles[n]
                a_, a_b = tl["aT"], tl["aTb"]
                j, nn = tl["blocks"][bi]
                tk = tl["toff"] + 512 * tl["i"] - 1 + j
                src, dst = tl["src"], tl["dst"]
                xr, xrb = self.backend_prefetch(be, src[tk:tk + nn, :], nn)
                for half in range(2):
                    for f in range(NFC):
                        fw.op(fw.pe, lambda: nc.tensor.matmul(pm[:nn, half, :], a_[:, f, j:j + nn], wout[:, f, half * 512:(half + 1) * 512],
                                                              start=(f == 0), stop=(f == NFC - 1)),
                              reads=[a_b, woutb], writes=[pmb], inc=(f == NFC - 1 and half == 1))
                self.backend(be, pm, pmb, nn, xr, xrb, dst[tk:tk + nn, :])

            def Bgroup(n, g, wtile):
                tl = tiles[n]
                h, hb = tl["h"], tl["hb"]
                a_, a_b = tl["aT"], tl["aTb"]
                W = tl["W"]
                w_, wb_ = wtile
                for jf in range(2):
                    f = 2 * g + jf
                    P_ = pgu[f % 2]
                    Pb = pgub[f % 2]
                    t3s = []
                    for gu in range(2):
                        cidx = f + NFC * gu
                        col = 256 * gu + 128 * jf
                        for kc in range(8):
                            fw.op(fw.pe, lambda: nc.tensor.matmul(P_[:, gu, :], w_[:, kc, col:col + 128], h[:, kc, :], start=(kc == 0), stop=(kc == 7)),
                                  reads=[wb_, hb], writes=[Pb[gu]], inc=(kc == 7))
                        Pg = P_[:, gu, :]
                        t1, t1b = T1.next()
                        t2, t2b = T2.next()
                        t3, t3b = (TG if gu == 0 else TU).next()
                        sc1, sc0, sc2, bi_ = cw[:, 1, cidx:cidx + 1], cw[:, 0, cidx:cidx + 1], cw[:, 2, cidx:cidx + 1], cb[:, cidx:cidx + 1]
                        fw.op(fw.act, lambda: nc.scalar.activation(out=t1[:, 1:W], in_=Pg[:, 0:W - 1], func=AF.Identity, scale=sc1, bias=bi_),
                              reads=[Pb[gu], cwb], writes=[t1b])
                        fw.op(fw.act, lambda: nc.scalar.activation(out=t1[:, 0:1], in_=carry[:, cidx, 1:2], func=AF.Identity, scale=sc1, bias=bi_),
                              reads=[carryb[cidx], cwb], writes=[t1b])
                        fw.op(fw.dve, lambda: nc.vector.scalar_tensor_tensor(out=t2[:, 2:W], in0=Pg[:, 0:W - 2], scalar=sc0, in1=t1[:, 2:W], op0=ALU.mult, op1=ALU.add),
                              reads=[Pb[gu], t1b, cwb], writes=[t2b])
                        fw.op(fw.dve, lambda: nc.vector.scalar_tensor_tensor(out=t2[:, 0:2], in0=carry[:, cidx, :], scalar=sc0, in1=t1[:, 0:2], op0=ALU.mult, op1=ALU.add),
                              reads=[carryb[cidx], t1b, cwb], writes=[t2b])
                        fw.op(fw.dve, lambda: nc.vector.scalar_tensor_tensor(out=t3[:, 0:512], in0=Pg[:, 0:512], scalar=sc2, in1=t2[:, 0:512], op0=ALU.mult, op1=ALU.add),
                              reads=[Pb[gu], t2b, cwb], writes=[t3b])
                        if W == 513:
                            fw.op(fw.dve, lambda: nc.vector.tensor_copy(out=t3[:, 512:513], in_=t2[:, 512:513]), reads=[t2b], writes=[t3b])
                        fw.op(fw.dve, lambda: nc.vector.tensor_copy(out=carry[:, cidx, :], in_=Pg[:, 510:512]), reads=[Pb[gu]], writes=[carryb[cidx]])
                        t3s.append((t3, t3b))
                    (tg, tgb), (tu, tub) = t3s
                    gg, ggb = GG.next()
                    fw.op(fw.act, lambda: nc.scalar.activation(out=gg[:, 0:W], in_=tg[:, 0:W], func=AF.Gelu_apprx_tanh), reads=[tgb], writes=[ggb])
                    fw.op(fw.pool, lambda: nc.gpsimd.tensor_tensor(out=a_[:, f, 0:W], in0=gg[:, 0:W], in1=tu[:, 0:W], op=ALU.mult),
                          reads=[ggb, tub], writes=[a_b])

            for b in range(4):
                A1(0, b)
                A2(0, b)
            for n, tl in enumerate(tiles):
                tl["aT"], tl["aTb"] = aT.next()
                if tl["first"]:
                    fw.op(fw.pool, lambda: nc.gpsimd.memset(carry[:, :, :], 0.0), writes=carryb)
                wq = []
                for g in range(2):
                    w_, wb_ = wg.next()
                    fw.dma(fw.sp, w_[:], self.WINB[l, g], writes=[wb_])
                    wq.append((w_, wb_))
                for g in range(11):
                    if g + 2 < 11:
                        w_, wb_ = wg.next()
                        fw.dma(fw.sp, w_[:], self.WINB[l, g + 2], writes=[wb_])
                        wq.append((w_, wb_))
                    Bgroup(n, g, wq[g])
                    if SEQ_FFN:
                        if g == 10:
                            for b in range(4):
                                A1(n + 1, b)
                                A2(n + 1, b)
                            for bi in range(5):
                                C(n, bi)
                        continue
                    if g in (0, 2, 4, 6):
                        A1(n + 1, g // 2)
                    if g in (1, 3, 5, 7):
                        C(n - 1, (g - 1) // 2)
                        A2(n + 1, (g - 1) // 2)
                    if g == 8:
                        C(n - 1, 4)
            if not SEQ_FFN:
                for bi in range(5):
                    C(NTL - 1, bi)
            fw.barrier()

    def phase_proj(self, mode, src):
        nc, fw = self.nc, self.fw
        na = (mode == "na")
        with ExitStack() as st:
            c = self.consts(st)
            gfm, gfmb = self.load_gfm(st, "pj_gfm", 0 if na else 1)
            w = self.sb(st, "pj_w", [128, 8, 3 * D], BF16)
            wb = Buf()
            if na:
                for k3 in range(3):
                    fw.dma(fw.pool, w[:, :, k3 * D:(k3 + 1) * D], self.w_qkv[:, k3 * D:(k3 + 1) * D].rearrange("(kc p) n -> p kc n", p=128), writes=[wb])
            else:
                for k3, ww in enumerate((self.da_wq, self.da_wk, self.da_wv)):
                    fw.dma(fw.pool, w[:, :, k3 * D:(k3 + 1) * D], ww.rearrange("(kc p) n -> p kc n", p=128), writes=[wb])
            tp = self.ps(st, "pj_tp", [128, 8, 128])
            tpb = Buf()
            pp = [self.ps(st, f"pj_pp{i}", [128, 2, 512]) for i in range(3)]
            ppb = [[Buf(), Buf()] for _ in range(3)]
            fe = self.make_frontend(st, c, gfm, gfmb, tp, tpb)
            hT = self.ring(st, "pj_hT", 2, [128, 8, 512], BF16)
            if na:
                stq = self.ring(st, "pj_stq", 2, [128, 4, 8, 128], BF16)
                stk = self.ring(st, "pj_stk", 2, [128, 4, 8, 128], BF16)
                vaug = self.ring(st, "pj_vaug", 2, [128, 4, NA_H, 65], BF16)
                for t_, b_ in zip(vaug.tiles, vaug.bufs):
                    fw.op(fw.pool, lambda t_=t_: nc.gpsimd.memset(t_[:, :, :, 64:65], 1.0), writes=[b_])
            else:
                stq = self.ring(st, "pj_stq", 2, [128, 8, 512], BF16)
                stk = self.ring(st, "pj_stk", 2, [128, 8, 512], BF16)
                vaug = self.ring(st, "pj_vaug", 2, [128, DA_H, 4, 129], BF16)
                for t_, b_ in zip(vaug.tiles, vaug.bufs):
                    fw.op(fw.pool, lambda t_=t_: nc.gpsimd.memset(t_[:, :, :, 128:129], 1.0), writes=[b_])
                cosr = self.ring(st, "pj_cos", 2, [128, 512], F32)
                sinr = self.ring(st, "pj_sin", 2, [128, 512], F32)
                rt = [self.ring(st, f"pj_rt{i}", 2, [128, 512], F32) for i in range(4)]
            pi = 0
            for (toff, T) in self.seqs:
                for sbi in range(T // 512):
                    t0 = toff + 512 * sbi
                    blk0 = t0 // 128
                    h, hb = hT.next()
                    for b in range(4):
                        self.frontend(fe, src[t0 + 128 * b:t0 + 128 * b + 128, :], 128, h, hb, 128 * b)
                    if not na:
                        cs, csb = cosr.next()
                        sn, snb = sinr.next()
                        fw.dma(fw.sp, cs[:], self.rot_cos[:, 512 * sbi:512 * sbi + 512], writes=[csb])
                        fw.dma(fw.sp, sn[:], self.rot_sin[:, 512 * sbi:512 * sbi + 512], writes=[snb])
                    for qk in range(2):
                        sg, sgb = (stq if qk == 0 else stk).next()
                        if na:
                            for oc in range(8):
                                P_, Pb = pp[pi % 3], ppb[pi % 3]
                                pi += 1
                                col = qk * D + oc * 128
                                for kc in range(8):
                                    fw.op(fw.pe, lambda kc=kc, col=col, P_=P_: nc.tensor.matmul(P_[:, 0, :], w[:, kc, col:col + 128], h[:, kc, :], start=(kc == 0), stop=(kc == 7)),
                                          reads=[wb, hb], writes=[Pb[0]], inc=(kc == 7))
                                if oc % 2 == 0:
                                    fw.op(fw.act, lambda P_=P_, oc=oc, sg=sg: nc.scalar.activation(out=sg[:, :, oc, :], in_=P_[:, 0, :].rearrange("p (b t) -> p b t", b=4), func=AF.Copy),
                                          reads=[Pb[0]], writes=[sgb])
                                else:
                                    fw.op(fw.dve, lambda P_=P_, oc=oc, sg=sg: nc.vector.tensor_copy(out=sg[:, :, oc, :], in_=P_[:, 0, :].rearrange("p (b t) -> p b t", b=4)),
                                          reads=[Pb[0]], writes=[sgb])
                            dstT = (self.NQT if qk == 0 else self.NKT)
                            fw.dma(fw.sp, dstT[blk0:blk0 + 4].rearrange("b p o t -> p b o t"), sg[:], reads=[sgb])
                        else:
                            for m in range(4):
                                P_, Pb = pp[pi % 3], ppb[pi % 3]
                                pi += 1
                                for ab in range(2):
                                    col = qk * D + m * 256 + ab * 128
                                    for kc in range(8):
                                        fw.op(fw.pe, lambda kc=kc, col=col, P_=P_, ab=ab: nc.tensor.matmul(P_[:, ab, :], w[:, kc, col:col + 128], h[:, kc, :], start=(kc == 0), stop=(kc == 7)),
                                              reads=[wb, hb], writes=[Pb[ab]], inc=(kc == 7))
                                t1, t1b = rt[0].next()
                                t2, t2b = rt[1].next()
                                t3, t3b = rt[2].next()
                                t4, t4b = rt[3].next()
                                TT = nc.vector.tensor_tensor
                                fw.op(fw.dve, lambda P_=P_, t1=t1: TT(out=t1[:], in0=P_[:, 0, :], in1=cs[:], op=ALU.mult), reads=[Pb[0], csb], writes=[t1b])
                                fw.op(fw.dve, lambda P_=P_, t2=t2: TT(out=t2[:], in0=P_[:, 1, :], in1=sn[:], op=ALU.mult), reads=[Pb[1], snb], writes=[t2b])
                                fw.op(fw.dve, lambda P_=P_, t3=t3: TT(out=t3[:], in0=P_[:, 1, :], in1=cs[:], op=ALU.mult), reads=[Pb[1], csb], writes=[t3b])
                                fw.op(fw.dve, lambda P_=P_, t4=t4: TT(out=t4[:], in0=P_[:, 0, :], in1=sn[:], op=ALU.mult), reads=[Pb[0], snb], writes=[t4b])
                                fw.op(fw.pool, lambda t1=t1, t2=t2, m=m, sg=sg: nc.gpsimd.tensor_tensor(out=sg[:, 2 * m, :], in0=t1[:], in1=t2[:], op=ALU.subtract),
                                      reads=[t1b, t2b], writes=[sgb])
                                fw.op(fw.pool, lambda t3=t3, t4=t4, m=m, sg=sg: nc.gpsimd.tensor_tensor(out=sg[:, 2 * m + 1, :], in0=t3[:], in1=t4[:], op=ALU.add),
                                      reads=[t3b, t4b], writes=[sgb])
                            dstT = (self.DQT if qk == 0 else self.DKT)
                            fw.dma(fw.sp, dstT[:, :, t0:t0 + 512].rearrange("o p t -> p o t"), sg[:], reads=[sgb])
                    va, vab = vaug.next()
                    for b in range(4):
                        P_, Pb = pp[pi % 3], ppb[pi % 3]
                        pi += 1
                        for half in range(2):
                            col = 2 * D + half * 512
                            for kc in range(8):
                                fw.op(fw.pe, lambda kc=kc, col=col, P_=P_, half=half, b=b: nc.tensor.matmul(P_[:, half, :], h[:, kc, 128 * b:128 * b + 128], w[:, kc, col:col + 512],
                                                                                                            start=(kc == 0), stop=(kc == 7)),
                                      reads=[wb, hb], writes=[Pb[half]], inc=(kc == 7))
                            if na:
                                o_ap = va[:, b, 8 * half:8 * half + 8, 0:64]
                                i_ap = P_[:, half, :].rearrange("p (h d) -> p h d", h=8)
                            else:
                                o_ap = va[:, 4 * half:4 * half + 4, b, 0:128]
                                i_ap = P_[:, half, :].rearrange("p (h d) -> p h d", h=4)
                            if half == 0:
                                fw.op(fw.act, lambda o_ap=o_ap, i_ap=i_ap: nc.scalar.activation(out=o_ap, in_=i_ap, func=AF.Copy), reads=[Pb[half]], writes=[vab])
                            else:
                                fw.op(fw.dve, lambda o_ap=o_ap, i_ap=i_ap: nc.vector.tensor_copy(out=o_ap, in_=i_ap), reads=[Pb[half]], writes=[vab])
                    if na:
                        fw.dma(fw.sp, self.NV[:, blk0:blk0 + 4, :], va[:].rearrange("p b h d -> p b (h d)"), reads=[vab])
                    else:
                        fw.dma(fw.sp, self.DV[:, :, blk0:blk0 + 4, :].rearrange("h p b x -> p h b x"), va[:], reads=[vab])
            fw.barrier()

    @staticmethod
    def na_kc(j, NB):
        R = 2 * NB
        r0a = min(max(2 * j - 4, 0), R - 8)
        r0b = min(max(2 * j + 1 - 4, 0), R - 8)
        return list(range(r0a // 2, (r0b + 7) // 2 + 1))

    def phase_na2(self, dst):
        nc, fw = self.nc, self.fw
        with ExitStack() as st:
            c = self.consts(st)
            gbc, gbcb = self.load_gbc(st, "na_gbc", 2)
            wo = self.sb(st, "na_wo", [128, 8, D], BF16)
            wob = Buf()
            ORDER = [0, 2, 4, 6, 8, 10, 12, 14, 1, 3, 5, 7, 9, 11, 13, 15]
            for pos, hh_ in enumerate(ORDER):
                fw.dma(fw.pool, wo[64 * (pos % 2):64 * (pos % 2) + 64, pos // 2, :], self.na_wo[hh_ * 64:hh_ * 64 + 64, :], writes=[wob])
            bint = self.sb(st, "na_bint", [128, NA_H, 640], F32)
            bintb = Buf()
            for h0 in range(0, NA_H, 4):
                fw.dma(fw.sp, bint[:, h0:h0 + 4, :], self.bias_int[h0:h0 + 4].rearrange("h p n -> p h n"), writes=[bintb])
            bsp = self.ring(st, "na_bsp", 3, [128, 512], F32)
            kring = self.ring(st, "na_k", 10, [128, 8, 128], BF16)
            vring = self.ring(st, "na_v", 10, [128, NA_H * 65], BF16)
            qr = self.ring(st, "na_q", 3, [128, 8, 128], BF16)
            ssb_r = self.ring(st, "na_ssb", 3, [128, 640], F32)
            pT = self.ring(st, "na_pT", 4, [128, 640], BF16)
            on_r = self.ring(st, "na_on", 2, [128, NA_H, 64], BF16)
            oT_r = self.ring(st, "na_oT", 2, [128, 8, 128], BF16)
            rec_r = self.ring(st, "na_rec", 4, [128, 8], F32)
            be = self.make_backend(st, c, gbc, gbcb)
            Sall = self.ps(st, "na_S", [128, 4, 512])
            Sflat = Sall[:, :, :].rearrange("p a b -> p (a b)")
            Sb = [Buf(), Buf()]
            NREG, RSTR = 2, 1024
            O = [self.ps(st, f"na_O{i}", [128, 512]) for i in range(2)]
            Ob = [Buf(), Buf()]
            M = self.ps(st, "na_M", [128, 2, 512])
            Mb = Buf()
            Mbf = M[:, 0, :].bitcast(BF16)
            GROUPS = ((0, 7, 0), (7, 14, 1), (14, 16, 0))
            units = []
            for si_, (toff, T) in enumerate(self.seqs):
                for j in range(T // 128):
                    for h in range(NA_H):
                        units.append((si_, j, h))
            blocks = {}
            seqstate = {}
            pend = []

            def block_setup(si_, j):
                toff, T = self.seqs[si_]
                NB = T // 128
                blkoff = toff // 128
                if j == 0:
                    seqstate[si_] = {"loaded": -1, "kslot": {}}
                ss_ = seqstate[si_]
                kcs = self.na_kc(j, NB)
                upto = min(max(kcs) + 1, NB - 1)
                while ss_["loaded"] < upto:
                    ss_["loaded"] += 1
                    kt, ktb = kring.next()
                    vt, vtb = vring.next()
                    fw.dma(fw.sp, kt[:], self.NKT[blkoff + ss_["loaded"]], writes=[ktb])
                    fw.dma(fw.sp, vt[:], self.NV[:, blkoff + ss_["loaded"], :], writes=[vtb])
                    ss_["kslot"][ss_["loaded"]] = (kt, ktb, vt, vtb)
                if j < 2:
                    sp_idx = j
                elif j >= NB - 2:
                    sp_idx = 2 + (j - (NB - 2))
                else:
                    sp_idx = None
                q, qb = qr.next()
                fw.dma(fw.sp, q[:], self.NQT[blkoff + j], writes=[qb])
                tk = toff + 128 * j
                xr, xrb = self.backend_prefetch(be, self.x[tk:tk + 128, :], 128)
                on, onb = on_r.next()
                blocks[(si_, j)] = dict(kcs=kcs, sp_idx=sp_idx, q=q, qb=qb, tk=tk, xr=xr, xrb=xrb, on=on, onb=onb,
                                        ks=[ss_["kslot"][kc] for kc in kcs])

            def emit_qk(ui):
                si_, j, pos = units[ui]
                h = ORDER[pos]
                if pos == 0:
                    block_setup(si_, j)
                B_ = blocks[(si_, j)]
                oc, r0 = h // 2, (h % 2) * 64
                reg = ui % NREG
                nt = len(B_["kcs"])
                for i in range(nt):
                    kt, ktb, vt, vtb = B_["ks"][i]
                    fw.op(fw.pe, lambda: nc.tensor.matmul(Sflat[:, RSTR * reg + i * 128:RSTR * reg + (i + 1) * 128], kt[r0:r0 + 64, oc, :], B_["q"][r0:r0 + 64, oc, :],
                                                          start=True, stop=True),
                          reads=[ktb, B_["qb"]], writes=[Sb[reg]], inc=(i == nt - 1))

            def emit_rest(ui):
                si_, j, pos = units[ui]
                h = ORDER[pos]
                B_ = blocks[(si_, j)]
                reg = ui % NREG
                nt = len(B_["kcs"])
                Sf = Sflat[:, RSTR * reg:RSTR * reg + nt * 128]
                if B_["sp_idx"] is None:
                    b_ap, b_b = bint[:, h, 0:nt * 128], bintb
                else:
                    bt, btb = bsp.next()
                    fw.dma(fw.sp, bt[:], self.bias_sp[B_["sp_idx"], h], writes=[btb])
                    b_ap, b_b = bt[:, 0:nt * 128], btb
                ss_, ss_b = ssb_r.next()
                fw.op(fw.dve, lambda: nc.vector.scalar_tensor_tensor(out=ss_[:, 0:nt * 128], in0=Sf, scalar=0.125, in1=b_ap, op0=ALU.mult, op1=ALU.add),
                      reads=[Sb[reg], b_b], writes=[ss_b])
                p_, p_b = pT.next()
                fw.op(fw.act, lambda: nc.scalar.activation(out=p_[:, 0:nt * 128], in_=ss_[:, 0:nt * 128], func=AF.Exp), reads=[ss_b], writes=[p_b])
                pts[ui] = (p_, p_b)

            def emit_pv(ui):
                si_, j, pos = units[ui]
                h = ORDER[pos]
                B_ = blocks[(si_, j)]
                nt = len(B_["kcs"])
                p_, p_b = pts.pop(ui)
                hg0, hg1, ob_i = [g for g in GROUPS if g[0] <= pos < g[1]][0]
                Ot, Otb = O[ob_i], Ob[ob_i]
                sl = pos - hg0
                for i in range(nt):
                    kt, ktb, vt, vtb = B_["ks"][i]
                    fw.op(fw.pe, lambda: nc.tensor.matmul(Ot[:, sl * 65:(sl + 1) * 65], p_[:, i * 128:(i + 1) * 128], vt[:, h * 65:(h + 1) * 65],
                                                          start=(i == 0), stop=(i == nt - 1)),
                          reads=[p_b, vtb], writes=[Otb], inc=(i == nt - 1))
                if pos == hg1 - 1:
                    nh_ = hg1 - hg0
                    rec, recb = rec_r.next()
                    on, onb = B_["on"], B_["onb"]
                    Ov = Ot[:, 0:nh_ * 65].rearrange("p (h d) -> p h d", d=65)
                    fw.op(fw.dve, lambda: nc.vector.reciprocal(out=rec[:, 0:nh_], in_=Ov[:, :, 64]), reads=[Otb], writes=[recb])
                    fw.op(fw.dve, lambda: nc.vector.tensor_tensor(out=on[:, hg0:hg1, :], in0=Ov[:, :, 0:64],
                                                                  in1=rec[:, 0:nh_].unsqueeze(2).to_broadcast([128, nh_, 64]), op=ALU.mult),
                          reads=[Otb, recb], writes=[onb])
                if pos == NA_H - 1:
                    def tail(B_=B_):
                        on, onb = B_["on"], B_["onb"]
                        onf = on[:, :, :].rearrange("p h d -> p (h d)")
                        for k in range(8):
                            fw.op(fw.pe, lambda: nc.tensor.transpose(out=Mbf[:, k * 128:(k + 1) * 128], in_=onf[:, k * 128:(k + 1) * 128], identity=c["idb"][:]),
                                  reads=[onb, c["B"]], writes=[Mb], inc=(k == 7))
                        oT, oTb = oT_r.next()
                        fw.op(fw.dve, lambda: nc.vector.tensor_copy(out=oT[:, :, :].rearrange("p a b -> p (a b)"), in_=Mbf[:, :]), reads=[Mb], writes=[oTb])
                        for half in range(2):
                            for k in range(8):
                                fw.op(fw.pe, lambda: nc.tensor.matmul(M[:, half, :], oT[:, k, :], wo[:, k, half * 512:(half + 1) * 512], start=(k == 0), stop=(k == 7)),
                                      reads=[oTb, wob], writes=[Mb], inc=(k == 7 and half == 1))
                        tk = B_["tk"]
                        self.backend(be, M, Mb, 128, B_["xr"], B_["xrb"], dst[tk:tk + 128, :])
                    pend.append(tail)
                if pos == 3:
                    while pend:
                        pend.pop(0)()

            pts = {}
            for ui in range(len(units) + 2):
                if ui < len(units):
                    emit_qk(ui)
                if 0 <= ui - 1 < len(units):
                    emit_rest(ui - 1)
                if ui - 2 >= 0:
                    emit_pv(ui - 2)
            while pend:
                pend.pop(0)()
            fw.barrier()

    def phase_select(self):
        nc, fw = self.nc, self.fw
        OWN, OFF1, tp0 = self.OWN, self.OFF1, self.TP_off
        with ExitStack() as st:
            f = self.sb(st, "sel_f", [128, 2], F32)
            fb = Buf()
            fw.dma(fw.sp, f[:], self.fsel[:, :], writes=[fb])
            qa = self.ring(st, "sel_qa", 2, [128, OWN], BF16)
            qb_ = self.ring(st, "sel_qb", 2, [128, OWN], BF16)
            qt = self.ring(st, "sel_qt", 2, [128, OWN], F32)
            qo = self.ring(st, "sel_qo", 2, [128, OWN], BF16)
            for oc in range(8):
                a, ab = qa.next()
                b, bb = qb_.next()
                t, tb = qt.next()
                o, ob = qo.next()
                fw.dma(fw.sp, a[:], self.DQT[oc, :, tp0:tp0 + OWN], writes=[ab])
                fw.dma(fw.sp, b[:], self.DQT[oc, :, tp0 + OFF1:tp0 + OFF1 + OWN], writes=[bb])
                fw.op(fw.dve, lambda: nc.vector.tensor_scalar(out=t[:], in0=a[:], scalar1=f[:, 0:1], scalar2=None, op0=ALU.mult), reads=[ab, fb], writes=[tb])
                fw.op(fw.dve, lambda: nc.vector.scalar_tensor_tensor(out=o[:], in0=b[:], scalar=f[:, 1:2], in1=t[:], op0=ALU.mult, op1=ALU.add), reads=[bb, tb, fb], writes=[ob])
                fw.dma(fw.sp, self.DQTo[oc, :, :], o[:], reads=[ob])
            xa = self.ring(st, "sel_xa", 2, [128, D], F32)
            xb = self.ring(st, "sel_xb", 2, [128, D], F32)
            xt = self.ring(st, "sel_xt", 2, [128, D], F32)
            xo = self.ring(st, "sel_xo", 2, [128, D], F32)
            for k in range(OWN // 128):
                a, ab = xa.next()
                b, bb = xb.next()
                t, tb = xt.next()
                o, ob = xo.next()
                fw.dma(fw.sp, a[:], self.XB[tp0 + 128 * k:tp0 + 128 * k + 128, :], writes=[ab])
                fw.dma(fw.sp, b[:], self.XB[tp0 + OFF1 + 128 * k:tp0 + OFF1 + 128 * k + 128, :], writes=[bb])
                fw.op(fw.dve, lambda: nc.vector.tensor_scalar(out=t[:], in0=a[:], scalar1=f[:, 0:1], scalar2=None, op0=ALU.mult), reads=[ab, fb], writes=[tb])
                fw.op(fw.dve, lambda: nc.vector.scalar_tensor_tensor(out=o[:], in0=b[:], scalar=f[:, 1:2], in1=t[:], op0=ALU.mult, op1=ALU.add), reads=[bb, tb, fb], writes=[ob])
                fw.dma(fw.sp, self.XBo[128 * k:128 * k + 128, :], o[:], reads=[ob])
            fw.barrier()

    def phase_da2(self):
        nc, fw = self.nc, self.fw
        with ExitStack() as st:
            c = self.consts(st)
            lam = self.sb(st, "da_lam", [128, 4, 64], F32)
            lamb = Buf()
            for i in range(4):
                fw.dma(fw.sp, lam[:, i, :], self.lam_in[i, :].partition_broadcast(128), writes=[lamb])
            lj = self.sb(st, "da_lj", [128, 64], F32)
            lv = self.sb(st, "da_lv", [128, 8], F32)
            lvb = Buf()
            for i in range(2):
                fw.op(fw.dve, lambda i=i: nc.vector.scalar_tensor_tensor(out=lj[:], in0=lam[:, 2 * i, :], scalar=1.0, in1=lam[:, 2 * i + 1, :], op0=ALU.mult, op1=ALU.mult,
                                                                       accum_out=lv[:, i:i + 1]), reads=[lamb], writes=[lvb])
            fw.op(fw.act, lambda: nc.scalar.activation(out=lv[:, 2:4], in_=lv[:, 0:2], func=AF.Exp), reads=[lvb], writes=[lvb])
            fw.op(fw.dve, lambda: nc.vector.scalar_tensor_tensor(out=lv[:, 4:5], in0=lv[:, 3:4], scalar=-LAMBDA_INIT, in1=lv[:, 2:3], op0=ALU.add, op1=ALU.subtract),
                  reads=[lvb], writes=[lvb])
            gsub = self.sb(st, "da_gsub", [128, 128], F32)
            gsubb = Buf()
            fw.dma(fw.sp, gsub[:], self.subln_g.partition_broadcast(128), writes=[gsubb])
            fw.op(fw.dve, lambda: nc.vector.tensor_scalar(out=gsub[:], in0=gsub[:], scalar1=(1.0 - LAMBDA_INIT), scalar2=None, op0=ALU.mult), reads=[gsubb], writes=[gsubb])

            TM = self.TMAX
            NCM = TM // 128
            KT = self.ring(st, "da_KT", 2, [128, TM], BF16)
            QT = self.ring(st, "da_QT", 2, [128, TM], BF16)
            VH = self.ring(st, "da_VH", 2, [128, NCM, 129], BF16)
            E = self.ring(st, "da_E", 3, [128, 2, 512], BF16)
            accs = self.ring(st, "da_accs", 2, [128, 8, 129], F32)
            rec_r = self.ring(st, "da_rec", 2, [128, 8], F32)
            o1_r = self.ring(st, "da_o1", 2, [128, 128], F32)
            o_r = self.ring(st, "da_o", 2, [128, 128], F32)
            oj = self.sb(st, "da_oj", [128, 128], F32)
            ojb = Buf()
            ss_r = self.ring(st, "da_ss", 4, [128, 4], F32)
            on_r = self.ring(st, "da_on", 8, [128, 128], BF16)
            oT_r = self.ring(st, "da_oT", 2, [128, 512], BF16)
            S = [self.ps(st, f"da_S{i}", [128, 2, 512]) for i in range(2)]
            Sb = [Buf(), Buf()]
            A = [self.ps(st, f"da_A{i}", [128, 512]) for i in range(3)]
            Ab = [Buf(), Buf(), Buf()]
            TPp = self.ps(st, "da_TP", [128, 512])
            TPb = Buf()
            TPbf = TPp[:, :].bitcast(BF16)
            pend = []

            def flush_pend():
                while pend:
                    pend.pop(0)()

            for (own, toff, T, TQ) in ((False, self.TS_off, self.TS, self.TS), (True, self.TP_off, self.TP, self.OWN)):
                NC_ = T // 128
                blkoff = toff // 128
                for h in range(DA_H):
                    kt, ktb = KT.next()
                    qt_, qtb = QT.next()
                    vh, vhb = VH.next()
                    m, hh = h // 2, h % 2
                    for i in range(2):
                        sl = 2 * hh + i
                        for ab in range(2):
                            prt = i * 64 + ab * 32
                            fw.dma(fw.sp, kt[prt:prt + 32, 0:T], self.DKT[2 * m + ab, sl * 32:sl * 32 + 32, toff:toff + T], writes=[ktb])
                            if own:
                                fw.dma(fw.sp, qt_[prt:prt + 32, 0:TQ], self.DQTo[2 * m + ab, sl * 32:sl * 32 + 32, 0:TQ], writes=[qtb])
                            else:
                                fw.dma(fw.sp, qt_[prt:prt + 32, 0:TQ], self.DQT[2 * m + ab, sl * 32:sl * 32 + 32, toff:toff + TQ], writes=[qtb])
                    fw.dma(fw.sp, vh[:, 0:NC_, :], self.DV[h, :, blkoff:blkoff + NC_, :], writes=[vhb])
                    for qi in range(TQ // 512):
                        q0 = qi * 512

                        def qk(kb):
                            S_, S_b = S[kb % 2], Sb[kb % 2]
                            for i in range(2):
                                fw.op(fw.pe, lambda: nc.tensor.matmul(S_[:, i, :], kt[64 * i:64 * i + 64, kb * 128:(kb + 1) * 128], qt_[64 * i:64 * i + 64, q0:q0 + 512],
                                                                      start=True, stop=True),
                                      reads=[ktb, qtb], writes=[S_b], inc=(i == 1))

                        qk(0)
                        for kb in range(NC_):
                            if kb + 1 < NC_:
                                qk(kb + 1)
                            S_, S_b = S[kb % 2], Sb[kb % 2]
                            e, eb = E.next()
                            fw.op(fw.act, lambda: nc.scalar.activation(out=e[:, :, :], in_=S_[:, :, :], func=AF.Exp, scale=0.125), reads=[S_b], writes=[eb])
                            for a in range(8):
                                i, s_ = a // 4, a % 4
                                bk, slot = a // 3, a % 3
                                fw.op(fw.pe, lambda: nc.tensor.matmul(A[bk][:, 160 * slot:160 * slot + 129], e[:, i, 128 * s_:128 * s_ + 128], vh[:, kb, :],
                                                                      start=(kb == 0 and slot == 0), stop=(kb == NC_ - 1), skip_group_check=True),
                                      reads=[eb, vhb], writes=[Ab[bk]], inc=(a == 7))
                            if kb == 3:
                                flush_pend()
                        flush_pend()
                        ac, acb = accs.next()
                        for bk in range(3):
                            na_ = min(3, 8 - 3 * bk)
                            fw.op(fw.dve, lambda: nc.vector.tensor_copy(out=ac[:, 3 * bk:3 * bk + na_, :],
                                                                        in_=A[bk][:, 0:160 * na_].rearrange("p (s x) -> p s x", x=160)[:, :, 0:129]),
                                  reads=[Ab[bk]], writes=[acb])
                        rec, recb = rec_r.next()
                        fw.op(fw.dve, lambda: nc.vector.reciprocal(out=rec[:, :], in_=ac[:, :, 128]), reads=[acb], writes=[recb])
                        fw.op(fw.dve, lambda: nc.vector.tensor_scalar(out=rec[:, 4:8], in0=rec[:, 4:8], scalar1=lv[:, 4:5], scalar2=None, op0=ALU.mult),
                              reads=[recb, lvb], writes=[recb])
                        oT, oTb = oT_r.next()
                        ons = []
                        for s_ in range(4):
                            o1, o1b = o1_r.next()
                            o_, o_b = o_r.next()
                            ss, ssb = ss_r.next()
                            on, onb = on_r.next()
                            fw.op(fw.dve, lambda: nc.vector.tensor_scalar(out=o1[:], in0=ac[:, s_, 0:128], scalar1=rec[:, s_:s_ + 1], scalar2=None, op0=ALU.mult),
                                  reads=[acb, recb], writes=[o1b])
                            fw.op(fw.dve, lambda: nc.vector.scalar_tensor_tensor(out=o_[:], in0=ac[:, 4 + s_, 0:128], scalar=rec[:, 4 + s_:5 + s_], in1=o1[:], op0=ALU.mult, op1=ALU.add),
                                  reads=[acb, recb, o1b], writes=[o_b])
                            fw.op(fw.dve, lambda: nc.vector.scalar_tensor_tensor(out=oj[:], in0=o_[:], scalar=1.0, in1=o_[:], op0=ALU.mult, op1=ALU.mult, accum_out=ss[:, 0:1]),
                                  reads=[o_b], writes=[ojb, ssb])
                            self.rstd(ss, ssb, 128, 1.0 / 128, SUBLN_EPS, c)
                            fw.op(fw.dve, lambda: nc.vector.scalar_tensor_tensor(out=on[:], in0=o_[:], scalar=ss[:, 2:3], in1=gsub[:], op0=ALU.mult, op1=ALU.mult),
                                  reads=[o_b, ssb, gsubb], writes=[onb])
                            ons.append((on, onb))

                        def tail(ons=ons, oT=oT, oTb=oTb, q0=q0, h=h, toff=toff, own=own):
                            for s_, (on, onb) in enumerate(ons):
                                fw.op(fw.pe, lambda: nc.tensor.transpose(out=TPbf[:, 128 * s_:128 * s_ + 128], in_=on[:], identity=c["idb"][:]),
                                      reads=[onb, c["B"]], writes=[TPb], inc=(s_ == 3))
                            fw.op(fw.dve, lambda: nc.vector.tensor_copy(out=oT[:, :], in_=TPbf[:, 0:512]), reads=[TPb], writes=[oTb])
                            if own:
                                fw.dma(fw.sp, self.DOTo[h, :, q0:q0 + 512], oT[:], reads=[oTb])
                            else:
                                fw.dma(fw.sp, self.DOT[h, :, toff + q0:toff + q0 + 512], oT[:], reads=[oTb])
                        pend.append(tail)
            flush_pend()
            fw.barrier()

    def phase_da3(self):
        nc, fw = self.nc, self.fw
        with ExitStack() as st:
            c = self.consts(st)
            gbc, gbcb = self.load_gbc(st, "d3_gbc", 3)
            wo = self.sb(st, "d3_wo", [128, 8, D], BF16)
            wob = Buf()
            fw.dma(fw.pool, wo[:], self.da_wo.rearrange("(kc p) n -> p kc n", p=128), writes=[wob])
            be = self.make_backend(st, c, gbc, gbcb)
            oT = self.ring(st, "d3_oT", 2, [128, 8, 512], BF16)
            M = [self.ps(st, f"d3_M{i}", [128, 2, 512]) for i in range(2)]
            Mb = [Buf(), Buf()]
            mi = 0
            for (dot, src, dst, T) in ((self.DOT[:, :, self.TS_off:self.TS_off + self.TS], self.XB[self.TS_off:self.TS_off + self.TS, :], self.XC[self.TS_off:self.TS_off + self.TS, :], self.TS),
                                       (self.DOTo, self.XBo, self.XCo, self.OWN)):
                for sbi in range(T // 512):
                    t0 = 512 * sbi
                    o, ob = oT.next()
                    fw.dma(fw.sp, o[:], dot[:, :, t0:t0 + 512].rearrange("h p t -> p h t"), writes=[ob])
                    for b in range(4):
                        tk = t0 + 128 * b
                        xr, xrb = self.backend_prefetch(be, src[tk:tk + 128, :], 128)
                        M_, M_b = M[mi % 2], Mb[mi % 2]
                        mi += 1
                        for half in range(2):
                            for k in range(8):
                                fw.op(fw.pe, lambda k=k, half=half, M_=M_, b=b, o=o: nc.tensor.matmul(M_[:, half, :], o[:, k, 128 * b:128 * b + 128], wo[:, k, half * 512:(half + 1) * 512],
                                                                                                      start=(k == 0), stop=(k == 7)),
                                      reads=[ob, wob], writes=[M_b], inc=(k == 7 and half == 1))
                        self.backend(be, M_, M_b, 128, xr, xrb, dst[tk:tk + 128, :])
            fw.barrier()

    def build(self, phases="all"):
        fw = self.fw
        P = phases
        if P == "all" or "p0" in P:
            self.phase_p0()
        if P == "all" or "na1" in P:
            self.phase_proj("na", self.x)
        if P == "all" or "na2" in P:
            self.phase_na2(self.XA)
        if P == "all" or "ffn0" in P:
            self.phase_ffn(0, [(self.XA[o:o + T, :], self.XB[o:o + T, :], T) for (o, T) in self.seqs])
        if P == "all" or "da1" in P:
            self.phase_proj("da", self.XB)
        if P == "all" or "sel" in P:
            self.phase_select()
        if P == "all" or "da2" in P:
            self.phase_da2()
        if P == "all" or "da3" in P:
            self.phase_da3()
        if P == "all" or "ffn1" in P:
            self.phase_ffn(1, [(self.XC[self.TS_off:self.TS_off + self.TS, :], self.y[0:self.TS, :], self.TS),
                               (self.XCo, self.y[self.TS:self.TS + self.OWN, :], self.OWN)])
        fw.finish()
        return self.nc


def _na_bias_tables(rpb):
    R = 16
    NB = R // 2
    W = 64

    def table(j):
        kcs = Builder.na_kc(j, NB)
        out = np.full((NA_H, 128, len(kcs), 128), NEG, np.float32)
        rq = 2 * j + np.arange(128) // 64
        cq = np.arange(128) % 64
        r0 = np.clip(rq - 4, 0, R - 8)
        c0 = np.clip(cq - 8, 0, W - 16)
        for i, kc in enumerate(kcs):
            rk = 2 * kc + np.arange(128) // 64
            ck = np.arange(128) % 64
            valid = ((rk[:, None] >= r0[None, :]) & (rk[:, None] < r0[None, :] + 8) &
                     (ck[:, None] >= c0[None, :]) & (ck[:, None] < c0[None, :] + 16))
            dr = np.clip(rk[:, None] - rq[None, :] + 7, 0, 14)
            dc = np.clip(ck[:, None] - cq[None, :] + 15, 0, 30)
            vals = rpb[:, dr, dc]
            out[:, :, i, :] = np.where(valid[None], vals, np.float32(NEG))
        return out.reshape(NA_H, 128, len(kcs) * 128)

    interior = table(3)
    assert interior.shape[2] == 640
    sp = np.stack([table(0), table(1), table(NB - 2), table(NB - 1)], 0)
    assert sp.shape[3] == 512
    return np.ascontiguousarray(interior), np.ascontiguousarray(sp)


def _rot_tables(T):
    inv = (1.0 / (ROPE_THETA ** (np.arange(0, 64, 2, dtype=np.float32) / np.float32(64)))).astype(np.float32)
    ang = np.arange(T, dtype=np.float32)[None, :] * inv[np.arange(128) % 32][:, None]
    return np.cos(ang).astype(np.float32), np.sin(ang).astype(np.float32)


def _perm_cols():
    cols = []
    for m in range(4):
        for ab in range(2):
            for sl in range(4):
                for dl in range(32):
                    cols.append((4 * m + sl) * 64 + ab * 32 + dl)
    return np.array(cols)


def prepare_shared(inp, TMAX):
    f = lambda a: np.ascontiguousarray(np.asarray(a, dtype=np.float32))
    perm = _perm_cols()
    bi, bsp = _na_bias_tables(f(inp["na_rpb"])[0])
    cos, sin = _rot_tables(TMAX)
    sh = {
        "gvec": f(np.concatenate([inp["attn_pre_g"], inp["attn_post_g"], inp["ffn_pre_g"], inp["ffn_post_g"]], 0)),
        "w_qkv": f(inp["na_w_qkv"][0]), "na_wo": f(inp["na_w_o"][0]),
        "da_wq": f(np.asarray(inp["da_w_q"][0])[:, perm]), "da_wk": f(np.asarray(inp["da_w_k"][0])[:, perm]),
        "da_wv": f(inp["da_w_v"][0]), "da_wo": f(inp["da_w_o"][0]),
        "w_in": f(inp["ffn_w_in"]), "w_out": f(inp["ffn_w_out"]),
        "conv_w": f(inp["ffn_conv_w"]), "conv_b": f(inp["ffn_conv_b"]),
        "bias_int": bi, "bias_sp": bsp, "rot_cos": cos, "rot_sin": sin,
        "lam_in": f(np.stack([inp["da_lambda_q1"][0], inp["da_lambda_k1"][0], inp["da_lambda_q2"][0], inp["da_lambda_k2"][0]], 0)),
        "subln_g": f(inp["da_subln_g"][0]), "ident": np.eye(128, dtype=np.float32),
    }
    return sh


def kernel(**inp):
    xs = np.asarray(inp["x_sample"], dtype=np.float32)
    xp = np.asarray(inp["x_prompt"], dtype=np.float32)
    NS, TS, _ = xs.shape
    NP_, TP, _ = xp.shape
    ncore = 8
    assert NS == ncore and NP_ * 2 == ncore
    seqs = [(0, TS), (TS, TP)]
    bld = Builder(seqs)
    nc = bld.build()
    sh = prepare_shared(inp, max(TS, TP))
    in_maps = []
    for cidx in range(ncore):
        m = dict(sh)
        m["x"] = np.ascontiguousarray(np.concatenate([xs[cidx], xp[cidx // 2]], 0))
        m["fsel"] = np.ascontiguousarray(np.tile(np.array([[1.0, 0.0]] if cidx % 2 == 0 else [[0.0, 1.0]], np.float32), (128, 1)))
        in_maps.append(m)
    res = run_bass_kernel_spmd(nc, in_maps, core_ids=list(range(ncore)))
    y_s = np.stack([res.results[cidx]["y"][:TS] for cidx in range(ncore)], 0)
    H2 = TP // 2
    y_p = np.stack([np.concatenate([res.results[2 * p]["y"][TS:TS + H2], res.results[2 * p + 1]["y"][TS + 512:TS + 512 + H2]], 0) for p in range(NP_)], 0)
    return (y_p.astype(np.float32), y_s.astype(np.float32))
```

```python
import math
from contextlib import ExitStack
import numpy as np
import ml_dtypes
import concourse.bass as bass
import concourse.mybir as mybir
from concourse.bass_utils import run_bass_kernel_spmd

F32 = mybir.dt.float32
BF16 = mybir.dt.bfloat16
AF = mybir.ActivationFunctionType
ALU = mybir.AluOpType

D = 1024
DFF = 2816
NFC = 22
NA_H = 16
DA_H = 8
NEG = -30000.0
NORM_EPS = 1e-6
SUBLN_EPS = 1e-5
LAMBDA_INIT = 0.8 - 0.6 * math.exp(-0.3 * 1)
ROPE_THETA = 10000.0
import os as _os
SEQ_FFN = bool(int(_os.environ.get('SEQ_FFN', '0')))


class Buf:
    __slots__ = ("name", "w", "r")

    def __init__(self, name=""):
        self.name = name
        self.w = None
        self.r = {}


class Eng:
    def __init__(self, nc, h, name):
        self.h = h
        self.name = name
        self.sem = nc.alloc_semaphore(name="s_" + name)
        self.count = 0
        self.known = {}


class FW:
    NDMASEM = 48

    def __init__(self, nc):
        self.nc = nc
        self.pe = Eng(nc, nc.tensor, "pe")
        self.act = Eng(nc, nc.scalar, "act")
        self.dve = Eng(nc, nc.vector, "dve")
        self.pool = Eng(nc, nc.gpsimd, "pool")
        self.sp = Eng(nc, nc.sync, "sp")
        self.engs = [self.pe, self.act, self.dve, self.pool, self.sp]
        self.dsems = [nc.alloc_semaphore(name=f"d{i}") for i in range(self.NDMASEM)]
        self.dval = [0] * self.NDMASEM
        self.dnext = 0
        self.nops = 0
        self.ndma = 0
        self.rec = None

    def wait(self, eng, tok):
        if tok is None:
            return
        sem, val = tok
        if sem is eng.sem:
            if eng is self.pe:
                return
            if val > eng.count:
                return
        k = id(sem)
        if eng.known.get(k, 0) >= val:
            return
        eng.h.wait_ge(sem, val)
        eng.known[k] = val
        if self.rec is not None:
            self.rec[eng.name].append(("w", k, val))

    def _deps(self, eng, reads, writes):
        for b in reads:
            self.wait(eng, b.w)
        for b in writes:
            self.wait(eng, b.w)
            for tok in list(b.r.values()):
                self.wait(eng, tok)

    def _commit(self, tok, reads, writes):
        k = id(tok[0])
        for b in reads:
            if k not in b.r or b.r[k][1] < tok[1]:
                b.r[k] = tok
        for b in writes:
            b.w = tok
            b.r = {}

    def op(self, eng, fn, reads=(), writes=(), inc=True):
        self._deps(eng, reads, writes)
        ins = fn()
        self.nops += 1
        if inc:
            eng.count += 1
            ins.then_inc(eng.sem, 1)
            tok = (eng.sem, eng.count)
            if self.rec is not None:
                self.rec[eng.name].append(("i", id(eng.sem), 1))
        else:
            tok = (eng.sem, eng.count + 1)
        self._commit(tok, reads, writes)
        return tok

    def dma(self, eng, out, in_, reads=(), writes=(), **kw):
        self._deps(eng, reads, writes)
        i = self.dnext
        self.dnext = (self.dnext + 1) % self.NDMASEM
        sem = self.dsems[i]
        if self.dval[i] > 0:
            self.wait(eng, (sem, self.dval[i]))
        self.dval[i] += 16
        eng.h.dma_start(out=out, in_=in_, **kw).then_inc(sem, 16)
        self.ndma += 1
        if self.rec is not None:
            self.rec[eng.name].append(("i", id(sem), 16))
        tok = (sem, self.dval[i])
        self._commit(tok, reads, writes)
        return tok

    def barrier(self):
        for e in self.engs:
            for o in self.engs:
                if o is not e and o.count > 0:
                    self.wait(e, (o.sem, o.count))
            for i, sem in enumerate(self.dsems):
                if self.dval[i] > 0:
                    self.wait(e, (sem, self.dval[i]))

    def finish(self):
        for i, sem in enumerate(self.dsems):
            if self.dval[i] > 0:
                self.wait(self.sp, (sem, self.dval[i]))
        for e in (self.pe, self.act, self.dve, self.pool):
            if e.count > 0:
                self.wait(self.sp, (e.sem, e.count))


class Ring:
    def __init__(self, tiles):
        self.tiles = tiles
        self.bufs = [Buf() for _ in tiles]
        self.i = -1

    def next(self):
        self.i = (self.i + 1) % len(self.tiles)
        return self.tiles[self.i], self.bufs[self.i]

    def cur(self):
        return self.tiles[self.i], self.bufs[self.i]


class Builder:
    def __init__(self, seqs, debug=False):
        self.seqs = seqs
        self.NT = sum(T for _, T in seqs)
        self.TMAX = max(T for _, T in seqs)
        self.debug = debug
        self.nc = nc = bass.Bass("TRN2", target_bir_lowering=False)
        self.fw = FW(nc)
        NT = self.NT
        dt = nc.dram_tensor
        I = "ExternalInput"
        self.x = dt("x", [NT, D], F32, kind=I).ap()
        self.gvec = dt("gvec", [8, D], F32, kind=I).ap()
        self.w_qkv = dt("w_qkv", [D, 3 * D], F32, kind=I).ap()
        self.na_wo = dt("na_wo", [D, D], F32, kind=I).ap()
        self.da_wq = dt("da_wq", [D, D], F32, kind=I).ap()
        self.da_wk = dt("da_wk", [D, D], F32, kind=I).ap()
        self.da_wv = dt("da_wv", [D, D], F32, kind=I).ap()
        self.da_wo = dt("da_wo", [D, D], F32, kind=I).ap()
        self.w_in = dt("w_in", [2, D, 2 * DFF], F32, kind=I).ap()
        self.w_out = dt("w_out", [2, DFF, D], F32, kind=I).ap()
        self.conv_w = dt("conv_w", [2, 3, 2 * DFF], F32, kind=I).ap()
        self.conv_b = dt("conv_b", [2, 2 * DFF], F32, kind=I).ap()
        self.bias_int = dt("bias_int", [NA_H, 128, 640], F32, kind=I).ap()
        self.bias_sp = dt("bias_sp", [4, NA_H, 128, 512], F32, kind=I).ap()
        self.rot_cos = dt("rot_cos", [128, self.TMAX], F32, kind=I).ap()
        self.rot_sin = dt("rot_sin", [128, self.TMAX], F32, kind=I).ap()
        self.lam_in = dt("lam_in", [4, 64], F32, kind=I).ap()
        self.subln_g = dt("subln_g", [128], F32, kind=I).ap()
        self.ident = dt("ident", [128, 128], F32, kind=I).ap()
        self.y = dt("y", [seqs[0][1] + seqs[1][1] // 2 + 512, D], F32, kind="ExternalOutput").ap()
        S = "ExternalOutput" if debug else "Internal"
        self.XA = dt("XA", [NT, D], F32, kind=S).ap()
        self.XB = dt("XB", [NT, D], F32, kind=S).ap()
        self.WINB = dt("WINB", [2, 11, 128, 8, 512], BF16, kind="Internal").ap()
        NBLK = NT // 128
        self.NQT = dt("NQT", [NBLK, 128, 8, 128], BF16, kind="Internal").ap()
        self.NKT = dt("NKT", [NBLK, 128, 8, 128], BF16, kind="Internal").ap()
        self.NV = dt("NV", [128, NBLK, NA_H * 65], BF16, kind="Internal").ap()
        self.DQT = dt("DQT", [8, 128, NT], BF16, kind="Internal").ap()
        self.DKT = dt("DKT", [8, 128, NT], BF16, kind="Internal").ap()
        self.DV = dt("DV", [DA_H, 128, NBLK, 129], BF16, kind="Internal").ap()
        self.DOT = dt("DOT", [DA_H, 128, NT], BF16, kind="Internal").ap()
        self.XC = self.XA
        (self.TS_off, self.TS), (self.TP_off, self.TP) = seqs
        self.OWN = self.TP // 2 + 512
        self.OFF1 = self.TP - self.OWN
        self.fsel = dt("fsel", [128, 2], F32, kind=I).ap()
        self.DQTo = dt("DQTo", [8, 128, self.OWN], BF16, kind="Internal").ap()
        self.DOTo = dt("DOTo", [DA_H, 128, self.OWN], BF16, kind="Internal").ap()
        self.XBo = dt("XBo", [self.OWN, D], F32, kind="Internal").ap()
        self.XCo = dt("XCo", [self.OWN, D], F32, kind="Internal").ap()

    _uid = 0

    def _nm(self, name):
        Builder._uid += 1
        return f"{name}_{Builder._uid}"

    def sb(self, st, name, shape, dtype):
        return st.enter_context(self.nc.sbuf_tensor(self._nm(name), shape, dtype))

    def ps(self, st, name, shape, dtype=F32):
        return st.enter_context(self.nc.psum_tensor(self._nm(name), shape, dtype))

    def ring(self, st, name, n, shape, dtype):
        return Ring([self.sb(st, f"{name}{i}", shape, dtype) for i in range(n)])

    def consts(self, st):
        nc, fw = self.nc, self.fw
        c = {}
        c["idf"] = self.sb(st, "idf", [128, 128], F32)
        c["idb"] = self.sb(st, "idb", [128, 128], BF16)
        c["nh"] = self.sb(st, "nhalf", [128, 1], F32)
        c["B"] = Buf("consts")
        fw.dma(fw.sp, c["idf"][:], self.ident[:, :], writes=[c["B"]])
        fw.dma(fw.pool, c["idb"][:], self.ident[:, :], writes=[c["B"]])
        fw.op(fw.pool, lambda: nc.gpsimd.memset(c["nh"][:], -0.5), writes=[c["B"]])
        return c

    def load_gfm(self, st, name, row):
        nc, fw = self.nc, self.fw
        t = self.sb(st, name, [128, 8], F32)
        b = Buf(name)
        with nc.allow_non_contiguous_dma(reason="tiny gain vector"):
            fw.dma(fw.sp, t[:], self.gvec[row, :].rearrange("(c p) -> p c", p=128), writes=[b])
        return t, b

    def load_gbc(self, st, name, row):
        fw = self.fw
        t = self.sb(st, name, [128, D], F32)
        b = Buf(name)
        fw.dma(fw.sp, t[:], self.gvec[row, :].partition_broadcast(128), writes=[b])
        return t, b

    def rstd(self, ss, ssb, n, inv_dim, eps, c):
        nc, fw = self.nc, self.fw
        fw.op(fw.pool, lambda: nc.gpsimd.tensor_scalar(out=ss[:n, 1:2], in0=ss[:n, 0:1], scalar1=inv_dim, scalar2=eps,
                                                        op0=ALU.mult, op1=ALU.add), reads=[ssb], writes=[ssb])
        fw.op(fw.pool, lambda: nc.gpsimd.tensor_tensor(out=ss[:n, 2:3], in0=ss[:n, 1:2], in1=c["nh"][:n, :], op=ALU.pow),
              reads=[ssb, c["B"]], writes=[ssb])

    def make_frontend(self, st, c, gfm, gfmb, tp, tpb):
        fe = {"xt": self.ring(st, "fe_xt", 2, [128, D], F32), "xs": self.ring(st, "fe_xs", 3, [128, D], F32),
              "junk": self.sb(st, "fe_junk", [128, D], BF16), "junkb": Buf(),
              "ss": self.ring(st, "fe_ss", 4, [128, 4], F32), "c": c, "gfm": gfm, "gfmb": gfmb, "tp": tp, "tpb": tpb}
        return fe

    def frontend_a(self, fe, src, n):
        nc, fw = self.nc, self.fw
        c = fe["c"]
        xt, xtb = fe["xt"].next()
        xs, xsb = fe["xs"].next()
        ss, ssb = fe["ss"].next()
        fw.dma(fw.sp, xt[:n, :], src, writes=[xtb])
        fw.op(fw.act, lambda: nc.scalar.activation(out=fe["junk"][:n, :], in_=xt[:n, :], func=AF.Square, accum_out=ss[:n, 0:1]),
              reads=[xtb], writes=[fe["junkb"], ssb])
        self.rstd(ss, ssb, n, 1.0 / D, NORM_EPS, c)
        fw.op(fw.act, lambda: nc.scalar.activation(out=xs[:n, :], in_=xt[:n, :], func=AF.Copy, scale=ss[:n, 2:3]),
              reads=[xtb, ssb], writes=[xsb])
        return xs, xsb

    def frontend_b(self, fe, xs, xsb, n, hT, hTb, col0):
        nc, fw = self.nc, self.fw
        c = fe["c"]
        tp, tpb = fe["tp"], fe["tpb"]
        for k in range(8):
            fw.op(fw.pe, lambda: nc.tensor.transpose(out=tp[:, k, :n], in_=xs[:n, k * 128:(k + 1) * 128], identity=c["idf"][:n, :n]),
                  reads=[xsb, c["B"]], writes=[tpb], inc=(k == 7))
        fw.op(fw.dve, lambda: nc.vector.tensor_tensor(out=hT[:, :, col0:col0 + n], in0=tp[:, :, :n],
                                                      in1=fe["gfm"][:, :].unsqueeze(2).to_broadcast([128, 8, n]), op=ALU.mult),
              reads=[tpb, fe["gfmb"]], writes=[hTb])

    def frontend(self, fe, src, n, hT, hTb, col0):
        xs, xsb = self.frontend_a(fe, src, n)
        self.frontend_b(fe, xs, xsb, n, hT, hTb, col0)

    def make_backend(self, st, c, gbc, gbcb):
        be = {"xr": self.ring(st, "be_xr", 2, [128, D], F32), "t": self.ring(st, "be_t", 2, [128, D], F32),
              "o": self.ring(st, "be_o", 2, [128, D], F32), "junk": self.sb(st, "be_junk", [128, D], BF16), "junkb": Buf(),
              "ss": self.ring(st, "be_ss", 4, [128, 4], F32), "c": c, "gbc": gbc, "gbcb": gbcb}
        return be

    def backend_prefetch(self, be, res_src, n):
        xr, xrb = be["xr"].next()
        self.fw.dma(self.fw.sp, xr[:n, :], res_src, writes=[xrb])
        return xr, xrb

    def backend(self, be, m, mb, n, xr, xrb, dst):
        nc, fw = self.nc, self.fw
        ss, ssb = be["ss"].next()
        t, tb = be["t"].next()
        o, ob = be["o"].next()
        m2 = m[:n, :, :].rearrange("p a b -> p (a b)")
        fw.op(fw.act, lambda: nc.scalar.activation(out=be["junk"][:n, :], in_=m2, func=AF.Square, accum_out=ss[:n, 0:1]),
              reads=[mb], writes=[be["junkb"], ssb])
        self.rstd(ss, ssb, n, 1.0 / D, NORM_EPS, be["c"])
        fw.op(fw.dve, lambda: nc.vector.scalar_tensor_tensor(out=t[:n, :], in0=m2, scalar=ss[:n, 2:3], in1=be["gbc"][:n, :],
                                                             op0=ALU.mult, op1=ALU.mult), reads=[mb, ssb, be["gbcb"]], writes=[tb])
        fw.op(fw.pool, lambda: nc.gpsimd.tensor_tensor(out=o[:n, :], in0=t[:n, :], in1=xr[:n, :], op=ALU.add),
              reads=[tb, xrb], writes=[ob])
        fw.dma(fw.sp, dst, o[:n, :], reads=[ob])

    def phase_p0(self):
        nc, fw = self.nc, self.fw
        with ExitStack() as st:
            stg = self.ring(st, "p0_stg", 3, [128, 8, 512], BF16)
            for l in range(2):
                for g in range(11):
                    t, tb = stg.next()
                    src = self.w_in[l]
                    fw.dma(fw.pool, t[:, :, 0:256], src[:, 256 * g:256 * g + 256].rearrange("(kc p) n -> p kc n", p=128), writes=[tb])
                    fw.dma(fw.pool, t[:, :, 256:512], src[:, DFF + 256 * g:DFF + 256 * g + 256].rearrange("(kc p) n -> p kc n", p=128), writes=[tb])
                    fw.dma(fw.sp, self.WINB[l, g], t[:], reads=[tb])
            fw.barrier()

    def phase_ffn(self, l, seqlist):
        nc, fw = self.nc, self.fw
        with ExitStack() as st:
            c = self.consts(st)
            gfm, gfmb = self.load_gfm(st, "ffn_gfm", 4 + l)
            gbc, gbcb = self.load_gbc(st, "ffn_gbc", 6 + l)
            wout = self.sb(st, "ffn_wout", [128, NFC, D], BF16)
            woutb = Buf()
            for f0 in range(0, NFC, 6):
                f1 = min(NFC, f0 + 6)
                fw.dma(fw.pool, wout[:, f0:f1, :], self.w_out[l, f0 * 128:f1 * 128, :].rearrange("(f p) n -> p f n", p=128), writes=[woutb])
            cw = self.sb(st, "ffn_cw", [128, 3, 2 * NFC], F32)
            cb = self.sb(st, "ffn_cb", [128, 2 * NFC], F32)
            cwb = Buf()
            with nc.allow_non_contiguous_dma(reason="small conv params"):
                for k in range(3):
                    fw.dma(fw.sp, cw[:, k, :], self.conv_w[l, k, :].rearrange("(c p) -> p c", p=128), writes=[cwb])
                fw.dma(fw.sp, cb[:, :], self.conv_b[l, :].rearrange("(c p) -> p c", p=128), writes=[cwb])
            tp = self.ps(st, "ffn_tp", [128, 8, 128])
            tpb = Buf()
            pgu = [self.ps(st, f"ffn_pgu{i}", [128, 2, 512]) for i in range(2)]
            pgub = [[Buf(), Buf()] for _ in range(2)]
            pm = self.ps(st, "ffn_pm", [128, 2, 512])
            pmb = Buf()
            fe = self.make_frontend(st, c, gfm, gfmb, tp, tpb)
            be = self.make_backend(st, c, gbc, gbcb)
            hT = self.ring(st, "ffn_hT", 2, [128, 8, 512], BF16)
            wg = self.ring(st, "ffn_wg", 3, [128, 8, 512], BF16)
            aT = self.ring(st, "ffn_aT", 2, [128, NFC, 513], BF16)
            carry = self.sb(st, "ffn_carry", [128, 2 * NFC, 2], F32)
            carryb = [Buf() for _ in range(2 * NFC)]
            T1 = self.ring(st, "ffn_T1", 2, [128, 513], F32)
            T2 = self.ring(st, "ffn_T2", 2, [128, 513], F32)
            TG = self.ring(st, "ffn_TG", 2, [128, 513], F32)
            TU = self.ring(st, "ffn_TU", 2, [128, 513], F32)
            GG = self.ring(st, "ffn_GG", 2, [128, 513], F32)

            tiles = []
            for (src_, dst_, T) in seqlist:
                nt_ = T // 512
                for i in range(nt_):
                    tiles.append(dict(toff=0, src=src_, dst=dst_, i=i, first=(i == 0), last=(i == nt_ - 1)))
            NTL = len(tiles)
            for n, tl in enumerate(tiles):
                tl["h"], tl["hb"] = None, None
                W = 513 if tl["last"] else 512
                j = 1 if tl["first"] else 0
                blocks = []
                while j < 512:
                    nn = min(128, 512 - j)
                    blocks.append((j, nn))
                    j += nn
                if W == 513:
                    blocks.append((512, 1))
                tl["W"], tl["blocks"] = W, blocks
            xs_pend = {}

            def A1(n, b):
                if n >= NTL:
                    return
                tl = tiles[n]
                if b == 0:
                    tl["h"], tl["hb"] = hT.next()
                t0 = tl["toff"] + 512 * tl["i"] + 128 * b
                xs_pend[(n, b)] = self.frontend_a(fe, tl["src"][t0:t0 + 128, :], 128)

            def A2(n, b):
                if n >= NTL:
                    return
                tl = tiles[n]
                xs, xsb = xs_pend.pop((n, b))
                self.frontend_b(fe, xs, xsb, 128, tl["h"], tl["hb"], 128 * b)

            def C(n, bi):
                if n < 0 or bi >= len(tiles[n]["blocks"]):
                    return
                tl = tiles[n]
                a_, a_b = tl["aT"], tl["aTb"]
                j, nn = tl["blocks"][bi]
                tk = tl["toff"] + 512 * tl["i"] - 1 + j
                src, dst = tl["src"], tl["dst"]
                xr, xrb = self.backend_prefetch(be, src[tk:tk + nn, :], nn)
                for half in range(2):
                    for f in range(NFC):
                        fw.op(fw.pe, lambda: nc.tensor.matmul(pm[:nn, half, :], a_[:, f, j:j + nn], wout[:, f, half * 512:(half + 1) * 512],
                                                              start=(f == 0), stop=(f == NFC - 1)),
                              reads=[a_b, woutb], writes=[pmb], inc=(f == NFC - 1 and half == 1))
                self.backend(be, pm, pmb, nn, xr, xrb, dst[tk:tk + nn, :])

            def Bgroup(n, g, wtile):
                tl = tiles[n]
                h, hb = tl["h"], tl["hb"]
                a_, a_b = tl["aT"], tl["aTb"]
                W = tl["W"]
                w_, wb_ = wtile
                for jf in range(2):
                    f = 2 * g + jf
                    P_ = pgu[f % 2]
                    Pb = pgub[f % 2]
                    t3s = []
                    for gu in range(2):
                        cidx = f + NFC * gu
                        col = 256 * gu + 128 * jf
                        for kc in range(8):
                            fw.op(fw.pe, lambda: nc.tensor.matmul(P_[:, gu, :], w_[:, kc, col:col + 128], h[:, kc, :], start=(kc == 0), stop=(kc == 7)),
                                  reads=[wb_, hb], writes=[Pb[gu]], inc=(kc == 7))
                        Pg = P_[:, gu, :]
                        t1, t1b = T1.next()
                        t2, t2b = T2.next()
                        t3, t3b = (TG if gu == 0 else TU).next()
                        sc1, sc0, sc2, bi_ = cw[:, 1, cidx:cidx + 1], cw[:, 0, cidx:cidx + 1], cw[:, 2, cidx:cidx + 1], cb[:, cidx:cidx + 1]
                        fw.op(fw.act, lambda: nc.scalar.activation(out=t1[:, 1:W], in_=Pg[:, 0:W - 1], func=AF.Identity, scale=sc1, bias=bi_),
                              reads=[Pb[gu], cwb], writes=[t1b])
                        fw.op(fw.act, lambda: nc.scalar.activation(out=t1[:, 0:1], in_=carry[:, cidx, 1:2], func=AF.Identity, scale=sc1, bias=bi_),
                              reads=[carryb[cidx], cwb], writes=[t1b])
                        fw.op(fw.dve, lambda: nc.vector.scalar_tensor_tensor(out=t2[:, 2:W], in0=Pg[:, 0:W - 2], scalar=sc0, in1=t1[:, 2:W], op0=ALU.mult, op1=ALU.add),
                              reads=[Pb[gu], t1b, cwb], writes=[t2b])
                        fw.op(fw.dve, lambda: nc.vector.scalar_tensor_tensor(out=t2[:, 0:2], in0=carry[:, cidx, :], scalar=sc0, in1=t1[:, 0:2], op0=ALU.mult, op1=ALU.add),
                              reads=[carryb[cidx], t1b, cwb], writes=[t2b])
                        fw.op(fw.dve, lambda: nc.vector.scalar_tensor_tensor(out=t3[:, 0:512], in0=Pg[:, 0:512], scalar=sc2, in1=t2[:, 0:512], op0=ALU.mult, op1=ALU.add),
                              reads=[Pb[gu], t2b, cwb], writes=[t3b])
                        if W == 513:
                            fw.op(fw.dve, lambda: nc.vector.tensor_copy(out=t3[:, 512:513], in_=t2[:, 512:513]), reads=[t2b], writes=[t3b])
                        fw.op(fw.dve, lambda: nc.vector.tensor_copy(out=carry[:, cidx, :], in_=Pg[:, 510:512]), reads=[Pb[gu]], writes=[carryb[cidx]])
                        t3s.append((t3, t3b))
                    (tg, tgb), (tu, tub) = t3s
                    gg, ggb = GG.next()
                    fw.op(fw.act, lambda: nc.scalar.activation(out=gg[:, 0:W], in_=tg[:, 0:W], func=AF.Gelu_apprx_tanh), reads=[tgb], writes=[ggb])
                    fw.op(fw.pool, lambda: nc.gpsimd.tensor_tensor(out=a_[:, f, 0:W], in0=gg[:, 0:W], in1=tu[:, 0:W], op=ALU.mult),
                          reads=[ggb, tub], writes=[a_b])

            for b in range(4):
                A1(0, b)
                A2(0, b)
            for n, tl in enumerate(tiles):
                tl["aT"], tl["aTb"] = aT.next()
                if tl["first"]:
                    fw.op(fw.pool, lambda: nc.gpsimd.memset(carry[:, :, :], 0.0), writes=carryb)
                wq = []
                for g in range(2):
                    w_, wb_ = wg.next()
                    fw.dma(fw.sp, w_[:], self.WINB[l, g], writes=[wb_])
                    wq.append((w_, wb_))
                for g in range(11):
                    if g + 2 < 11:
                        w_, wb_ = wg.next()
                        fw.dma(fw.sp, w_[:], self.WINB[l, g + 2], writes=[wb_])
                        wq.append((w_, wb_))
                    Bgroup(n, g, wq[g])
                    if SEQ_FFN:
                        if g == 10:
                            for b in range(4):
                                A1(n + 1, b)
                                A2(n + 1, b)
                            for bi in range(5):
                                C(n, bi)
                        continue
                    if g in (0, 2, 4, 6):
                        A1(n + 1, g // 2)
                    if g in (1, 3, 5, 7):
                        C(n - 1, (g - 1) // 2)
                        A2(n + 1, (g - 1) // 2)
                    if g == 8:
                        C(n - 1, 4)
            if not SEQ_FFN:
                for bi in range(5):
                    C(NTL - 1, bi)
            fw.barrier()

    def phase_proj(self, mode, src):
        nc, fw = self.nc, self.fw
        na = (mode == "na")
        with ExitStack() as st:
            c = self.consts(st)
            gfm, gfmb = self.load_gfm(st, "pj_gfm", 0 if na else 1)
            w = self.sb(st, "pj_w", [128, 8, 3 * D], BF16)
            wb = Buf()
            if na:
                for k3 in range(3):
                    fw.dma(fw.pool, w[:, :, k3 * D:(k3 + 1) * D], self.w_qkv[:, k3 * D:(k3 + 1) * D].rearrange("(kc p) n -> p kc n", p=128), writes=[wb])
            else:
                for k3, ww in enumerate((self.da_wq, self.da_wk, self.da_wv)):
                    fw.dma(fw.pool, w[:, :, k3 * D:(k3 + 1) * D], ww.rearrange("(kc p) n -> p kc n", p=128), writes=[wb])
            tp = self.ps(st, "pj_tp", [128, 8, 128])
            tpb = Buf()
            pp = [self.ps(st, f"pj_pp{i}", [128, 2, 512]) for i in range(3)]
            ppb = [[Buf(), Buf()] for _ in range(3)]
            fe = self.make_frontend(st, c, gfm, gfmb, tp, tpb)
            hT = self.ring(st, "pj_hT", 2, [128, 8, 512], BF16)
            if na:
                stq = self.ring(st, "pj_stq", 2, [128, 4, 8, 128], BF16)
                stk = self.ring(st, "pj_stk", 2, [128, 4, 8, 128], BF16)
                vaug = self.ring(st, "pj_vaug", 2, [128, 4, NA_H, 65], BF16)
                for t_, b_ in zip(vaug.tiles, vaug.bufs):
                    fw.op(fw.pool, lambda t_=t_: nc.gpsimd.memset(t_[:, :, :, 64:65], 1.0), writes=[b_])
            else:
                stq = self.ring(st, "pj_stq", 2, [128, 8, 512], BF16)
                stk = self.ring(st, "pj_stk", 2, [128, 8, 512], BF16)
                vaug = self.ring(st, "pj_vaug", 2, [128, DA_H, 4, 129], BF16)
                for t_, b_ in zip(vaug.tiles, vaug.bufs):
                    fw.op(fw.pool, lambda t_=t_: nc.gpsimd.memset(t_[:, :, :, 128:129], 1.0), writes=[b_])
                cosr = self.ring(st, "pj_cos", 2, [128, 512], F32)
                sinr = self.ring(st, "pj_sin", 2, [128, 512], F32)
                rt = [self.ring(st, f"pj_rt{i}", 2, [128, 512], F32) for i in range(4)]
            sbs = []
            for (toff, T) in self.seqs:
                for sbi in range(T // 512):
                    sbs.append(dict(t0=toff + 512 * sbi, sbi=sbi))
            xs_pend = {}

            def A1(n, b):
                if n >= len(sbs):
                    return
                sb_ = sbs[n]
                if b == 0:
                    sb_["h"], sb_["hb"] = hT.next()
                t0 = sb_["t0"] + 128 * b
                xs_pend[(n, b)] = self.frontend_a(fe, src[t0:t0 + 128, :], 128)

            def A2(n, b):
                if n >= len(sbs):
                    return
                sb_ = sbs[n]
                xs, xsb = xs_pend.pop((n, b))
                self.frontend_b(fe, xs, xsb, 128, sb_["h"], sb_["hb"], 128 * b)

            pis = [0]

            def items_for(n):
                sb_ = sbs[n]
                t0, sbi = sb_["t0"], sb_["sbi"]
                blk0 = t0 // 128
                h, hb = sb_["h"], sb_["hb"]
                items = []
                st_ = {}
                if not na:
                    def ld():
                        st_["cs"], st_["csb"] = cosr.next()
                        st_["sn"], st_["snb"] = sinr.next()
                        fw.dma(fw.sp, st_["cs"][:], self.rot_cos[:, 512 * sbi:512 * sbi + 512], writes=[st_["csb"]])
                        fw.dma(fw.sp, st_["sn"][:], self.rot_sin[:, 512 * sbi:512 * sbi + 512], writes=[st_["snb"]])
                    items.append(ld)
                for qk in range(2):
                    def start_qk(qk=qk):
                        st_["sg"], st_["sgb"] = (stq if qk == 0 else stk).next()
                    items.append(start_qk)
                    if na:
                        for oc in range(8):
                            def it(qk=qk, oc=oc):
                                sg, sgb = st_["sg"], st_["sgb"]
                                P_, Pb = pp[pis[0] % 3], ppb[pis[0] % 3]
                                pis[0] += 1
                                col = qk * D + oc * 128
                                for kc in range(8):
                                    fw.op(fw.pe, lambda: nc.tensor.matmul(P_[:, 0, :], w[:, kc, col:col + 128], h[:, kc, :], start=(kc == 0), stop=(kc == 7)),
                                          reads=[wb, hb], writes=[Pb[0]], inc=(kc == 7))
                                if oc % 2 == 0:
                                    fw.op(fw.act, lambda: nc.scalar.activation(out=sg[:, :, oc, :], in_=P_[:, 0, :].rearrange("p (b t) -> p b t", b=4), func=AF.Copy),
                                          reads=[Pb[0]], writes=[sgb])
                                else:
                                    fw.op(fw.dve, lambda: nc.vector.tensor_copy(out=sg[:, :, oc, :], in_=P_[:, 0, :].rearrange("p (b t) -> p b t", b=4)),
                                          reads=[Pb[0]], writes=[sgb])
                            items.append(it)

                        def fin(qk=qk):
                            dstT = (self.NQT if qk == 0 else self.NKT)
                            fw.dma(fw.sp, dstT[blk0:blk0 + 4].rearrange("b p o t -> p b o t"), st_["sg"][:], reads=[st_["sgb"]])
                        items.append(fin)
                    else:
                        for m in range(4):
                            def it(qk=qk, m=m):
                                sg, sgb = st_["sg"], st_["sgb"]
                                cs, csb, sn, snb = st_["cs"], st_["csb"], st_["sn"], st_["snb"]
                                P_, Pb = pp[pis[0] % 3], ppb[pis[0] % 3]
                                pis[0] += 1
                                for ab in range(2):
                                    col = qk * D + m * 256 + ab * 128
                                    for kc in range(8):
                                        fw.op(fw.pe, lambda: nc.tensor.matmul(P_[:, ab, :], w[:, kc, col:col + 128], h[:, kc, :], start=(kc == 0), stop=(kc == 7)),
                                              reads=[wb, hb], writes=[Pb[ab]], inc=(kc == 7))
                                t1, t1b = rt[0].next()
                                t2, t2b = rt[1].next()
                                t3, t3b = rt[2].next()
                                t4, t4b = rt[3].next()
                                TT = nc.vector.tensor_tensor
                                fw.op(fw.dve, lambda: TT(out=t1[:], in0=P_[:, 0, :], in1=cs[:], op=ALU.mult), reads=[Pb[0], csb], writes=[t1b])
                                fw.op(fw.dve, lambda: TT(out=t2[:], in0=P_[:, 1, :], in1=sn[:], op=ALU.mult), reads=[Pb[1], snb], writes=[t2b])
                                fw.op(fw.dve, lambda: TT(out=t3[:], in0=P_[:, 1, :], in1=cs[:], op=ALU.mult), reads=[Pb[1], csb], writes=[t3b])
                                fw.op(fw.dve, lambda: TT(out=t4[:], in0=P_[:, 0, :], in1=sn[:], op=ALU.mult), reads=[Pb[0], snb], writes=[t4b])
                                fw.op(fw.pool, lambda: nc.gpsimd.tensor_tensor(out=sg[:, 2 * m, :], in0=t1[:], in1=t2[:], op=ALU.subtract), reads=[t1b, t2b], writes=[sgb])
                                fw.op(fw.pool, lambda: nc.gpsimd.tensor_tensor(out=sg[:, 2 * m + 1, :], in0=t3[:], in1=t4[:], op=ALU.add), reads=[t3b, t4b], writes=[sgb])
                            items.append(it)

                        def fin(qk=qk):
                            dstT = (self.DQT if qk == 0 else self.DKT)
                            fw.dma(fw.sp, dstT[:, :, t0:t0 + 512].rearrange("o p t -> p o t"), st_["sg"][:], reads=[st_["sgb"]])
                        items.append(fin)

                def start_v():
                    st_["va"], st_["vab"] = vaug.next()
                items.append(start_v)
                for b in range(4):
                    def itv(b=b):
                        va, vab = st_["va"], st_["vab"]
                        P_, Pb = pp[pis[0] % 3], ppb[pis[0] % 3]
                        pis[0] += 1
                        for half in range(2):
                            col = 2 * D + half * 512
                            for kc in range(8):
                                fw.op(fw.pe, lambda: nc.tensor.matmul(P_[:, half, :], h[:, kc, 128 * b:128 * b + 128], w[:, kc, col:col + 512], start=(kc == 0), stop=(kc == 7)),
                                      reads=[wb, hb], writes=[Pb[half]], inc=(kc == 7))
                            if na:
                                o_ap = va[:, b, 8 * half:8 * half + 8, 0:64]
                                i_ap = P_[:, half, :].rearrange("p (h d) -> p h d", h=8)
                            else:
                                o_ap = va[:, 4 * half:4 * half + 4, b, 0:128]
                                i_ap = P_[:, half, :].rearrange("p (h d) -> p h d", h=4)
                            if half == 0:
                                fw.op(fw.act, lambda: nc.scalar.activation(out=o_ap, in_=i_ap, func=AF.Copy), reads=[Pb[half]], writes=[vab])
                            else:
                                fw.op(fw.dve, lambda: nc.vector.tensor_copy(out=o_ap, in_=i_ap), reads=[Pb[half]], writes=[vab])
                    items.append(itv)

                def finv():
                    va, vab = st_["va"], st_["vab"]
                    if na:
                        fw.dma(fw.sp, self.NV[:, blk0:blk0 + 4, :], va[:].rearrange("p b h d -> p b (h d)"), reads=[vab])
                    else:
                        fw.dma(fw.sp, self.DV[:, :, blk0:blk0 + 4, :].rearrange("h p b x -> p h b x"), va[:], reads=[vab])
                items.append(finv)
                return items

            for b in range(4):
                A1(0, b)
                A2(0, b)
            for n in range(len(sbs)):
                items = items_for(n)
                L = len(items)
                hooks = {}
                for k in range(8):
                    hooks.setdefault(min(L - 1, (L * (k + 1)) // 9), []).append(k)
                for idx, itf in enumerate(items):
                    itf()
                    for k in hooks.get(idx, []):
                        if k % 2 == 0:
                            A1(n + 1, k // 2)
                        else:
                            A2(n + 1, k // 2)
            fw.barrier()

    @staticmethod
    def na_kc(j, NB):
        R = 2 * NB
        r0a = min(max(2 * j - 4, 0), R - 8)
        r0b = min(max(2 * j + 1 - 4, 0), R - 8)
        return list(range(r0a // 2, (r0b + 7) // 2 + 1))

    def phase_na2(self, dst):
        nc, fw = self.nc, self.fw
        with ExitStack() as st:
            c = self.consts(st)
            gbc, gbcb = self.load_gbc(st, "na_gbc", 2)
            wo = self.sb(st, "na_wo", [128, 8, D], BF16)
            wob = Buf()
            ORDER = [0, 2, 4, 6, 8, 10, 12, 14, 1, 3, 5, 7, 9, 11, 13, 15]
            for pos, hh_ in enumerate(ORDER):
                fw.dma(fw.pool, wo[64 * (pos % 2):64 * (pos % 2) + 64, pos // 2, :], self.na_wo[hh_ * 64:hh_ * 64 + 64, :], writes=[wob])
            bint = self.sb(st, "na_bint", [128, NA_H, 640], F32)
            bintb = Buf()
            for h0 in range(0, NA_H, 4):
                fw.dma(fw.sp, bint[:, h0:h0 + 4, :], self.bias_int[h0:h0 + 4].rearrange("h p n -> p h n"), writes=[bintb])
            bsp = self.ring(st, "na_bsp", 3, [128, 512], F32)
            kring = self.ring(st, "na_k", 10, [128, 8, 128], BF16)
            vring = self.ring(st, "na_v", 10, [128, NA_H * 65], BF16)
            qr = self.ring(st, "na_q", 3, [128, 8, 128], BF16)
            ssb_r = self.ring(st, "na_ssb", 3, [128, 640], F32)
            pT = self.ring(st, "na_pT", 4, [128, 640], BF16)
            on_r = self.ring(st, "na_on", 2, [128, NA_H, 64], BF16)
            oT_r = self.ring(st, "na_oT", 2, [128, 8, 128], BF16)
            rec_r = self.ring(st, "na_rec", 4, [128, 8], F32)
            be = self.make_backend(st, c, gbc, gbcb)
            Sall = self.ps(st, "na_S", [128, 4, 512])
            Sflat = Sall[:, :, :].rearrange("p a b -> p (a b)")
            Sb = [Buf(), Buf()]
            NREG, RSTR = 2, 1024
            O = [self.ps(st, f"na_O{i}", [128, 512]) for i in range(2)]
            Ob = [Buf(), Buf()]
            M = self.ps(st, "na_M", [128, 2, 512])
            Mb = Buf()
            Mbf = M[:, 0, :].bitcast(BF16)
            GROUPS = ((0, 7, 0), (7, 14, 1), (14, 16, 0))
            units = []
            for si_, (toff, T) in enumerate(self.seqs):
                for j in range(T // 128):
                    for h in range(NA_H):
                        units.append((si_, j, h))
            blocks = {}
            seqstate = {}
            pend = []

            def block_setup(si_, j):
                toff, T = self.seqs[si_]
                NB = T // 128
                blkoff = toff // 128
                if j == 0:
                    seqstate[si_] = {"loaded": -1, "kslot": {}}
                ss_ = seqstate[si_]
                kcs = self.na_kc(j, NB)
                upto = min(max(kcs) + 1, NB - 1)
                while ss_["loaded"] < upto:
                    ss_["loaded"] += 1
                    kt, ktb = kring.next()
                    vt, vtb = vring.next()
                    fw.dma(fw.sp, kt[:], self.NKT[blkoff + ss_["loaded"]], writes=[ktb])
                    fw.dma(fw.sp, vt[:], self.NV[:, blkoff + ss_["loaded"], :], writes=[vtb])
                    ss_["kslot"][ss_["loaded"]] = (kt, ktb, vt, vtb)
                if j < 2:
                    sp_idx = j
                elif j >= NB - 2:
                    sp_idx = 2 + (j - (NB - 2))
                else:
                    sp_idx = None
                q, qb = qr.next()
                fw.dma(fw.sp, q[:], self.NQT[blkoff + j], writes=[qb])
                tk = toff + 128 * j
                xr, xrb = self.backend_prefetch(be, self.x[tk:tk + 128, :], 128)
                on, onb = on_r.next()
                blocks[(si_, j)] = dict(kcs=kcs, sp_idx=sp_idx, q=q, qb=qb, tk=tk, xr=xr, xrb=xrb, on=on, onb=onb,
                                        ks=[ss_["kslot"][kc] for kc in kcs])

            def emit_qk(ui):
                si_, j, pos = units[ui]
                h = ORDER[pos]
                if pos == 0:
                    block_setup(si_, j)
                B_ = blocks[(si_, j)]
                oc, r0 = h // 2, (h % 2) * 64
                reg = ui % NREG
                nt = len(B_["kcs"])
                for i in range(nt):
                    kt, ktb, vt, vtb = B_["ks"][i]
                    fw.op(fw.pe, lambda: nc.tensor.matmul(Sflat[:, RSTR * reg + i * 128:RSTR * reg + (i + 1) * 128], kt[r0:r0 + 64, oc, :], B_["q"][r0:r0 + 64, oc, :],
                                                          start=True, stop=True),
                          reads=[ktb, B_["qb"]], writes=[Sb[reg]], inc=(i == nt - 1))

            def emit_rest(ui):
                si_, j, pos = units[ui]
                h = ORDER[pos]
                B_ = blocks[(si_, j)]
                reg = ui % NREG
                nt = len(B_["kcs"])
                Sf = Sflat[:, RSTR * reg:RSTR * reg + nt * 128]
                if B_["sp_idx"] is None:
                    b_ap, b_b = bint[:, h, 0:nt * 128], bintb
                else:
                    bt, btb = bsp.next()
                    fw.dma(fw.sp, bt[:], self.bias_sp[B_["sp_idx"], h], writes=[btb])
                    b_ap, b_b = bt[:, 0:nt * 128], btb
                ss_, ss_b = ssb_r.next()
                fw.op(fw.dve, lambda: nc.vector.scalar_tensor_tensor(out=ss_[:, 0:nt * 128], in0=Sf, scalar=0.125, in1=b_ap, op0=ALU.mult, op1=ALU.add),
                      reads=[Sb[reg], b_b], writes=[ss_b])
                p_, p_b = pT.next()
                fw.op(fw.act, lambda: nc.scalar.activation(out=p_[:, 0:nt * 128], in_=ss_[:, 0:nt * 128], func=AF.Exp), reads=[ss_b], writes=[p_b])
                pts[ui] = (p_, p_b)

            def emit_pv(ui):
                si_, j, pos = units[ui]
                h = ORDER[pos]
                B_ = blocks[(si_, j)]
                nt = len(B_["kcs"])
                p_, p_b = pts.pop(ui)
                hg0, hg1, ob_i = [g for g in GROUPS if g[0] <= pos < g[1]][0]
                Ot, Otb = O[ob_i], Ob[ob_i]
                sl = pos - hg0
                for i in range(nt):
                    kt, ktb, vt, vtb = B_["ks"][i]
                    fw.op(fw.pe, lambda: nc.tensor.matmul(Ot[:, sl * 65:(sl + 1) * 65], p_[:, i * 128:(i + 1) * 128], vt[:, h * 65:(h + 1) * 65],
                                                          start=(i == 0), stop=(i == nt - 1)),
                          reads=[p_b, vtb], writes=[Otb], inc=(i == nt - 1))
                if pos == hg1 - 1:
                    nh_ = hg1 - hg0
                    rec, recb = rec_r.next()
                    on, onb = B_["on"], B_["onb"]
                    Ov = Ot[:, 0:nh_ * 65].rearrange("p (h d) -> p h d", d=65)
                    fw.op(fw.dve, lambda: nc.vector.reciprocal(out=rec[:, 0:nh_], in_=Ov[:, :, 64]), reads=[Otb], writes=[recb])
                    fw.op(fw.dve, lambda: nc.vector.tensor_tensor(out=on[:, hg0:hg1, :], in0=Ov[:, :, 0:64],
                                                                  in1=rec[:, 0:nh_].unsqueeze(2).to_broadcast([128, nh_, 64]), op=ALU.mult),
                          reads=[Otb, recb], writes=[onb])
                if pos == NA_H - 1:
                    def tail(B_=B_):
                        on, onb = B_["on"], B_["onb"]
                        onf = on[:, :, :].rearrange("p h d -> p (h d)")
                        for k in range(8):
                            fw.op(fw.pe, lambda: nc.tensor.transpose(out=Mbf[:, k * 128:(k + 1) * 128], in_=onf[:, k * 128:(k + 1) * 128], identity=c["idb"][:]),
                                  reads=[onb, c["B"]], writes=[Mb], inc=(k == 7))
                        oT, oTb = oT_r.next()
                        fw.op(fw.dve, lambda: nc.vector.tensor_copy(out=oT[:, :, :].rearrange("p a b -> p (a b)"), in_=Mbf[:, :]), reads=[Mb], writes=[oTb])
                        for half in range(2):
                            for k in range(8):
                                fw.op(fw.pe, lambda: nc.tensor.matmul(M[:, half, :], oT[:, k, :], wo[:, k, half * 512:(half + 1) * 512], start=(k == 0), stop=(k == 7)),
                                      reads=[oTb, wob], writes=[Mb], inc=(k == 7 and half == 1))
                        tk = B_["tk"]
                        self.backend(be, M, Mb, 128, B_["xr"], B_["xrb"], dst[tk:tk + 128, :])
                    pend.append(tail)
                if pos == 3:
                    while pend:
                        pend.pop(0)()

            pts = {}
            for ui in range(len(units) + 2):
                if ui < len(units):
                    emit_qk(ui)
                if 0 <= ui - 1 < len(units):
                    emit_rest(ui - 1)
                if ui - 2 >= 0:
                    emit_pv(ui - 2)
            while pend:
                pend.pop(0)()
            fw.barrier()

    def phase_select(self):
        nc, fw = self.nc, self.fw
        OWN, OFF1, tp0 = self.OWN, self.OFF1, self.TP_off
        with ExitStack() as st:
            f = self.sb(st, "sel_f", [128, 2], F32)
            fb = Buf()
            fw.dma(fw.sp, f[:], self.fsel[:, :], writes=[fb])
            qa = self.ring(st, "sel_qa", 2, [128, OWN], BF16)
            qb_ = self.ring(st, "sel_qb", 2, [128, OWN], BF16)
            qt = self.ring(st, "sel_qt", 2, [128, OWN], F32)
            qo = self.ring(st, "sel_qo", 2, [128, OWN], BF16)
            for oc in range(8):
                a, ab = qa.next()
                b, bb = qb_.next()
                t, tb = qt.next()
                o, ob = qo.next()
                fw.dma(fw.sp, a[:], self.DQT[oc, :, tp0:tp0 + OWN], writes=[ab])
                fw.dma(fw.sp, b[:], self.DQT[oc, :, tp0 + OFF1:tp0 + OFF1 + OWN], writes=[bb])
                fw.op(fw.dve, lambda: nc.vector.tensor_scalar(out=t[:], in0=a[:], scalar1=f[:, 0:1], scalar2=None, op0=ALU.mult), reads=[ab, fb], writes=[tb])
                fw.op(fw.dve, lambda: nc.vector.scalar_tensor_tensor(out=o[:], in0=b[:], scalar=f[:, 1:2], in1=t[:], op0=ALU.mult, op1=ALU.add), reads=[bb, tb, fb], writes=[ob])
                fw.dma(fw.sp, self.DQTo[oc, :, :], o[:], reads=[ob])
            xa = self.ring(st, "sel_xa", 2, [128, D], F32)
            xb = self.ring(st, "sel_xb", 2, [128, D], F32)
            xt = self.ring(st, "sel_xt", 2, [128, D], F32)
            xo = self.ring(st, "sel_xo", 2, [128, D], F32)
            for k in range(OWN // 128):
                a, ab = xa.next()
                b, bb = xb.next()
                t, tb = xt.next()
                o, ob = xo.next()
                fw.dma(fw.sp, a[:], self.XB[tp0 + 128 * k:tp0 + 128 * k + 128, :], writes=[ab])
                fw.dma(fw.sp, b[:], self.XB[tp0 + OFF1 + 128 * k:tp0 + OFF1 + 128 * k + 128, :], writes=[bb])
                fw.op(fw.dve, lambda: nc.vector.tensor_scalar(out=t[:], in0=a[:], scalar1=f[:, 0:1], scalar2=None, op0=ALU.mult), reads=[ab, fb], writes=[tb])
                fw.op(fw.dve, lambda: nc.vector.scalar_tensor_tensor(out=o[:], in0=b[:], scalar=f[:, 1:2], in1=t[:], op0=ALU.mult, op1=ALU.add), reads=[bb, tb, fb], writes=[ob])
                fw.dma(fw.sp, self.XBo[128 * k:128 * k + 128, :], o[:], reads=[ob])
            fw.barrier()

    def phase_da2(self):
        nc, fw = self.nc, self.fw
        with ExitStack() as st:
            c = self.consts(st)
            lam = self.sb(st, "da_lam", [128, 4, 64], F32)
            lamb = Buf()
            for i in range(4):
                fw.dma(fw.sp, lam[:, i, :], self.lam_in[i, :].partition_broadcast(128), writes=[lamb])
            lj = self.sb(st, "da_lj", [128, 64], F32)
            lv = self.sb(st, "da_lv", [128, 8], F32)
            lvb = Buf()
            for i in range(2):
                fw.op(fw.dve, lambda i=i: nc.vector.scalar_tensor_tensor(out=lj[:], in0=lam[:, 2 * i, :], scalar=1.0, in1=lam[:, 2 * i + 1, :], op0=ALU.mult, op1=ALU.mult,
                                                                       accum_out=lv[:, i:i + 1]), reads=[lamb], writes=[lvb])
            fw.op(fw.act, lambda: nc.scalar.activation(out=lv[:, 2:4], in_=lv[:, 0:2], func=AF.Exp), reads=[lvb], writes=[lvb])
            fw.op(fw.dve, lambda: nc.vector.scalar_tensor_tensor(out=lv[:, 4:5], in0=lv[:, 3:4], scalar=-LAMBDA_INIT, in1=lv[:, 2:3], op0=ALU.add, op1=ALU.subtract),
                  reads=[lvb], writes=[lvb])
            gsub = self.sb(st, "da_gsub", [128, 128], F32)
            gsubb = Buf()
            fw.dma(fw.sp, gsub[:], self.subln_g.partition_broadcast(128), writes=[gsubb])
            fw.op(fw.dve, lambda: nc.vector.tensor_scalar(out=gsub[:], in0=gsub[:], scalar1=(1.0 - LAMBDA_INIT), scalar2=None, op0=ALU.mult), reads=[gsubb], writes=[gsubb])

            TM = self.TMAX
            NCM = TM // 128
            KT = self.ring(st, "da_KT", 2, [128, TM], BF16)
            QT = self.ring(st, "da_QT", 2, [128, TM], BF16)
            VH = self.ring(st, "da_VH", 2, [128, NCM, 129], BF16)
            E = self.ring(st, "da_E", 3, [128, 2, 512], BF16)
            accs = self.ring(st, "da_accs", 2, [128, 8, 129], F32)
            rec_r = self.ring(st, "da_rec", 2, [128, 8], F32)
            o1_r = self.ring(st, "da_o1", 2, [128, 128], F32)
            o_r = self.ring(st, "da_o", 2, [128, 128], F32)
            oj = self.sb(st, "da_oj", [128, 128], F32)
            ojb = Buf()
            ss_r = self.ring(st, "da_ss", 4, [128, 4], F32)
            on_r = self.ring(st, "da_on", 8, [128, 128], BF16)
            oT_r = self.ring(st, "da_oT", 2, [128, 512], BF16)
            S = [self.ps(st, f"da_S{i}", [128, 2, 512]) for i in range(2)]
            Sb = [Buf(), Buf()]
            A = [self.ps(st, f"da_A{i}", [128, 512]) for i in range(3)]
            Ab = [Buf(), Buf(), Buf()]
            TPp = self.ps(st, "da_TP", [128, 512])
            TPb = Buf()
            TPbf = TPp[:, :].bitcast(BF16)
            pend = []

            def flush_pend():
                while pend:
                    pend.pop(0)()

            for (own, toff, T, TQ) in ((False, self.TS_off, self.TS, self.TS), (True, self.TP_off, self.TP, self.OWN)):
                NC_ = T // 128
                blkoff = toff // 128
                for h in range(DA_H):
                    kt, ktb = KT.next()
                    qt_, qtb = QT.next()
                    vh, vhb = VH.next()
                    m, hh = h // 2, h % 2
                    for i in range(2):
                        sl = 2 * hh + i
                        for ab in range(2):
                            prt = i * 64 + ab * 32
                            fw.dma(fw.sp, kt[prt:prt + 32, 0:T], self.DKT[2 * m + ab, sl * 32:sl * 32 + 32, toff:toff + T], writes=[ktb])
                            if own:
                                fw.dma(fw.sp, qt_[prt:prt + 32, 0:TQ], self.DQTo[2 * m + ab, sl * 32:sl * 32 + 32, 0:TQ], writes=[qtb])
                            else:
                                fw.dma(fw.sp, qt_[prt:prt + 32, 0:TQ], self.DQT[2 * m + ab, sl * 32:sl * 32 + 32, toff:toff + TQ], writes=[qtb])
                    fw.dma(fw.sp, vh[:, 0:NC_, :], self.DV[h, :, blkoff:blkoff + NC_, :], writes=[vhb])
                    for qi in range(TQ // 512):
                        q0 = qi * 512

                        def qk(kb):
                            S_, S_b = S[kb % 2], Sb[kb % 2]
                            for i in range(2):
                                fw.op(fw.pe, lambda: nc.tensor.matmul(S_[:, i, :], kt[64 * i:64 * i + 64, kb * 128:(kb + 1) * 128], qt_[64 * i:64 * i + 64, q0:q0 + 512],
                                                                      start=True, stop=True),
                                      reads=[ktb, qtb], writes=[S_b], inc=(i == 1))

                        qk(0)
                        qk(1)
                        for kb in range(NC_):
                            S_, S_b = S[kb % 2], Sb[kb % 2]
                            e, eb = E.next()
                            fw.op(fw.act, lambda: nc.scalar.activation(out=e[:, :, :], in_=S_[:, :, :], func=AF.Exp, scale=0.125), reads=[S_b], writes=[eb])
                            if kb + 2 < NC_:
                                qk(kb + 2)
                            for a in range(8):
                                i, s_ = a // 4, a % 4
                                bk, slot = a // 3, a % 3
                                for cc in range(2):
                                    fw.op(fw.pe, lambda: nc.tensor.matmul(A[bk][64 * cc:64 * cc + 64, 160 * slot:160 * slot + 129],
                                                                          e[:, i, 128 * s_ + 64 * cc:128 * s_ + 64 * cc + 64], vh[:, kb, :],
                                                                          start=(kb == 0 and slot == 0), stop=(kb == NC_ - 1), skip_group_check=True,
                                                                          tile_position=(0, 64 * cc)),
                                          reads=[eb, vhb], writes=[Ab[bk]], inc=(a == 7 and cc == 1))
                            if kb == 3:
                                flush_pend()
                        flush_pend()
                        ac, acb = accs.next()
                        for bk in range(3):
                            na_ = min(3, 8 - 3 * bk)
                            fw.op(fw.dve, lambda: nc.vector.tensor_copy(out=ac[:, 3 * bk:3 * bk + na_, :],
                                                                        in_=A[bk][:, 0:160 * na_].rearrange("p (s x) -> p s x", x=160)[:, :, 0:129]),
                                  reads=[Ab[bk]], writes=[acb])
                        rec, recb = rec_r.next()
                        fw.op(fw.dve, lambda: nc.vector.reciprocal(out=rec[:, :], in_=ac[:, :, 128]), reads=[acb], writes=[recb])
                        fw.op(fw.dve, lambda: nc.vector.tensor_scalar(out=rec[:, 4:8], in0=rec[:, 4:8], scalar1=lv[:, 4:5], scalar2=None, op0=ALU.mult),
                              reads=[recb, lvb], writes=[recb])
                        oT, oTb = oT_r.next()
                        ons = []
                        for s_ in range(4):
                            o1, o1b = o1_r.next()
                            o_, o_b = o_r.next()
                            ss, ssb = ss_r.next()
                            on, onb = on_r.next()
                            fw.op(fw.dve, lambda: nc.vector.tensor_scalar(out=o1[:], in0=ac[:, s_, 0:128], scalar1=rec[:, s_:s_ + 1], scalar2=None, op0=ALU.mult),
                                  reads=[acb, recb], writes=[o1b])
                            fw.op(fw.dve, lambda: nc.vector.scalar_tensor_tensor(out=o_[:], in0=ac[:, 4 + s_, 0:128], scalar=rec[:, 4 + s_:5 + s_], in1=o1[:], op0=ALU.mult, op1=ALU.add),
                                  reads=[acb, recb, o1b], writes=[o_b])
                            fw.op(fw.dve, lambda: nc.vector.scalar_tensor_tensor(out=oj[:], in0=o_[:], scalar=1.0, in1=o_[:], op0=ALU.mult, op1=ALU.mult, accum_out=ss[:, 0:1]),
                                  reads=[o_b], writes=[ojb, ssb])
                            self.rstd(ss, ssb, 128, 1.0 / 128, SUBLN_EPS, c)
                            fw.op(fw.dve, lambda: nc.vector.scalar_tensor_tensor(out=on[:], in0=o_[:], scalar=ss[:, 2:3], in1=gsub[:], op0=ALU.mult, op1=ALU.mult),
                                  reads=[o_b, ssb, gsubb], writes=[onb])
                            ons.append((on, onb))

                        def tail(ons=ons, oT=oT, oTb=oTb, q0=q0, h=h, toff=toff, own=own):
                            for s_, (on, onb) in enumerate(ons):
                                fw.op(fw.pe, lambda: nc.tensor.transpose(out=TPbf[:, 128 * s_:128 * s_ + 128], in_=on[:], identity=c["idb"][:]),
                                      reads=[onb, c["B"]], writes=[TPb], inc=(s_ == 3))
                            fw.op(fw.dve, lambda: nc.vector.tensor_copy(out=oT[:, :], in_=TPbf[:, 0:512]), reads=[TPb], writes=[oTb])
                            if own:
                                fw.dma(fw.sp, self.DOTo[h, :, q0:q0 + 512], oT[:], reads=[oTb])
                            else:
                                fw.dma(fw.sp, self.DOT[h, :, toff + q0:toff + q0 + 512], oT[:], reads=[oTb])
                        pend.append(tail)
            flush_pend()
            fw.barrier()

    def phase_da3(self):
        nc, fw = self.nc, self.fw
        with ExitStack() as st:
            c = self.consts(st)
            gbc, gbcb = self.load_gbc(st, "d3_gbc", 3)
            wo = self.sb(st, "d3_wo", [128, 8, D], BF16)
            wob = Buf()
            fw.dma(fw.pool, wo[:], self.da_wo.rearrange("(kc p) n -> p kc n", p=128), writes=[wob])
            be = self.make_backend(st, c, gbc, gbcb)
            oT = self.ring(st, "d3_oT", 2, [128, 8, 512], BF16)
            M = [self.ps(st, f"d3_M{i}", [128, 2, 512]) for i in range(2)]
            Mb = [Buf(), Buf()]
            mi = 0
            for (dot, src, dst, T) in ((self.DOT[:, :, self.TS_off:self.TS_off + self.TS], self.XB[self.TS_off:self.TS_off + self.TS, :], self.XC[self.TS_off:self.TS_off + self.TS, :], self.TS),
                                       (self.DOTo, self.XBo, self.XCo, self.OWN)):
                for sbi in range(T // 512):
                    t0 = 512 * sbi
                    o, ob = oT.next()
                    fw.dma(fw.sp, o[:], dot[:, :, t0:t0 + 512].rearrange("h p t -> p h t"), writes=[ob])
                    for b in range(4):
                        tk = t0 + 128 * b
                        xr, xrb = self.backend_prefetch(be, src[tk:tk + 128, :], 128)
                        M_, M_b = M[mi % 2], Mb[mi % 2]
                        mi += 1
                        for half in range(2):
                            for k in range(8):
                                fw.op(fw.pe, lambda k=k, half=half, M_=M_, b=b, o=o: nc.tensor.matmul(M_[:, half, :], o[:, k, 128 * b:128 * b + 128], wo[:, k, half * 512:(half + 1) * 512],
                                                                                                      start=(k == 0), stop=(k == 7)),
                                      reads=[ob, wob], writes=[M_b], inc=(k == 7 and half == 1))
                        self.backend(be, M_, M_b, 128, xr, xrb, dst[tk:tk + 128, :])
            fw.barrier()

    def build(self, phases="all"):
        fw = self.fw
        P = phases
        if P == "all" or "p0" in P:
            self.phase_p0()
        if P == "all" or "na1" in P:
            self.phase_proj("na", self.x)
        if P == "all" or "na2" in P:
            self.phase_na2(self.XA)
        if P == "all" or "ffn0" in P:
            self.phase_ffn(0, [(self.XA[o:o + T, :], self.XB[o:o + T, :], T) for (o, T) in self.seqs])
        if P == "all" or "da1" in P:
            self.phase_proj("da", self.XB)
        if P == "all" or "sel" in P:
            self.phase_select()
        if P == "all" or "da2" in P:
            self.phase_da2()
        if P == "all" or "da3" in P:
            self.phase_da3()
        if P == "all" or "ffn1" in P:
            self.phase_ffn(1, [(self.XC[self.TS_off:self.TS_off + self.TS, :], self.y[0:self.TS, :], self.TS),
                               (self.XCo, self.y[self.TS:self.TS + self.OWN, :], self.OWN)])
        fw.finish()
        return self.nc


def _na_bias_tables(rpb):
    R = 16
    NB = R // 2
    W = 64

    def table(j):
        kcs = Builder.na_kc(j, NB)
        out = np.full((NA_H, 128, len(kcs), 128), NEG, np.float32)
        rq = 2 * j + np.arange(128) // 64
        cq = np.arange(128) % 64
        r0 = np.clip(rq - 4, 0, R - 8)
        c0 = np.clip(cq - 8, 0, W - 16)
        for i, kc in enumerate(kcs):
            rk = 2 * kc + np.arange(128) // 64
            ck = np.arange(128) % 64
            valid = ((rk[:, None] >= r0[None, :]) & (rk[:, None] < r0[None, :] + 8) &
                     (ck[:, None] >= c0[None, :]) & (ck[:, None] < c0[None, :] + 16))
            dr = np.clip(rk[:, None] - rq[None, :] + 7, 0, 14)
            dc = np.clip(ck[:, None] - cq[None, :] + 15, 0, 30)
            vals = rpb[:, dr, dc]
            out[:, :, i, :] = np.where(valid[None], vals, np.float32(NEG))
        return out.reshape(NA_H, 128, len(kcs) * 128)

    interior = table(3)
    assert interior.shape[2] == 640
    sp = np.stack([table(0), table(1), table(NB - 2), table(NB - 1)], 0)
    assert sp.shape[3] == 512
    return np.ascontiguousarray(interior), np.ascontiguousarray(sp)


def _rot_tables(T):
    inv = (1.0 / (ROPE_THETA ** (np.arange(0, 64, 2, dtype=np.float32) / np.float32(64)))).astype(np.float32)
    ang = np.arange(T, dtype=np.float32)[None, :] * inv[np.arange(128) % 32][:, None]
    return np.cos(ang).astype(np.float32), np.sin(ang).astype(np.float32)


def _perm_cols():
    cols = []
    for m in range(4):
        for ab in range(2):
            for sl in range(4):
                for dl in range(32):
                    cols.append((4 * m + sl) * 64 + ab * 32 + dl)
    return np.array(cols)


def prepare_shared(inp, TMAX):
    f = lambda a: np.ascontiguousarray(np.asarray(a, dtype=np.float32))
    perm = _perm_cols()
    bi, bsp = _na_bias_tables(f(inp["na_rpb"])[0])
    cos, sin = _rot_tables(TMAX)
    sh = {
        "gvec": f(np.concatenate([inp["attn_pre_g"], inp["attn_post_g"], inp["ffn_pre_g"], inp["ffn_post_g"]], 0)),
        "w_qkv": f(inp["na_w_qkv"][0]), "na_wo": f(inp["na_w_o"][0]),
        "da_wq": f(np.asarray(inp["da_w_q"][0])[:, perm]), "da_wk": f(np.asarray(inp["da_w_k"][0])[:, perm]),
        "da_wv": f(inp["da_w_v"][0]), "da_wo": f(inp["da_w_o"][0]),
        "w_in": f(inp["ffn_w_in"]), "w_out": f(inp["ffn_w_out"]),
        "conv_w": f(inp["ffn_conv_w"]), "conv_b": f(inp["ffn_conv_b"]),
        "bias_int": bi, "bias_sp": bsp, "rot_cos": cos, "rot_sin": sin,
        "lam_in": f(np.stack([inp["da_lambda_q1"][0], inp["da_lambda_k1"][0], inp["da_lambda_q2"][0], inp["da_lambda_k2"][0]], 0)),
        "subln_g": f(inp["da_subln_g"][0]), "ident": np.eye(128, dtype=np.float32),
    }
    return sh


def kernel(**inp):
    xs = np.asarray(inp["x_sample"], dtype=np.float32)
    xp = np.asarray(inp["x_prompt"], dtype=np.float32)
    NS, TS, _ = xs.shape
    NP_, TP, _ = xp.shape
    ncore = 8
    assert NS == ncore and NP_ * 2 == ncore
    seqs = [(0, TS), (TS, TP)]
    bld = Builder(seqs)
    nc = bld.build()
    sh = prepare_shared(inp, max(TS, TP))
    in_maps = []
    for cidx in range(ncore):
        m = dict(sh)
        m["x"] = np.ascontiguousarray(np.concatenate([xs[cidx], xp[cidx // 2]], 0))
        m["fsel"] = np.ascontiguousarray(np.tile(np.array([[1.0, 0.0]] if cidx % 2 == 0 else [[0.0, 1.0]], np.float32), (128, 1)))
        in_maps.append(m)
    res = run_bass_kernel_spmd(nc, in_maps, core_ids=list(range(ncore)))
    y_s = np.stack([res.results[cidx]["y"][:TS] for cidx in range(ncore)], 0)
    H2 = TP // 2
    y_p = np.stack([np.concatenate([res.results[2 * p]["y"][TS:TS + H2], res.results[2 * p + 1]["y"][TS + 512:TS + 512 + H2]], 0) for p in range(NP_)], 0)
    return (y_p.astype(np.float32), y_s.astype(np.float32))
```

```python
import math
from contextlib import ExitStack
import numpy as np
import ml_dtypes
import concourse.bass as bass
import concourse.mybir as mybir
from concourse.bass_utils import run_bass_kernel_spmd

F32 = mybir.dt.float32
BF16 = mybir.dt.bfloat16
AF = mybir.ActivationFunctionType
ALU = mybir.AluOpType

D = 1024
DFF = 2816
NFC = 22
NA_H = 16
DA_H = 8
NEG = -30000.0
NORM_EPS = 1e-6
SUBLN_EPS = 1e-5
LAMBDA_INIT = 0.8 - 0.6 * math.exp(-0.3 * 1)
ROPE_THETA = 10000.0
import os as _os
SEQ_FFN = bool(int(_os.environ.get('SEQ_FFN', '0')))


class Buf:
    __slots__ = ("name", "w", "r")

    def __init__(self, name=""):
        self.name = name
        self.w = None
        self.r = {}


class Eng:
    def __init__(self, nc, h, name):
        self.h = h
        self.name = name
        self.sem = nc.alloc_semaphore(name="s_" + name)
        self.count = 0
        self.known = {}


class FW:
    NDMASEM = 48

    def __init__(self, nc):
        self.nc = nc
        self.pe = Eng(nc, nc.tensor, "pe")
        self.act = Eng(nc, nc.scalar, "act")
        self.dve = Eng(nc, nc.vector, "dve")
        self.pool = Eng(nc, nc.gpsimd, "pool")
        self.sp = Eng(nc, nc.sync, "sp")
        self.engs = [self.pe, self.act, self.dve, self.pool, self.sp]
        self.dsems = [nc.alloc_semaphore(name=f"d{i}") for i in range(self.NDMASEM)]
        self.dval = [0] * self.NDMASEM
        self.dnext = 0
        self.nops = 0
        self.ndma = 0
        self.rec = None
        self.pending = []

    def wait(self, eng, tok):
        if tok is None:
            return
        sem, val = tok
        if sem is eng.sem:
            if eng is self.pe:
                return
            if val > eng.count:
                return
        k = id(sem)
        if eng.known.get(k, 0) >= val:
            return
        eng.known[k] = val
        self.pending.append((sem, val))

    def _flush(self, eng, keep_last):
        p, self.pending = self.pending, []
        last = None
        if keep_last and p and eng is not self.pe:
            last = p.pop()
        for sem, val in p:
            eng.h.wait_ge(sem, val)
            if self.rec is not None:
                self.rec[eng.name].append(("w", id(sem), val))
        return last

    def _embed(self, eng, ins, last):
        if last is not None:
            ins._wait_ge(last[0], last[1])
            if self.rec is not None:
                self.rec[eng.name].append(("w", id(last[0]), last[1]))

    def _deps(self, eng, reads, writes):
        for b in reads:
            self.wait(eng, b.w)
        for b in writes:
            if b.w is not None and b.w[0] is not eng.sem:
                self.wait(eng, b.w)
            for tok in list(b.r.values()):
                self.wait(eng, tok)

    def _commit(self, tok, reads, writes):
        k = id(tok[0])
        for b in reads:
            if k not in b.r or b.r[k][1] < tok[1]:
                b.r[k] = tok
        for b in writes:
            b.w = tok
            b.r = {}

    def op(self, eng, fn, reads=(), writes=(), inc=True):
        self._deps(eng, reads, writes)
        last = self._flush(eng, True)
        ins = fn()
        self._embed(eng, ins, last)
        self.nops += 1
        if inc:
            eng.count += 1
            ins.then_inc(eng.sem, 1)
            tok = (eng.sem, eng.count)
            if self.rec is not None:
                self.rec[eng.name].append(("i", id(eng.sem), 1))
        else:
            tok = (eng.sem, eng.count + 1)
        self._commit(tok, reads, writes)
        return tok

    def dma(self, eng, out, in_, reads=(), writes=(), **kw):
        self._deps(eng, reads, writes)
        i = self.dnext
        self.dnext = (self.dnext + 1) % self.NDMASEM
        sem = self.dsems[i]
        if self.dval[i] > 0:
            self.wait(eng, (sem, self.dval[i]))
        last = self._flush(eng, True)
        self.dval[i] += 16
        ins = eng.h.dma_start(out=out, in_=in_, **kw)
        self._embed(eng, ins, last)
        ins.then_inc(sem, 16)
        self.ndma += 1
        if self.rec is not None:
            self.rec[eng.name].append(("i", id(sem), 16))
        tok = (sem, self.dval[i])
        self._commit(tok, reads, writes)
        return tok

    def barrier(self):
        for e in self.engs:
            for o in self.engs:
                if o is not e and o.count > 0:
                    self.wait(e, (o.sem, o.count))
            for i, sem in enumerate(self.dsems):
                if self.dval[i] > 0:
                    self.wait(e, (sem, self.dval[i]))
            self._flush(e, False)

    def finish(self):
        for i, sem in enumerate(self.dsems):
            if self.dval[i] > 0:
                self.wait(self.sp, (sem, self.dval[i]))
        for e in (self.pe, self.act, self.dve, self.pool):
            if e.count > 0:
                self.wait(self.sp, (e.sem, e.count))
        self._flush(self.sp, False)


class Ring:
    def __init__(self, tiles):
        self.tiles = tiles
        self.bufs = [Buf() for _ in tiles]
        self.i = -1

    def next(self):
        self.i = (self.i + 1) % len(self.tiles)
        return self.tiles[self.i], self.bufs[self.i]

    def cur(self):
        return self.tiles[self.i], self.bufs[self.i]


class Builder:
    def __init__(self, seqs, debug=False):
        self.seqs = seqs
        self.NT = sum(T for _, T in seqs)
        self.TMAX = max(T for _, T in seqs)
        self.debug = debug
        self.nc = nc = bass.Bass("TRN2", target_bir_lowering=False)
        self.fw = FW(nc)
        NT = self.NT
        dt = nc.dram_tensor
        I = "ExternalInput"
        self.x = dt("x", [NT, D], F32, kind=I).ap()
        self.gvec = dt("gvec", [8, D], F32, kind=I).ap()
        self.w_qkv = dt("w_qkv", [D, 3 * D], F32, kind=I).ap()
        self.na_wo = dt("na_wo", [D, D], F32, kind=I).ap()
        self.da_wq = dt("da_wq", [D, D], F32, kind=I).ap()
        self.da_wk = dt("da_wk", [D, D], F32, kind=I).ap()
        self.da_wv = dt("da_wv", [D, D], F32, kind=I).ap()
        self.da_wo = dt("da_wo", [D, D], F32, kind=I).ap()
        self.w_in = dt("w_in", [2, D, 2 * DFF], F32, kind=I).ap()
        self.w_out = dt("w_out", [2, DFF, D], F32, kind=I).ap()
        self.conv_w = dt("conv_w", [2, 3, 2 * DFF], F32, kind=I).ap()
        self.conv_b = dt("conv_b", [2, 2 * DFF], F32, kind=I).ap()
        self.bias_int = dt("bias_int", [NA_H, 128, 640], F32, kind=I).ap()
        self.bias_sp = dt("bias_sp", [4, NA_H, 128, 512], F32, kind=I).ap()
        self.rot_cos = dt("rot_cos", [128, self.TMAX], F32, kind=I).ap()
        self.rot_sin = dt("rot_sin", [128, self.TMAX], F32, kind=I).ap()
        self.lam_in = dt("lam_in", [4, 64], F32, kind=I).ap()
        self.subln_g = dt("subln_g", [128], F32, kind=I).ap()
        self.ident = dt("ident", [128, 128], F32, kind=I).ap()
        self.y = dt("y", [seqs[0][1] + seqs[1][1] // 2 + 512, D], F32, kind="ExternalOutput").ap()
        S = "ExternalOutput" if debug else "Internal"
        self.XA = dt("XA", [NT, D], F32, kind=S).ap()
        self.XB = dt("XB", [NT, D], F32, kind=S).ap()
        self.WINB = dt("WINB", [2, 11, 128, 8, 512], BF16, kind="Internal").ap()
        NBLK = NT // 128
        self.NQT = dt("NQT", [NBLK, 128, 8, 128], BF16, kind="Internal").ap()
        self.NKT = dt("NKT", [NBLK, 128, 8, 128], BF16, kind="Internal").ap()
        self.NV = dt("NV", [128, NBLK, NA_H * 65], BF16, kind="Internal").ap()
        self.DQT = dt("DQT", [8, 128, NT], BF16, kind="Internal").ap()
        self.DKT = dt("DKT", [8, 128, NT], BF16, kind="Internal").ap()
        self.DV = dt("DV", [DA_H, 128, NBLK, 129], BF16, kind="Internal").ap()
        self.DOT = dt("DOT", [DA_H, 128, NT], BF16, kind="Internal").ap()
        self.XC = self.XA
        (self.TS_off, self.TS), (self.TP_off, self.TP) = seqs
        self.OWN = self.TP // 2 + 512
        self.OFF1 = self.TP - self.OWN
        self.fsel = dt("fsel", [128, 2], F32, kind=I).ap()
        self.DQTo = dt("DQTo", [8, 128, self.OWN], BF16, kind="Internal").ap()
        self.DOTo = dt("DOTo", [DA_H, 128, self.OWN], BF16, kind="Internal").ap()
        self.XBo = dt("XBo", [self.OWN, D], F32, kind="Internal").ap()
        self.XCo = dt("XCo", [self.OWN, D], F32, kind="Internal").ap()

    _uid = 0

    def _nm(self, name):
        Builder._uid += 1
        return f"{name}_{Builder._uid}"

    def sb(self, st, name, shape, dtype):
        return st.enter_context(self.nc.sbuf_tensor(self._nm(name), shape, dtype))

    def ps(self, st, name, shape, dtype=F32):
        return st.enter_context(self.nc.psum_tensor(self._nm(name), shape, dtype))

    def ring(self, st, name, n, shape, dtype):
        return Ring([self.sb(st, f"{name}{i}", shape, dtype) for i in range(n)])

    def consts(self, st):
        nc, fw = self.nc, self.fw
        c = {}
        c["idf"] = self.sb(st, "idf", [128, 128], F32)
        c["idb"] = self.sb(st, "idb", [128, 128], BF16)
        c["nh"] = self.sb(st, "nhalf", [128, 1], F32)
        c["B"] = Buf("consts")
        fw.dma(fw.sp, c["idf"][:], self.ident[:, :], writes=[c["B"]])
        fw.dma(fw.pool, c["idb"][:], self.ident[:, :], writes=[c["B"]])
        fw.op(fw.pool, lambda: nc.gpsimd.memset(c["nh"][:], -0.5), writes=[c["B"]])
        return c

    def load_gfm(self, st, name, row):
        nc, fw = self.nc, self.fw
        t = self.sb(st, name, [128, 8], F32)
        b = Buf(name)
        with nc.allow_non_contiguous_dma(reason="tiny gain vector"):
            fw.dma(fw.sp, t[:], self.gvec[row, :].rearrange("(c p) -> p c", p=128), writes=[b])
        return t, b

    def load_gbc(self, st, name, row):
        fw = self.fw
        t = self.sb(st, name, [128, D], F32)
        b = Buf(name)
        fw.dma(fw.sp, t[:], self.gvec[row, :].partition_broadcast(128), writes=[b])
        return t, b

    def rstd(self, ss, ssb, n, inv_dim, eps, c):
        nc, fw = self.nc, self.fw
        fw.op(fw.pool, lambda: nc.gpsimd.tensor_scalar(out=ss[:n, 1:2], in0=ss[:n, 0:1], scalar1=inv_dim, scalar2=eps,
                                                        op0=ALU.mult, op1=ALU.add), reads=[ssb], writes=[ssb])
        fw.op(fw.pool, lambda: nc.gpsimd.tensor_tensor(out=ss[:n, 2:3], in0=ss[:n, 1:2], in1=c["nh"][:n, :], op=ALU.pow),
              reads=[ssb, c["B"]], writes=[ssb])

    def make_frontend(self, st, c, gfm, gfmb, tp, tpb):
        fe = {"xt": self.ring(st, "fe_xt", 2, [128, D], F32), "xs": self.ring(st, "fe_xs", 3, [128, D], F32),
              "junk": self.sb(st, "fe_junk", [128, D], BF16), "junkb": Buf(),
              "ss": self.ring(st, "fe_ss", 4, [128, 4], F32), "c": c, "gfm": gfm, "gfmb": gfmb, "tp": tp, "tpb": tpb}
        return fe

    def frontend_a(self, fe, src, n):
        nc, fw = self.nc, self.fw
        c = fe["c"]
        xt, xtb = fe["xt"].next()
        xs, xsb = fe["xs"].next()
        ss, ssb = fe["ss"].next()
        fw.dma(fw.sp, xt[:n, :], src, writes=[xtb])
        fw.op(fw.act, lambda: nc.scalar.activation(out=fe["junk"][:n, :], in_=xt[:n, :], func=AF.Square, accum_out=ss[:n, 0:1]),
              reads=[xtb], writes=[fe["junkb"], ssb])
        self.rstd(ss, ssb, n, 1.0 / D, NORM_EPS, c)
        fw.op(fw.act, lambda: nc.scalar.activation(out=xs[:n, :], in_=xt[:n, :], func=AF.Copy, scale=ss[:n, 2:3]),
              reads=[xtb, ssb], writes=[xsb])
        return xs, xsb

    def frontend_b(self, fe, xs, xsb, n, hT, hTb, col0):
        nc, fw = self.nc, self.fw
        c = fe["c"]
        tp, tpb = fe["tp"], fe["tpb"]
        for k in range(8):
            fw.op(fw.pe, lambda: nc.tensor.transpose(out=tp[:, k, :n], in_=xs[:n, k * 128:(k + 1) * 128], identity=c["idf"][:n, :n]),
                  reads=[xsb, c["B"]], writes=[tpb], inc=(k == 7))
        fw.op(fw.dve, lambda: nc.vector.tensor_tensor(out=hT[:, :, col0:col0 + n], in0=tp[:, :, :n],
                                                      in1=fe["gfm"][:, :].unsqueeze(2).to_broadcast([128, 8, n]), op=ALU.mult),
              reads=[tpb, fe["gfmb"]], writes=[hTb])

    def frontend(self, fe, src, n, hT, hTb, col0):
        xs, xsb = self.frontend_a(fe, src, n)
        self.frontend_b(fe, xs, xsb, n, hT, hTb, col0)

    def make_backend(self, st, c, gbc, gbcb):
        be = {"xr": self.ring(st, "be_xr", 2, [128, D], F32), "t": self.ring(st, "be_t", 2, [128, D], F32),
              "o": self.ring(st, "be_o", 2, [128, D], F32), "junk": self.sb(st, "be_junk", [128, D], BF16), "junkb": Buf(),
              "ss": self.ring(st, "be_ss", 4, [128, 4], F32), "c": c, "gbc": gbc, "gbcb": gbcb}
        return be

    def backend_prefetch(self, be, res_src, n):
        xr, xrb = be["xr"].next()
        self.fw.dma(self.fw.sp, xr[:n, :], res_src, writes=[xrb])
        return xr, xrb

    def backend(self, be, m, mb, n, xr, xrb, dst):
        nc, fw = self.nc, self.fw
        ss, ssb = be["ss"].next()
        t, tb = be["t"].next()
        o, ob = be["o"].next()
        m2 = m[:n, :, :].rearrange("p a b -> p (a b)")
        fw.op(fw.act, lambda: nc.scalar.activation(out=be["junk"][:n, :], in_=m2, func=AF.Square, accum_out=ss[:n, 0:1]),
              reads=[mb], writes=[be["junkb"], ssb])
        self.rstd(ss, ssb, n, 1.0 / D, NORM_EPS, be["c"])
        fw.op(fw.dve, lambda: nc.vector.scalar_tensor_tensor(out=t[:n, :], in0=m2, scalar=ss[:n, 2:3], in1=be["gbc"][:n, :],
                                                             op0=ALU.mult, op1=ALU.mult), reads=[mb, ssb, be["gbcb"]], writes=[tb])
        fw.op(fw.pool, lambda: nc.gpsimd.tensor_tensor(out=o[:n, :], in0=t[:n, :], in1=xr[:n, :], op=ALU.add),
              reads=[tb, xrb], writes=[ob])
        fw.dma(fw.sp, dst, o[:n, :], reads=[ob])

    def phase_p0(self):
        nc, fw = self.nc, self.fw
        with ExitStack() as st:
            stg = self.ring(st, "p0_stg", 3, [128, 8, 512], BF16)
            for l in range(2):
                for g in range(11):
                    t, tb = stg.next()
                    src = self.w_in[l]
                    fw.dma(fw.pool, t[:, :, 0:256], src[:, 256 * g:256 * g + 256].rearrange("(kc p) n -> p kc n", p=128), writes=[tb])
                    fw.dma(fw.pool, t[:, :, 256:512], src[:, DFF + 256 * g:DFF + 256 * g + 256].rearrange("(kc p) n -> p kc n", p=128), writes=[tb])
                    fw.dma(fw.sp, self.WINB[l, g], t[:], reads=[tb])
            fw.barrier()

    def phase_ffn(self, l, seqlist):
        nc, fw = self.nc, self.fw
        with ExitStack() as st:
            c = self.consts(st)
            gfm, gfmb = self.load_gfm(st, "ffn_gfm", 4 + l)
            gbc, gbcb = self.load_gbc(st, "ffn_gbc", 6 + l)
            wout = self.sb(st, "ffn_wout", [128, NFC, D], BF16)
            woutb = Buf()
            for f0 in range(0, NFC, 6):
                f1 = min(NFC, f0 + 6)
                fw.dma(fw.pool, wout[:, f0:f1, :], self.w_out[l, f0 * 128:f1 * 128, :].rearrange("(f p) n -> p f n", p=128), writes=[woutb])
            cw = self.sb(st, "ffn_cw", [128, 3, 2 * NFC], F32)
            cb = self.sb(st, "ffn_cb", [128, 2 * NFC], F32)
            cwb = Buf()
            with nc.allow_non_contiguous_dma(reason="small conv params"):
                for k in range(3):
                    fw.dma(fw.sp, cw[:, k, :], self.conv_w[l, k, :].rearrange("(c p) -> p c", p=128), writes=[cwb])
                fw.dma(fw.sp, cb[:, :], self.conv_b[l, :].rearrange("(c p) -> p c", p=128), writes=[cwb])
            tp = self.ps(st, "ffn_tp", [128, 8, 128])
            tpb = Buf()
            pgu = [self.ps(st, f"ffn_pgu{i}", [128, 2, 512]) for i in range(2)]
            pgub = [[Buf(), Buf()] for _ in range(2)]
            pm = self.ps(st, "ffn_pm", [128, 2, 512])
            pmb = Buf()
            fe = self.make_frontend(st, c, gfm, gfmb, tp, tpb)
            be = self.make_backend(st, c, gbc, gbcb)
            hT = self.ring(st, "ffn_hT", 2, [128, 8, 512], BF16)
            wg = self.ring(st, "ffn_wg", 3, [128, 8, 512], BF16)
            aT = self.ring(st, "ffn_aT", 2, [128, NFC, 513], BF16)
            carry = self.sb(st, "ffn_carry", [128, 2 * NFC, 2], F32)
            carryb = [Buf() for _ in range(2 * NFC)]
            T1 = self.ring(st, "ffn_T1", 2, [128, 513], F32)
            T2 = self.ring(st, "ffn_T2", 2, [128, 513], F32)
            TG = self.ring(st, "ffn_TG", 2, [128, 513], F32)
            TU = self.ring(st, "ffn_TU", 2, [128, 513], F32)
            GG = self.ring(st, "ffn_GG", 2, [128, 513], F32)

            tiles = []
            for (src_, dst_, T) in seqlist:
                nt_ = T // 512
                for i in range(nt_):
                    tiles.append(dict(toff=0, src=src_, dst=dst_, i=i, first=(i == 0), last=(i == nt_ - 1)))
            NTL = len(tiles)
            for n, tl in enumerate(tiles):
                tl["h"], tl["hb"] = None, None
                W = 513 if tl["last"] else 512
                j = 1 if tl["first"] else 0
                blocks = []
                while j < 512:
                    nn = min(128, 512 - j)
                    blocks.append((j, nn))
                    j += nn
                if W == 513:
                    blocks.append((512, 1))
                tl["W"], tl["blocks"] = W, blocks
            xs_pend = {}

            def A1(n, b):
                if n >= NTL:
                    return
                tl = tiles[n]
                if b == 0:
                    tl["h"], tl["hb"] = hT.next()
                t0 = tl["toff"] + 512 * tl["i"] + 128 * b
                xs_pend[(n, b)] = self.frontend_a(fe, tl["src"][t0:t0 + 128, :], 128)

            def A2(n, b):
                if n >= NTL:
                    return
                tl = tiles[n]
                xs, xsb = xs_pend.pop((n, b))
                self.frontend_b(fe, xs, xsb, 128, tl["h"], tl["hb"], 128 * b)

            def C(n, bi):
                if n < 0 or bi >= len(tiles[n]["blocks"]):
                    return
                tl = tiles[n]
                a_, a_b = tl["aT"], tl["aTb"]
                j, nn = tl["blocks"][bi]
                tk = tl["toff"] + 512 * tl["i"] - 1 + j
                src, dst = tl["src"], tl["dst"]
                xr, xrb = self.backend_prefetch(be, src[tk:tk + nn, :], nn)
                for half in range(2):
                    for f in range(NFC):
                        fw.op(fw.pe, lambda: nc.tensor.matmul(pm[:nn, half, :], a_[:, f, j:j + nn], wout[:, f, half * 512:(half + 1) * 512],
                                                              start=(f == 0), stop=(f == NFC - 1)),
                              reads=[a_b, woutb], writes=[pmb], inc=(f == NFC - 1 and half == 1))
                self.backend(be, pm, pmb, nn, xr, xrb, dst[tk:tk + nn, :])

            def Bgroup(n, g, wtile):
                tl = tiles[n]
                h, hb = tl["h"], tl["hb"]
                a_, a_b = tl["aT"], tl["aTb"]
                W = tl["W"]
                w_, wb_ = wtile
                for jf in range(2):
                    f = 2 * g + jf
                    P_ = pgu[f % 2]
                    Pb = pgub[f % 2]
                    t3s = []
                    for gu in range(2):
                        cidx = f + NFC * gu
                        col = 256 * gu + 128 * jf
                        for kc in range(8):
                            fw.op(fw.pe, lambda: nc.tensor.matmul(P_[:, gu, :], w_[:, kc, col:col + 128], h[:, kc, :], start=(kc == 0), stop=(kc == 7)),
                                  reads=[wb_, hb], writes=[Pb[gu]], inc=(kc == 7))
                        Pg = P_[:, gu, :]
                        t1, t1b = T1.next()
                        t2, t2b = T2.next()
                        t3, t3b = (TG if gu == 0 else TU).next()
                        sc1, sc0, sc2, bi_ = cw[:, 1, cidx:cidx + 1], cw[:, 0, cidx:cidx + 1], cw[:, 2, cidx:cidx + 1], cb[:, cidx:cidx + 1]
                        fw.op(fw.act, lambda: nc.scalar.activation(out=t1[:, 1:W], in_=Pg[:, 0:W - 1], func=AF.Identity, scale=sc1, bias=bi_),
                              reads=[Pb[gu], cwb], writes=[t1b])
                        fw.op(fw.act, lambda: nc.scalar.activation(out=t1[:, 0:1], in_=carry[:, cidx, 1:2], func=AF.Identity, scale=sc1, bias=bi_),
                              reads=[carryb[cidx], cwb], writes=[t1b])
                        fw.op(fw.dve, lambda: nc.vector.scalar_tensor_tensor(out=t2[:, 2:W], in0=Pg[:, 0:W - 2], scalar=sc0, in1=t1[:, 2:W], op0=ALU.mult, op1=ALU.add),
                              reads=[Pb[gu], t1b, cwb], writes=[t2b])
                        fw.op(fw.dve, lambda: nc.vector.scalar_tensor_tensor(out=t2[:, 0:2], in0=carry[:, cidx, :], scalar=sc0, in1=t1[:, 0:2], op0=ALU.mult, op1=ALU.add),
                              reads=[carryb[cidx], t1b, cwb], writes=[t2b])
                        fw.op(fw.dve, lambda: nc.vector.scalar_tensor_tensor(out=t3[:, 0:512], in0=Pg[:, 0:512], scalar=sc2, in1=t2[:, 0:512], op0=ALU.mult, op1=ALU.add),
                              reads=[Pb[gu], t2b, cwb], writes=[t3b])
                        if W == 513:
                            fw.op(fw.dve, lambda: nc.vector.tensor_copy(out=t3[:, 512:513], in_=t2[:, 512:513]), reads=[t2b], writes=[t3b])
                        fw.op(fw.dve, lambda: nc.vector.tensor_copy(out=carry[:, cidx, :], in_=Pg[:, 510:512]), reads=[Pb[gu]], writes=[carryb[cidx]])
                        t3s.append((t3, t3b))
                    (tg, tgb), (tu, tub) = t3s
                    gg, ggb = GG.next()
                    fw.op(fw.act, lambda: nc.scalar.activation(out=gg[:, 0:W], in_=tg[:, 0:W], func=AF.Gelu_apprx_tanh), reads=[tgb], writes=[ggb])
                    fw.op(fw.pool, lambda: nc.gpsimd.tensor_tensor(out=a_[:, f, 0:W], in0=gg[:, 0:W], in1=tu[:, 0:W], op=ALU.mult),
                          reads=[ggb, tub], writes=[a_b])

            for b in range(4):
                A1(0, b)
                A2(0, b)
            for n, tl in enumerate(tiles):
                tl["aT"], tl["aTb"] = aT.next()
                if tl["first"]:
                    fw.op(fw.pool, lambda: nc.gpsimd.memset(carry[:, :, :], 0.0), writes=carryb)
                wq = []
                for g in range(2):
                    w_, wb_ = wg.next()
                    fw.dma(fw.sp, w_[:], self.WINB[l, g], writes=[wb_])
                    wq.append((w_, wb_))
                for g in range(11):
                    if g + 2 < 11:
                        w_, wb_ = wg.next()
                        fw.dma(fw.sp, w_[:], self.WINB[l, g + 2], writes=[wb_])
                        wq.append((w_, wb_))
                    Bgroup(n, g, wq[g])
                    if SEQ_FFN:
                        if g == 10:
                            for b in range(4):
                                A1(n + 1, b)
                                A2(n + 1, b)
                            for bi in range(5):
                                C(n, bi)
                        continue
                    if g in (0, 2, 4, 6):
                        A1(n + 1, g // 2)
                    if g in (1, 3, 5, 7):
                        C(n - 1, (g - 1) // 2)
                        A2(n + 1, (g - 1) // 2)
                    if g == 8:
                        C(n - 1, 4)
            if not SEQ_FFN:
                for bi in range(5):
                    C(NTL - 1, bi)
            fw.barrier()

    def phase_proj(self, mode, src):
        nc, fw = self.nc, self.fw
        na = (mode == "na")
        with ExitStack() as st:
            c = self.consts(st)
            gfm, gfmb = self.load_gfm(st, "pj_gfm", 0 if na else 1)
            w = self.sb(st, "pj_w", [128, 8, 3 * D], BF16)
            wb = Buf()
            if na:
                for k3 in range(3):
                    fw.dma(fw.pool, w[:, :, k3 * D:(k3 + 1) * D], self.w_qkv[:, k3 * D:(k3 + 1) * D].rearrange("(kc p) n -> p kc n", p=128), writes=[wb])
            else:
                for k3, ww in enumerate((self.da_wq, self.da_wk, self.da_wv)):
                    fw.dma(fw.pool, w[:, :, k3 * D:(k3 + 1) * D], ww.rearrange("(kc p) n -> p kc n", p=128), writes=[wb])
            tp = self.ps(st, "pj_tp", [128, 8, 128])
            tpb = Buf()
            pp = [self.ps(st, f"pj_pp{i}", [128, 2, 512]) for i in range(3)]
            ppb = [[Buf(), Buf()] for _ in range(3)]
            fe = self.make_frontend(st, c, gfm, gfmb, tp, tpb)
            hT = self.ring(st, "pj_hT", 2, [128, 8, 512], BF16)
            if na:
                stq = self.ring(st, "pj_stq", 2, [128, 4, 8, 128], BF16)
                stk = self.ring(st, "pj_stk", 2, [128, 4, 8, 128], BF16)
                vaug = self.ring(st, "pj_vaug", 2, [128, 4, NA_H, 65], BF16)
                for t_, b_ in zip(vaug.tiles, vaug.bufs):
                    fw.op(fw.pool, lambda t_=t_: nc.gpsimd.memset(t_[:, :, :, 64:65], 1.0), writes=[b_])
            else:
                stq = self.ring(st, "pj_stq", 2, [128, 8, 512], BF16)
                stk = self.ring(st, "pj_stk", 2, [128, 8, 512], BF16)
                vaug = self.ring(st, "pj_vaug", 2, [128, DA_H, 4, 129], BF16)
                for t_, b_ in zip(vaug.tiles, vaug.bufs):
                    fw.op(fw.pool, lambda t_=t_: nc.gpsimd.memset(t_[:, :, :, 128:129], 1.0), writes=[b_])
                cosr = self.ring(st, "pj_cos", 2, [128, 512], F32)
                sinr = self.ring(st, "pj_sin", 2, [128, 512], F32)
                rt = [self.ring(st, f"pj_rt{i}", 2, [128, 512], F32) for i in range(4)]
            sbs = []
            for (toff, T) in self.seqs:
                for sbi in range(T // 512):
                    sbs.append(dict(t0=toff + 512 * sbi, sbi=sbi))
            xs_pend = {}

            def A1(n, b):
                if n >= len(sbs):
                    return
                sb_ = sbs[n]
                if b == 0:
                    sb_["h"], sb_["hb"] = hT.next()
                t0 = sb_["t0"] + 128 * b
                xs_pend[(n, b)] = self.frontend_a(fe, src[t0:t0 + 128, :], 128)

            def A2(n, b):
                if n >= len(sbs):
                    return
                sb_ = sbs[n]
                xs, xsb = xs_pend.pop((n, b))
                self.frontend_b(fe, xs, xsb, 128, sb_["h"], sb_["hb"], 128 * b)

            pis = [0]

            def items_for(n):
                sb_ = sbs[n]
                t0, sbi = sb_["t0"], sb_["sbi"]
                blk0 = t0 // 128
                h, hb = sb_["h"], sb_["hb"]
                items = []
                st_ = {}
                if not na:
                    def ld():
                        st_["cs"], st_["csb"] = cosr.next()
                        st_["sn"], st_["snb"] = sinr.next()
                        fw.dma(fw.sp, st_["cs"][:], self.rot_cos[:, 512 * sbi:512 * sbi + 512], writes=[st_["csb"]])
                        fw.dma(fw.sp, st_["sn"][:], self.rot_sin[:, 512 * sbi:512 * sbi + 512], writes=[st_["snb"]])
                    items.append(ld)
                for qk in range(2):
                    def start_qk(qk=qk):
                        st_["sg"], st_["sgb"] = (stq if qk == 0 else stk).next()
                    items.append(start_qk)
                    if na:
                        for oc in range(8):
                            def it(qk=qk, oc=oc):
                                sg, sgb = st_["sg"], st_["sgb"]
                                P_, Pb = pp[pis[0] % 3], ppb[pis[0] % 3]
                                pis[0] += 1
                                col = qk * D + oc * 128
                                for kc in range(8):
                                    fw.op(fw.pe, lambda: nc.tensor.matmul(P_[:, 0, :], w[:, kc, col:col + 128], h[:, kc, :], start=(kc == 0), stop=(kc == 7)),
                                          reads=[wb, hb], writes=[Pb[0]], inc=(kc == 7))
                                if oc % 2 == 0:
                                    fw.op(fw.act, lambda: nc.scalar.activation(out=sg[:, :, oc, :], in_=P_[:, 0, :].rearrange("p (b t) -> p b t", b=4), func=AF.Copy),
                                          reads=[Pb[0]], writes=[sgb])
                                else:
                                    fw.op(fw.dve, lambda: nc.vector.tensor_copy(out=sg[:, :, oc, :], in_=P_[:, 0, :].rearrange("p (b t) -> p b t", b=4)),
                                          reads=[Pb[0]], writes=[sgb])
                            items.append(it)

                        def fin(qk=qk):
                            dstT = (self.NQT if qk == 0 else self.NKT)
                            fw.dma(fw.sp, dstT[blk0:blk0 + 4].rearrange("b p o t -> p b o t"), st_["sg"][:], reads=[st_["sgb"]])
                        items.append(fin)
                    else:
                        for m in range(4):
                            def it(qk=qk, m=m):
                                sg, sgb = st_["sg"], st_["sgb"]
                                cs, csb, sn, snb = st_["cs"], st_["csb"], st_["sn"], st_["snb"]
                                P_, Pb = pp[pis[0] % 3], ppb[pis[0] % 3]
                                pis[0] += 1
                                for ab in range(2):
                                    col = qk * D + m * 256 + ab * 128
                                    for kc in range(8):
                                        fw.op(fw.pe, lambda: nc.tensor.matmul(P_[:, ab, :], w[:, kc, col:col + 128], h[:, kc, :], start=(kc == 0), stop=(kc == 7)),
                                              reads=[wb, hb], writes=[Pb[ab]], inc=(kc == 7))
                                t1, t1b = rt[0].next()
                                t2, t2b = rt[1].next()
                                t3, t3b = rt[2].next()
                                t4, t4b = rt[3].next()
                                TT = nc.vector.tensor_tensor
                                fw.op(fw.dve, lambda: TT(out=t1[:], in0=P_[:, 0, :], in1=cs[:], op=ALU.mult), reads=[Pb[0], csb], writes=[t1b])
                                fw.op(fw.dve, lambda: TT(out=t2[:], in0=P_[:, 1, :], in1=sn[:], op=ALU.mult), reads=[Pb[1], snb], writes=[t2b])
                                fw.op(fw.dve, lambda: TT(out=t3[:], in0=P_[:, 1, :], in1=cs[:], op=ALU.mult), reads=[Pb[1], csb], writes=[t3b])
                                fw.op(fw.dve, lambda: TT(out=t4[:], in0=P_[:, 0, :], in1=sn[:], op=ALU.mult), reads=[Pb[0], snb], writes=[t4b])
                                fw.op(fw.pool, lambda: nc.gpsimd.tensor_tensor(out=sg[:, 2 * m, :], in0=t1[:], in1=t2[:], op=ALU.subtract), reads=[t1b, t2b], writes=[sgb])
                                fw.op(fw.pool, lambda: nc.gpsimd.tensor_tensor(out=sg[:, 2 * m + 1, :], in0=t3[:], in1=t4[:], op=ALU.add), reads=[t3b, t4b], writes=[sgb])
                            items.append(it)

                        def fin(qk=qk):
                            dstT = (self.DQT if qk == 0 else self.DKT)
                            fw.dma(fw.sp, dstT[:, :, t0:t0 + 512].rearrange("o p t -> p o t"), st_["sg"][:], reads=[st_["sgb"]])
                        items.append(fin)

                def start_v():
                    st_["va"], st_["vab"] = vaug.next()
                items.append(start_v)
                for b in range(4):
                    def itv(b=b):
                        va, vab = st_["va"], st_["vab"]
                        P_, Pb = pp[pis[0] % 3], ppb[pis[0] % 3]
                        pis[0] += 1
                        for half in range(2):
                            col = 2 * D + half * 512
                            for kc in range(8):
                                fw.op(fw.pe, lambda: nc.tensor.matmul(P_[:, half, :], h[:, kc, 128 * b:128 * b + 128], w[:, kc, col:col + 512], start=(kc == 0), stop=(kc == 7)),
                                      reads=[wb, hb], writes=[Pb[half]], inc=(kc == 7))
                            if na:
                                o_ap = va[:, b, 8 * half:8 * half + 8, 0:64]
                                i_ap = P_[:, half, :].rearrange("p (h d) -> p h d", h=8)
                            else:
                                o_ap = va[:, 4 * half:4 * half + 4, b, 0:128]
                                i_ap = P_[:, half, :].rearrange("p (h d) -> p h d", h=4)
                            if half == 0:
                                fw.op(fw.act, lambda: nc.scalar.activation(out=o_ap, in_=i_ap, func=AF.Copy), reads=[Pb[half]], writes=[vab])
                            else:
                                fw.op(fw.dve, lambda: nc.vector.tensor_copy(out=o_ap, in_=i_ap), reads=[Pb[half]], writes=[vab])
                    items.append(itv)

                def finv():
                    va, vab = st_["va"], st_["vab"]
                    if na:
                        fw.dma(fw.sp, self.NV[:, blk0:blk0 + 4, :], va[:].rearrange("p b h d -> p b (h d)"), reads=[vab])
                    else:
                        fw.dma(fw.sp, self.DV[:, :, blk0:blk0 + 4, :].rearrange("h p b x -> p h b x"), va[:], reads=[vab])
                items.append(finv)
                return items

            for b in range(4):
                A1(0, b)
                A2(0, b)
            for n in range(len(sbs)):
                items = items_for(n)
                L = len(items)
                hooks = {}
                for k in range(8):
                    hooks.setdefault(min(L - 1, (L * (k + 1)) // 9), []).append(k)
                for idx, itf in enumerate(items):
                    itf()
                    for k in hooks.get(idx, []):
                        if k % 2 == 0:
                            A1(n + 1, k // 2)
                        else:
                            A2(n + 1, k // 2)
            fw.barrier()

    @staticmethod
    def na_kc(j, NB):
        R = 2 * NB
        r0a = min(max(2 * j - 4, 0), R - 8)
        r0b = min(max(2 * j + 1 - 4, 0), R - 8)
        return list(range(r0a // 2, (r0b + 7) // 2 + 1))

    def phase_na2(self, dst):
        nc, fw = self.nc, self.fw
        with ExitStack() as st:
            c = self.consts(st)
            gbc, gbcb = self.load_gbc(st, "na_gbc", 2)
            wo = self.sb(st, "na_wo", [128, 8, D], BF16)
            wob = Buf()
            ORDER = [0, 2, 4, 6, 8, 10, 12, 14, 1, 3, 5, 7, 9, 11, 13, 15]
            for pos, hh_ in enumerate(ORDER):
                fw.dma(fw.pool, wo[64 * (pos % 2):64 * (pos % 2) + 64, pos // 2, :], self.na_wo[hh_ * 64:hh_ * 64 + 64, :], writes=[wob])
            bint = self.sb(st, "na_bint", [128, NA_H, 640], F32)
            bintb = Buf()
            for h0 in range(0, NA_H, 4):
                fw.dma(fw.sp, bint[:, h0:h0 + 4, :], self.bias_int[h0:h0 + 4].rearrange("h p n -> p h n"), writes=[bintb])
            bsp = self.ring(st, "na_bsp", 3, [128, 512], F32)
            kring = self.ring(st, "na_k", 10, [128, 8, 128], BF16)
            vring = self.ring(st, "na_v", 10, [128, NA_H * 65], BF16)
            qr = self.ring(st, "na_q", 3, [128, 8, 128], BF16)
            ssb_r = self.ring(st, "na_ssb", 3, [128, 640], F32)
            pT = self.ring(st, "na_pT", 4, [128, 640], BF16)
            on_r = self.ring(st, "na_on", 2, [128, NA_H, 64], BF16)
            oT_r = self.ring(st, "na_oT", 2, [128, 8, 128], BF16)
            rec_r = self.ring(st, "na_rec", 4, [128, 8], F32)
            be = self.make_backend(st, c, gbc, gbcb)
            Sall = self.ps(st, "na_S", [128, 4, 512])
            Sflat = Sall[:, :, :].rearrange("p a b -> p (a b)")
            Sb = [Buf(), Buf()]
            NREG, RSTR = 2, 1024
            O = [self.ps(st, f"na_O{i}", [128, 512]) for i in range(2)]
            Ob = [Buf(), Buf()]
            M = self.ps(st, "na_M", [128, 2, 512])
            Mb = Buf()
            Mbf = M[:, 0, :].bitcast(BF16)
            GROUPS = ((0, 7, 0), (7, 14, 1), (14, 16, 0))
            units = []
            for si_, (toff, T) in enumerate(self.seqs):
                for j in range(T // 128):
                    for h in range(NA_H):
                        units.append((si_, j, h))
            blocks = {}
            seqstate = {}
            pend = []

            def block_setup(si_, j):
                toff, T = self.seqs[si_]
                NB = T // 128
                blkoff = toff // 128
                if j == 0:
                    seqstate[si_] = {"loaded": -1, "kslot": {}}
                ss_ = seqstate[si_]
                kcs = self.na_kc(j, NB)
                upto = min(max(kcs) + 1, NB - 1)
                while ss_["loaded"] < upto:
                    ss_["loaded"] += 1
                    kt, ktb = kring.next()
                    vt, vtb = vring.next()
                    fw.dma(fw.sp, kt[:], self.NKT[blkoff + ss_["loaded"]], writes=[ktb])
                    fw.dma(fw.sp, vt[:], self.NV[:, blkoff + ss_["loaded"], :], writes=[vtb])
                    ss_["kslot"][ss_["loaded"]] = (kt, ktb, vt, vtb)
                if j < 2:
                    sp_idx = j
                elif j >= NB - 2:
                    sp_idx = 2 + (j - (NB - 2))
                else:
                    sp_idx = None
                q, qb = qr.next()
                fw.dma(fw.sp, q[:], self.NQT[blkoff + j], writes=[qb])
                tk = toff + 128 * j
                xr, xrb = self.backend_prefetch(be, self.x[tk:tk + 128, :], 128)
                on, onb = on_r.next()
                blocks[(si_, j)] = dict(kcs=kcs, sp_idx=sp_idx, q=q, qb=qb, tk=tk, xr=xr, xrb=xrb, on=on, onb=onb,
                                        ks=[ss_["kslot"][kc] for kc in kcs])

            def emit_qk(ui):
                si_, j, pos = units[ui]
                h = ORDER[pos]
                if pos == 0:
                    block_setup(si_, j)
                B_ = blocks[(si_, j)]
                oc, r0 = h // 2, (h % 2) * 64
                reg = ui % NREG
                nt = len(B_["kcs"])
                for i in range(nt):
                    kt, ktb, vt, vtb = B_["ks"][i]
                    fw.op(fw.pe, lambda: nc.tensor.matmul(Sflat[:, RSTR * reg + i * 128:RSTR * reg + (i + 1) * 128], kt[r0:r0 + 64, oc, :], B_["q"][r0:r0 + 64, oc, :],
                                                          start=True, stop=True),
                          reads=[ktb, B_["qb"]], writes=[Sb[reg]], inc=(i == nt - 1))

            def emit_rest(ui):
                si_, j, pos = units[ui]
                h = ORDER[pos]
                B_ = blocks[(si_, j)]
                reg = ui % NREG
                nt = len(B_["kcs"])
                Sf = Sflat[:, RSTR * reg:RSTR * reg + nt * 128]
                if B_["sp_idx"] is None:
                    b_ap, b_b = bint[:, h, 0:nt * 128], bintb
                else:
                    bt, btb = bsp.next()
                    fw.dma(fw.sp, bt[:], self.bias_sp[B_["sp_idx"], h], writes=[btb])
                    b_ap, b_b = bt[:, 0:nt * 128], btb
                ss_, ss_b = ssb_r.next()
                fw.op(fw.dve, lambda: nc.vector.scalar_tensor_tensor(out=ss_[:, 0:nt * 128], in0=Sf, scalar=0.125, in1=b_ap, op0=ALU.mult, op1=ALU.add),
                      reads=[Sb[reg], b_b], writes=[ss_b])
                p_, p_b = pT.next()
                fw.op(fw.act, lambda: nc.scalar.activation(out=p_[:, 0:nt * 128], in_=ss_[:, 0:nt * 128], func=AF.Exp), reads=[ss_b], writes=[p_b])
                pts[ui] = (p_, p_b)

            def emit_pv(ui):
                si_, j, pos = units[ui]
                h = ORDER[pos]
                B_ = blocks[(si_, j)]
                nt = len(B_["kcs"])
                p_, p_b = pts.pop(ui)
                hg0, hg1, ob_i = [g for g in GROUPS if g[0] <= pos < g[1]][0]
                Ot, Otb = O[ob_i], Ob[ob_i]
                sl = pos - hg0
                for i in range(nt):
                    kt, ktb, vt, vtb = B_["ks"][i]
                    fw.op(fw.pe, lambda: nc.tensor.matmul(Ot[:, sl * 65:(sl + 1) * 65], p_[:, i * 128:(i + 1) * 128], vt[:, h * 65:(h + 1) * 65],
                                                          start=(i == 0), stop=(i == nt - 1)),
                          reads=[p_b, vtb], writes=[Otb], inc=(i == nt - 1))
                if pos == hg1 - 1:
                    nh_ = hg1 - hg0
                    rec, recb = rec_r.next()
                    on, onb = B_["on"], B_["onb"]
                    Ov = Ot[:, 0:nh_ * 65].rearrange("p (h d) -> p h d", d=65)
                    fw.op(fw.dve, lambda: nc.vector.reciprocal(out=rec[:, 0:nh_], in_=Ov[:, :, 64]), reads=[Otb], writes=[recb])
                    fw.op(fw.dve, lambda: nc.vector.tensor_tensor(out=on[:, hg0:hg1, :], in0=Ov[:, :, 0:64],
                                                                  in1=rec[:, 0:nh_].unsqueeze(2).to_broadcast([128, nh_, 64]), op=ALU.mult),
                          reads=[Otb, recb], writes=[onb])
                if pos == NA_H - 1:
                    def tail(B_=B_):
                        on, onb = B_["on"], B_["onb"]
                        onf = on[:, :, :].rearrange("p h d -> p (h d)")
                        for k in range(8):
                            fw.op(fw.pe, lambda: nc.tensor.transpose(out=Mbf[:, k * 128:(k + 1) * 128], in_=onf[:, k * 128:(k + 1) * 128], identity=c["idb"][:]),
                                  reads=[onb, c["B"]], writes=[Mb], inc=(k == 7))
                        oT, oTb = oT_r.next()
                        fw.op(fw.dve, lambda: nc.vector.tensor_copy(out=oT[:, :, :].rearrange("p a b -> p (a b)"), in_=Mbf[:, :]), reads=[Mb], writes=[oTb])
                        for half in range(2):
                            for k in range(8):
                                fw.op(fw.pe, lambda: nc.tensor.matmul(M[:, half, :], oT[:, k, :], wo[:, k, half * 512:(half + 1) * 512], start=(k == 0), stop=(k == 7)),
                                      reads=[oTb, wob], writes=[Mb], inc=(k == 7 and half == 1))
                        tk = B_["tk"]
                        self.backend(be, M, Mb, 128, B_["xr"], B_["xrb"], dst[tk:tk + 128, :])
                    pend.append(tail)
                if pos == 3:
                    while pend:
                        pend.pop(0)()

            pts = {}
            for ui in range(len(units) + 2):
                if ui < len(units):
                    emit_qk(ui)
                if 0 <= ui - 1 < len(units):
                    emit_rest(ui - 1)
                if ui - 2 >= 0:
                    emit_pv(ui - 2)
            while pend:
                pend.pop(0)()
            fw.barrier()

    def phase_select(self):
        nc, fw = self.nc, self.fw
        OWN, OFF1, tp0 = self.OWN, self.OFF1, self.TP_off
        with ExitStack() as st:
            f = self.sb(st, "sel_f", [128, 2], F32)
            fb = Buf()
            fw.dma(fw.sp, f[:], self.fsel[:, :], writes=[fb])
            qa = self.ring(st, "sel_qa", 2, [128, OWN], BF16)
            qb_ = self.ring(st, "sel_qb", 2, [128, OWN], BF16)
            qt = self.ring(st, "sel_qt", 2, [128, OWN], F32)
            qo = self.ring(st, "sel_qo", 2, [128, OWN], BF16)
            for oc in range(8):
                a, ab = qa.next()
                b, bb = qb_.next()
                t, tb = qt.next()
                o, ob = qo.next()
                fw.dma(fw.sp, a[:], self.DQT[oc, :, tp0:tp0 + OWN], writes=[ab])
                fw.dma(fw.sp, b[:], self.DQT[oc, :, tp0 + OFF1:tp0 + OFF1 + OWN], writes=[bb])
                fw.op(fw.dve, lambda: nc.vector.tensor_scalar(out=t[:], in0=a[:], scalar1=f[:, 0:1], scalar2=None, op0=ALU.mult), reads=[ab, fb], writes=[tb])
                fw.op(fw.dve, lambda: nc.vector.scalar_tensor_tensor(out=o[:], in0=b[:], scalar=f[:, 1:2], in1=t[:], op0=ALU.mult, op1=ALU.add), reads=[bb, tb, fb], writes=[ob])
                fw.dma(fw.sp, self.DQTo[oc, :, :], o[:], reads=[ob])
            xa = self.ring(st, "sel_xa", 2, [128, D], F32)
            xb = self.ring(st, "sel_xb", 2, [128, D], F32)
            xt = self.ring(st, "sel_xt", 2, [128, D], F32)
            xo = self.ring(st, "sel_xo", 2, [128, D], F32)
            for k in range(OWN // 128):
                a, ab = xa.next()
                b, bb = xb.next()
                t, tb = xt.next()
                o, ob = xo.next()
                fw.dma(fw.sp, a[:], self.XB[tp0 + 128 * k:tp0 + 128 * k + 128, :], writes=[ab])
                fw.dma(fw.sp, b[:], self.XB[tp0 + OFF1 + 128 * k:tp0 + OFF1 + 128 * k + 128, :], writes=[bb])
                fw.op(fw.dve, lambda: nc.vector.tensor_scalar(out=t[:], in0=a[:], scalar1=f[:, 0:1], scalar2=None, op0=ALU.mult), reads=[ab, fb], writes=[tb])
                fw.op(fw.dve, lambda: nc.vector.scalar_tensor_tensor(out=o[:], in0=b[:], scalar=f[:, 1:2], in1=t[:], op0=ALU.mult, op1=ALU.add), reads=[bb, tb, fb], writes=[ob])
                fw.dma(fw.sp, self.XBo[128 * k:128 * k + 128, :], o[:], reads=[ob])
            fw.barrier()

    def phase_da2(self):
        nc, fw = self.nc, self.fw
        with ExitStack() as st:
            c = self.consts(st)
            lam = self.sb(st, "da_lam", [128, 4, 64], F32)
            lamb = Buf()
            for i in range(4):
                fw.dma(fw.sp, lam[:, i, :], self.lam_in[i, :].partition_broadcast(128), writes=[lamb])
            lj = self.sb(st, "da_lj", [128, 64], F32)
            lv = self.sb(st, "da_lv", [128, 8], F32)
            lvb = Buf()
            for i in range(2):
                fw.op(fw.dve, lambda i=i: nc.vector.scalar_tensor_tensor(out=lj[:], in0=lam[:, 2 * i, :], scalar=1.0, in1=lam[:, 2 * i + 1, :], op0=ALU.mult, op1=ALU.mult,
                                                                       accum_out=lv[:, i:i + 1]), reads=[lamb], writes=[lvb])
            fw.op(fw.act, lambda: nc.scalar.activation(out=lv[:, 2:4], in_=lv[:, 0:2], func=AF.Exp), reads=[lvb], writes=[lvb])
            fw.op(fw.dve, lambda: nc.vector.scalar_tensor_tensor(out=lv[:, 4:5], in0=lv[:, 3:4], scalar=-LAMBDA_INIT, in1=lv[:, 2:3], op0=ALU.add, op1=ALU.subtract),
                  reads=[lvb], writes=[lvb])
            gsub = self.sb(st, "da_gsub", [128, 128], F32)
            gsubb = Buf()
            fw.dma(fw.sp, gsub[:], self.subln_g.partition_broadcast(128), writes=[gsubb])
            fw.op(fw.dve, lambda: nc.vector.tensor_scalar(out=gsub[:], in0=gsub[:], scalar1=(1.0 - LAMBDA_INIT), scalar2=None, op0=ALU.mult), reads=[gsubb], writes=[gsubb])

            TM = self.TMAX
            NCM = TM // 128
            KT = self.ring(st, "da_KT", 2, [128, TM], BF16)
            QT = self.ring(st, "da_QT", 2, [128, TM], BF16)
            VH = self.ring(st, "da_VH", 2, [128, NCM, 129], BF16)
            E = self.ring(st, "da_E", 3, [128, 2, 512], BF16)
            accs = self.ring(st, "da_accs", 2, [128, 8, 129], F32)
            rec_r = self.ring(st, "da_rec", 2, [128, 8], F32)
            o1_r = self.ring(st, "da_o1", 2, [128, 128], F32)
            o_r = self.ring(st, "da_o", 2, [128, 128], F32)
            oj = self.sb(st, "da_oj", [128, 128], F32)
            ojb = Buf()
            ss_r = self.ring(st, "da_ss", 4, [128, 4], F32)
            on_r = self.ring(st, "da_on", 8, [128, 128], BF16)
            oT_r = self.ring(st, "da_oT", 2, [128, 512], BF16)
            S = [self.ps(st, f"da_S{i}", [128, 2, 512]) for i in range(2)]
            Sb = [Buf(), Buf()]
            A = [self.ps(st, f"da_A{i}", [128, 512]) for i in range(3)]
            Ab = [Buf(), Buf(), Buf()]
            TPp = self.ps(st, "da_TP", [128, 512])
            TPb = Buf()
            TPbf = TPp[:, :].bitcast(BF16)
            pend = []

            def flush_pend():
                while pend:
                    pend.pop(0)()

            for (own, toff, T, TQ) in ((False, self.TS_off, self.TS, self.TS), (True, self.TP_off, self.TP, self.OWN)):
                NC_ = T // 128
                blkoff = toff // 128
                for h in range(DA_H):
                    kt, ktb = KT.next()
                    qt_, qtb = QT.next()
                    vh, vhb = VH.next()
                    m, hh = h // 2, h % 2
                    for i in range(2):
                        sl = 2 * hh + i
                        for ab in range(2):
                            prt = i * 64 + ab * 32
                            fw.dma(fw.sp, kt[prt:prt + 32, 0:T], self.DKT[2 * m + ab, sl * 32:sl * 32 + 32, toff:toff + T], writes=[ktb])
                            if own:
                                fw.dma(fw.sp, qt_[prt:prt + 32, 0:TQ], self.DQTo[2 * m + ab, sl * 32:sl * 32 + 32, 0:TQ], writes=[qtb])
                            else:
                                fw.dma(fw.sp, qt_[prt:prt + 32, 0:TQ], self.DQT[2 * m + ab, sl * 32:sl * 32 + 32, toff:toff + TQ], writes=[qtb])
                    fw.dma(fw.sp, vh[:, 0:NC_, :], self.DV[h, :, blkoff:blkoff + NC_, :], writes=[vhb])
                    for qi in range(TQ // 512):
                        q0 = qi * 512

                        def qk(kb):
                            S_, S_b = S[kb % 2], Sb[kb % 2]
                            for i in range(2):
                                fw.op(fw.pe, lambda: nc.tensor.matmul(S_[:, i, :], kt[64 * i:64 * i + 64, kb * 128:(kb + 1) * 128], qt_[64 * i:64 * i + 64, q0:q0 + 512],
                                                                      start=True, stop=True),
                                      reads=[ktb, qtb], writes=[S_b], inc=(i == 1))

                        qk(0)
                        qk(1)
                        for kb in range(NC_):
                            S_, S_b = S[kb % 2], Sb[kb % 2]
                            e, eb = E.next()
                            fw.op(fw.act, lambda: nc.scalar.activation(out=e[:, :, :], in_=S_[:, :, :], func=AF.Exp, scale=0.125), reads=[S_b], writes=[eb])
                            if kb + 2 < NC_:
                                qk(kb + 2)
                            for a in range(8):
                                i, s_ = a // 4, a % 4
                                bk, slot = a // 3, a % 3
                                for cc in range(2):
                                    fw.op(fw.pe, lambda: nc.tensor.matmul(A[bk][64 * cc:64 * cc + 64, 160 * slot:160 * slot + 129],
                                                                          e[:, i, 128 * s_ + 64 * cc:128 * s_ + 64 * cc + 64], vh[:, kb, :],
                                                                          start=(kb == 0 and slot == 0), stop=(kb == NC_ - 1), skip_group_check=True,
                                                                          tile_position=(0, 64 * cc)),
                                          reads=[eb, vhb], writes=[Ab[bk]], inc=(a == 7 and cc == 1))
                            if kb == 3:
                                flush_pend()
                        flush_pend()
                        ac, acb = accs.next()
                        for bk in range(3):
                            na_ = min(3, 8 - 3 * bk)
                            fw.op(fw.dve, lambda: nc.vector.tensor_copy(out=ac[:, 3 * bk:3 * bk + na_, :],
                                                                        in_=A[bk][:, 0:160 * na_].rearrange("p (s x) -> p s x", x=160)[:, :, 0:129]),
                                  reads=[Ab[bk]], writes=[acb])
                        rec, recb = rec_r.next()
                        fw.op(fw.dve, lambda: nc.vector.reciprocal(out=rec[:, :], in_=ac[:, :, 128]), reads=[acb], writes=[recb])
                        fw.op(fw.dve, lambda: nc.vector.tensor_scalar(out=rec[:, 4:8], in0=rec[:, 4:8], scalar1=lv[:, 4:5], scalar2=None, op0=ALU.mult),
                              reads=[recb, lvb], writes=[recb])
                        oT, oTb = oT_r.next()
                        ons = []
                        for s_ in range(4):
                            o1, o1b = o1_r.next()
                            o_, o_b = o_r.next()
                            ss, ssb = ss_r.next()
                            on, onb = on_r.next()
                            fw.op(fw.dve, lambda: nc.vector.tensor_scalar(out=o1[:], in0=ac[:, s_, 0:128], scalar1=rec[:, s_:s_ + 1], scalar2=None, op0=ALU.mult),
                                  reads=[acb, recb], writes=[o1b])
                            fw.op(fw.dve, lambda: nc.vector.scalar_tensor_tensor(out=o_[:], in0=ac[:, 4 + s_, 0:128], scalar=rec[:, 4 + s_:5 + s_], in1=o1[:], op0=ALU.mult, op1=ALU.add),
                                  reads=[acb, recb, o1b], writes=[o_b])
                            fw.op(fw.dve, lambda: nc.vector.scalar_tensor_tensor(out=oj[:], in0=o_[:], scalar=1.0, in1=o_[:], op0=ALU.mult, op1=ALU.mult, accum_out=ss[:, 0:1]),
                                  reads=[o_b], writes=[ojb, ssb])
                            self.rstd(ss, ssb, 128, 1.0 / 128, SUBLN_EPS, c)
                            fw.op(fw.dve, lambda: nc.vector.scalar_tensor_tensor(out=on[:], in0=o_[:], scalar=ss[:, 2:3], in1=gsub[:], op0=ALU.mult, op1=ALU.mult),
                                  reads=[o_b, ssb, gsubb], writes=[onb])
                            ons.append((on, onb))

                        def tail(ons=ons, oT=oT, oTb=oTb, q0=q0, h=h, toff=toff, own=own):
                            for s_, (on, onb) in enumerate(ons):
                                fw.op(fw.pe, lambda: nc.tensor.transpose(out=TPbf[:, 128 * s_:128 * s_ + 128], in_=on[:], identity=c["idb"][:]),
                                      reads=[onb, c["B"]], writes=[TPb], inc=(s_ == 3))
                            fw.op(fw.dve, lambda: nc.vector.tensor_copy(out=oT[:, :], in_=TPbf[:, 0:512]), reads=[TPb], writes=[oTb])
                            if own:
                                fw.dma(fw.sp, self.DOTo[h, :, q0:q0 + 512], oT[:], reads=[oTb])
                            else:
                                fw.dma(fw.sp, self.DOT[h, :, toff + q0:toff + q0 + 512], oT[:], reads=[oTb])
                        pend.append(tail)
            flush_pend()
            fw.barrier()

    def phase_da3(self):
        nc, fw = self.nc, self.fw
        with ExitStack() as st:
            c = self.consts(st)
            gbc, gbcb = self.load_gbc(st, "d3_gbc", 3)
            wo = self.sb(st, "d3_wo", [128, 8, D], BF16)
            wob = Buf()
            fw.dma(fw.pool, wo[:], self.da_wo.rearrange("(kc p) n -> p kc n", p=128), writes=[wob])
            be = self.make_backend(st, c, gbc, gbcb)
            oT = self.ring(st, "d3_oT", 2, [128, 8, 512], BF16)
            M = [self.ps(st, f"d3_M{i}", [128, 2, 512]) for i in range(2)]
            Mb = [Buf(), Buf()]
            mi = 0
            for (dot, src, dst, T) in ((self.DOT[:, :, self.TS_off:self.TS_off + self.TS], self.XB[self.TS_off:self.TS_off + self.TS, :], self.XC[self.TS_off:self.TS_off + self.TS, :], self.TS),
                                       (self.DOTo, self.XBo, self.XCo, self.OWN)):
                for sbi in range(T // 512):
                    t0 = 512 * sbi
                    o, ob = oT.next()
                    fw.dma(fw.sp, o[:], dot[:, :, t0:t0 + 512].rearrange("h p t -> p h t"), writes=[ob])
                    for b in range(4):
                        tk = t0 + 128 * b
                        xr, xrb = self.backend_prefetch(be, src[tk:tk + 128, :], 128)
                        M_, M_b = M[mi % 2], Mb[mi % 2]
                        mi += 1
                        for half in range(2):
                            for k in range(8):
                                fw.op(fw.pe, lambda k=k, half=half, M_=M_, b=b, o=o: nc.tensor.matmul(M_[:, half, :], o[:, k, 128 * b:128 * b + 128], wo[:, k, half * 512:(half + 1) * 512],
                                                                                                      start=(k == 0), stop=(k == 7)),
                                      reads=[ob, wob], writes=[M_b], inc=(k == 7 and half == 1))
                        self.backend(be, M_, M_b, 128, xr, xrb, dst[tk:tk + 128, :])
            fw.barrier()

    def build(self, phases="all"):
        fw = self.fw
        P = phases
        if P == "all" or "p0" in P:
            self.phase_p0()
        if P == "all" or "na1" in P:
            self.phase_proj("na", self.x)
        if P == "all" or "na2" in P:
            self.phase_na2(self.XA)
        if P == "all" or "ffn0" in P:
            self.phase_ffn(0, [(self.XA[o:o + T, :], self.XB[o:o + T, :], T) for (o, T) in self.seqs])
        if P == "all" or "da1" in P:
            self.phase_proj("da", self.XB)
        if P == "all" or "sel" in P:
            self.phase_select()
        if P == "all" or "da2" in P:
            self.phase_da2()
        if P == "all" or "da3" in P:
            self.phase_da3()
        if P == "all" or "ffn1" in P:
            self.phase_ffn(1, [(self.XC[self.TS_off:self.TS_off + self.TS, :], self.y[0:self.TS, :], self.TS),
                               (self.XCo, self.y[self.TS:self.TS + self.OWN, :], self.OWN)])
        fw.finish()
        return self.nc


def _na_bias_tables(rpb):
    R = 16
    NB = R // 2
    W = 64

    def table(j):
        kcs = Builder.na_kc(j, NB)
        out = np.full((NA_H, 128, len(kcs), 128), NEG, np.float32)
        rq = 2 * j + np.arange(128) // 64
        cq = np.arange(128) % 64
        r0 = np.clip(rq - 4, 0, R - 8)
        c0 = np.clip(cq - 8, 0, W - 16)
        for i, kc in enumerate(kcs):
            rk = 2 * kc + np.arange(128) // 64
            ck = np.arange(128) % 64
            valid = ((rk[:, None] >= r0[None, :]) & (rk[:, None] < r0[None, :] + 8) &
                     (ck[:, None] >= c0[None, :]) & (ck[:, None] < c0[None, :] + 16))
            dr = np.clip(rk[:, None] - rq[None, :] + 7, 0, 14)
            dc = np.clip(ck[:, None] - cq[None, :] + 15, 0, 30)
            vals = rpb[:, dr, dc]
            out[:, :, i, :] = np.where(valid[None], vals, np.float32(NEG))
        return out.reshape(NA_H, 128, len(kcs) * 128)

    interior = table(3)
    assert interior.shape[2] == 640
    sp = np.stack([table(0), table(1), table(NB - 2), table(NB - 1)], 0)
    assert sp.shape[3] == 512
    return np.ascontiguousarray(interior), np.ascontiguousarray(sp)


def _rot_tables(T):
    inv = (1.0 / (ROPE_THETA ** (np.arange(0, 64, 2, dtype=np.float32) / np.float32(64)))).astype(np.float32)
    ang = np.arange(T, dtype=np.float32)[None, :] * inv[np.arange(128) % 32][:, None]
    return np.cos(ang).astype(np.float32), np.sin(ang).astype(np.float32)


def _perm_cols():
    cols = []
    for m in range(4):
        for ab in range(2):
            for sl in range(4):
                for dl in range(32):
                    cols.append((4 * m + sl) * 64 + ab * 32 + dl)
    return np.array(cols)


def prepare_shared(inp, TMAX):
    f = lambda a: np.ascontiguousarray(np.asarray(a, dtype=np.float32))
    perm = _perm_cols()
    bi, bsp = _na_bias_tables(f(inp["na_rpb"])[0])
    cos, sin = _rot_tables(TMAX)
    sh = {
        "gvec": f(np.concatenate([inp["attn_pre_g"], inp["attn_post_g"], inp["ffn_pre_g"], inp["ffn_post_g"]], 0)),
        "w_qkv": f(inp["na_w_qkv"][0]), "na_wo": f(inp["na_w_o"][0]),
        "da_wq": f(np.asarray(inp["da_w_q"][0])[:, perm]), "da_wk": f(np.asarray(inp["da_w_k"][0])[:, perm]),
        "da_wv": f(inp["da_w_v"][0]), "da_wo": f(inp["da_w_o"][0]),
        "w_in": f(inp["ffn_w_in"]), "w_out": f(inp["ffn_w_out"]),
        "conv_w": f(inp["ffn_conv_w"]), "conv_b": f(inp["ffn_conv_b"]),
        "bias_int": bi, "bias_sp": bsp, "rot_cos": cos, "rot_sin": sin,
        "lam_in": f(np.stack([inp["da_lambda_q1"][0], inp["da_lambda_k1"][0], inp["da_lambda_q2"][0], inp["da_lambda_k2"][0]], 0)),
        "subln_g": f(inp["da_subln_g"][0]), "ident": np.eye(128, dtype=np.float32),
    }
    return sh


def kernel(**inp):
    xs = np.asarray(inp["x_sample"], dtype=np.float32)
    xp = np.asarray(inp["x_prompt"], dtype=np.float32)
    NS, TS, _ = xs.shape
    NP_, TP, _ = xp.shape
    ncore = 8
    assert NS == ncore and NP_ * 2 == ncore
    seqs = [(0, TS), (TS, TP)]
    bld = Builder(seqs)
    nc = bld.build()
    sh = prepare_shared(inp, max(TS, TP))
    in_maps = []
    for cidx in range(ncore):
        m = dict(sh)
        m["x"] = np.ascontiguousarray(np.concatenate([xs[cidx], xp[cidx // 2]], 0))
        m["fsel"] = np.ascontiguousarray(np.tile(np.array([[1.0, 0.0]] if cidx % 2 == 0 else [[0.0, 1.0]], np.float32), (128, 1)))
        in_maps.append(m)
    res = run_bass_kernel_spmd(nc, in_maps, core_ids=list(range(ncore)))
    y_s = np.stack([res.results[cidx]["y"][:TS] for cidx in range(ncore)], 0)
    H2 = TP // 2
    y_p = np.stack([np.concatenate([res.results[2 * p]["y"][TS:TS + H2], res.results[2 * p + 1]["y"][TS + 512:TS + 512 + H2]], 0) for p in range(NP_)], 0)
    return (y_p.astype(np.float32), y_s.astype(np.float32))
```
